# Optimizing a Trainium2 kernel written in Bass

```python
import jax, jax.numpy as jnp
from jax import lax
import numpy as np

D_MODEL = 2048
BATCH = 1
SEQ = 16384
DEPTH = 1
DEC_BATCH = 8
DEC_SEQ = 64
PAST_LEN = 1024

CHUNK = 64
Q_BLOCK = 128
MIX_WIDTH = D_MODEL
N_ATTN_HEADS = 8
ATTN_HEAD_DIM = MIX_WIDTH // 2 // N_ATTN_HEADS
ROPE_DIM = ATTN_HEAD_DIM // 4
ROPE_THETA = 500000.0
N_IDX_HEADS = 16
IDX_DIM = 64
IDX_ROPE_DIM = IDX_DIM // 4
TOPK_MAX = 256
GLA_HEADS = 4
GLA_DV = MIX_WIDTH // 2 // GLA_HEADS
GLA_DK = GLA_DV // 2
GLA_GATE_RANK = 16
GLA_GATE_TEMP = 16.0
D_FF = 11 * D_MODEL // 4
CONV_W = 3
EPS = 1e-6
IN_SPLITS = (N_ATTN_HEADS * ATTN_HEAD_DIM, N_ATTN_HEADS * ATTN_HEAD_DIM, N_ATTN_HEADS * ATTN_HEAD_DIM,
             N_IDX_HEADS * IDX_DIM, IDX_DIM, N_IDX_HEADS,
             GLA_HEADS * GLA_DK, GLA_HEADS * GLA_DK, GLA_HEADS * GLA_DV, GLA_HEADS * GLA_DV, GLA_GATE_RANK)
IN_DIM = sum(IN_SPLITS)

kernel_name = 'hymba_dsa_gla_convffn_stream_step'


def rmsnorm(x, g):
    xf = x.astype(jnp.float32)
    y = xf * lax.rsqrt(jnp.mean(xf * xf, axis=-1, keepdims=True) + EPS)
    return (y * g.astype(jnp.float32)).astype(x.dtype)


def rope(x, pos, rot):
    half = rot // 2
    inv = ROPE_THETA ** (-jnp.arange(half, dtype=jnp.float32) / half)
    ang = pos.astype(jnp.float32)[:, None] * inv[None, :]
    cos = jnp.cos(ang)[None, :, None, :]
    sin = jnp.sin(ang)[None, :, None, :]
    xf = x.astype(jnp.float32)
    x1 = xf[..., :half]
    x2 = xf[..., half:rot]
    out = jnp.concatenate([x1 * cos - x2 * sin, x2 * cos + x1 * sin, xf[..., rot:]], axis=-1)
    return out.astype(x.dtype)


def dsa_attend(q, k, v, qi, ki, wi, q_pos, k_pos):
    B, T, H, Dh = q.shape
    L = k.shape[1]
    topk = min(TOPK_MAX, L // 4)
    qb = min(Q_BLOCK, T)
    nb = T // qb
    ki32 = ki.astype(jnp.float32)
    k_chunk = k_pos // CHUNK
    idx_scale = (IDX_DIM * N_IDX_HEADS) ** -0.5

    def blocks(a):
        return a.reshape((B, nb, qb) + a.shape[2:]).swapaxes(0, 1)

    def one_block(args):
        qs, qis, wis, qp = args
        q_chunk = qp // CHUNK
        logits = jnp.einsum('bthd,bsd->bths', qis.astype(jnp.float32), ki32)
        score = jnp.einsum('bths,bth->bts', jax.nn.relu(logits), wis.astype(jnp.float32)) * idx_scale
        admissible = k_chunk[None, :] <= q_chunk[:, None]
        score = jnp.where(admissible[None], score, -jnp.inf)
        _, idx = lax.top_k(score, topk)
        valid = jnp.take(k_chunk, idx) <= q_chunk[None, :, None]
        k_sel = jax.vmap(lambda kk, ii: kk[ii])(k, idx)
        v_sel = jax.vmap(lambda vv, ii: vv[ii])(v, idx)
        s = jnp.einsum('bthd,btkhd->bhtk', qs.astype(jnp.float32), k_sel.astype(jnp.float32)) * Dh ** -0.5
        s = jnp.where(valid[:, None], s, -jnp.inf)
        p = jax.nn.softmax(s, axis=-1)
        o = jnp.einsum('bhtk,btkhd->bthd', p, v_sel.astype(jnp.float32))
        return o.astype(q.dtype)

    out = lax.map(one_block, (blocks(q), blocks(qi), blocks(wi), q_pos.reshape(nb, qb)))
    return out.swapaxes(0, 1).reshape(B, T, H, Dh)


def gla_attend(s0, q, k, v, lg):
    B, T, H, dk = q.shape
    dv = v.shape[-1]
    C = min(CHUNK, T)
    n = T // C
    tri = jnp.tril(jnp.ones((C, C), dtype=bool))

    def to_chunks(a):
        return a.astype(jnp.float32).reshape((B, n, C) + a.shape[2:]).swapaxes(0, 1)

    def step(S, inp):
        qc, kc, vc, gc = inp
        b = jnp.cumsum(gc, axis=1)
        inter = jnp.einsum('bthk,bhkv->bthv', qc * jnp.exp(b), S)
        diff = b[:, :, None] - b[:, None, :]
        decay = jnp.exp(jnp.where(tri[None, :, :, None, None], diff, -jnp.inf))
        A = jnp.einsum('bthk,bshk,btshk->bhts', qc, kc, decay)
        intra = jnp.einsum('bhts,bshv->bthv', A, vc)
        bl = b[:, -1]
        S = jnp.exp(bl)[..., None] * S + jnp.einsum('bshk,bshv->bhkv', kc * jnp.exp(bl[:, None] - b), vc)
        return S, inter + intra

    S, o = lax.scan(step, s0.astype(jnp.float32), (to_chunks(q), to_chunks(k), to_chunks(v), to_chunks(lg)))
    o = o.swapaxes(0, 1).reshape(B, T, H, dv)
    return o, S.astype(s0.dtype)


def layer_forward(x, c, pos, past_k, past_v, past_ik, past_pos, gla_s0, conv_buf, lw):
    (w_ada, b_ada, g_mix, g_ffn, w_in, g_q, g_k, w_gate2, b_gate2, g_gla,
     w_out, w_up, w_conv, b_conv, w_down) = lw
    B, T, _ = x.shape
    mod = jax.nn.silu(c) @ w_ada + b_ada
    sh_m, sc_m, gt_m, sh_f, sc_f, gt_f = [m[:, None, :] for m in jnp.split(mod, 6, axis=-1)]

    h = rmsnorm(x, g_mix) * (1 + sc_m) + sh_m
    proj = h @ w_in
    offsets = np.cumsum(IN_SPLITS)[:-1].tolist()
    aq, ak, av, iq, ik, iw, gq, gk, gv, gr, glr = jnp.split(proj, offsets, axis=-1)

    aq = rope(rmsnorm(aq.reshape(B, T, N_ATTN_HEADS, ATTN_HEAD_DIM), g_q), pos, ROPE_DIM)
    ak = rope(rmsnorm(ak.reshape(B, T, N_ATTN_HEADS, ATTN_HEAD_DIM), g_k), pos, ROPE_DIM)
    av = av.reshape(B, T, N_ATTN_HEADS, ATTN_HEAD_DIM)
    iq = rope(iq.reshape(B, T, N_IDX_HEADS, IDX_DIM), pos, IDX_ROPE_DIM)
    ik = rope(ik[:, :, None, :], pos, IDX_ROPE_DIM)[:, :, 0, :]
    k_all = jnp.concatenate([past_k.astype(ak.dtype), ak], axis=1)
    v_all = jnp.concatenate([past_v.astype(av.dtype), av], axis=1)
    ik_all = jnp.concatenate([past_ik.astype(ik.dtype), ik], axis=1)
    kpos_all = jnp.concatenate([past_pos, pos])
    o_a = dsa_attend(aq, k_all, v_all, iq, ik_all, iw, pos, kpos_all).reshape(B, T, N_ATTN_HEADS * ATTN_HEAD_DIM)

    gq = gq.reshape(B, T, GLA_HEADS, GLA_DK) * GLA_DK ** -0.5
    gk = gk.reshape(B, T, GLA_HEADS, GLA_DK)
    gv = gv.reshape(B, T, GLA_HEADS, GLA_DV)
    lg = jax.nn.log_sigmoid((glr @ w_gate2 + b_gate2).astype(jnp.float32)) / GLA_GATE_TEMP
    lg = lg.reshape(B, T, GLA_HEADS, GLA_DK)
    o_g, s_new = gla_attend(gla_s0, gq, gk, gv, lg)
    o_g = rmsnorm(o_g.astype(x.dtype), g_gla) * jax.nn.silu(gr.reshape(B, T, GLA_HEADS, GLA_DV))
    o_g = o_g.reshape(B, T, GLA_HEADS * GLA_DV)

    x = x + gt_m * (jnp.concatenate([o_a, o_g], axis=-1) @ w_out)

    h2 = rmsnorm(x, g_ffn) * (1 + sc_f) + sh_f
    u = h2 @ w_up
    ext = jnp.concatenate([conv_buf.astype(u.dtype), u], axis=1)
    uc = b_conv + sum(w_conv[j] * ext[:, j:j + T] for j in range(CONV_W))
    ua, ub = jnp.split(uc, 2, axis=-1)
    x = x + gt_f * ((jax.nn.silu(ua) * ub) @ w_down)
    new_buf = ext[:, T:]
    return x, ak, av, ik, s_new, new_buf


def setup_inputs(seed: int = 0) -> dict:
    key = jax.random.key(seed)
    ks = jax.random.split(key, 24)

    def nrm(k, shape, s):
        return jax.random.normal(k, shape, jnp.float32) * s

    return {
        'x_prompt': nrm(ks[0], (BATCH, SEQ, D_MODEL), 1.0),
        'x_sample': nrm(ks[1], (DEC_BATCH, DEC_SEQ, D_MODEL), 1.0),
        'c_prompt': nrm(ks[2], (BATCH, D_MODEL), 1.0),
        'c_sample': nrm(ks[3], (DEC_BATCH, D_MODEL), 1.0),
        'cache_k': nrm(ks[4], (DEPTH, DEC_BATCH, PAST_LEN, N_ATTN_HEADS, ATTN_HEAD_DIM), 1.0),
        'cache_v': nrm(ks[5], (DEPTH, DEC_BATCH, PAST_LEN, N_ATTN_HEADS, ATTN_HEAD_DIM), 1.0),
        'cache_idx_k': nrm(ks[6], (DEPTH, DEC_BATCH, PAST_LEN, IDX_DIM), 1.0),
        'state_gla': nrm(ks[7], (DEPTH, DEC_BATCH, GLA_HEADS, GLA_DK, GLA_DV), 0.5),
        'state_ffn_conv': nrm(ks[8], (DEPTH, DEC_BATCH, CONV_W - 1, 2 * D_FF), 1.0),
        'w_ada': nrm(ks[9], (DEPTH, D_MODEL, 6 * D_MODEL), 0.5 * D_MODEL ** -0.5),
        'b_ada': nrm(ks[10], (DEPTH, 6 * D_MODEL), 0.01),
        'g_mix': 1.0 + nrm(ks[11], (DEPTH, D_MODEL), 0.02),
        'g_ffn': 1.0 + nrm(ks[12], (DEPTH, D_MODEL), 0.02),
        'w_in': nrm(ks[13], (DEPTH, D_MODEL, IN_DIM), D_MODEL ** -0.5),
        'g_q': 1.0 + nrm(ks[14], (DEPTH, ATTN_HEAD_DIM), 0.02),
        'g_k': 1.0 + nrm(ks[15], (DEPTH, ATTN_HEAD_DIM), 0.02),
        'w_gate2': nrm(ks[16], (DEPTH, GLA_GATE_RANK, GLA_HEADS * GLA_DK), GLA_GATE_RANK ** -0.5),
        'b_gate2': nrm(ks[17], (DEPTH, GLA_HEADS * GLA_DK), 0.01),
        'g_gla': 1.0 + nrm(ks[18], (DEPTH, GLA_DV), 0.02),
        'w_out': nrm(ks[19], (DEPTH, MIX_WIDTH, D_MODEL), MIX_WIDTH ** -0.5),
        'w_up': nrm(ks[20], (DEPTH, D_MODEL, 2 * D_FF), D_MODEL ** -0.5),
        'w_conv': nrm(ks[21], (DEPTH, CONV_W, 2 * D_FF), CONV_W ** -0.5),
        'b_conv': nrm(ks[22], (DEPTH, 2 * D_FF), 0.01),
        'w_down': nrm(ks[23], (DEPTH, D_FF, D_MODEL), D_FF ** -0.5),
    }


def reference(x_prompt, x_sample, c_prompt, c_sample, cache_k, cache_v, cache_idx_k, state_gla,
              state_ffn_conv, w_ada, b_ada, g_mix, g_ffn, w_in, g_q, g_k, w_gate2, b_gate2, g_gla,
              w_out, w_up, w_conv, b_conv, w_down):
    B, S, _ = x_prompt.shape
    Ts = x_sample.shape[1]
    P = cache_k.shape[2]
    dt = x_prompt.dtype
    pos_p = jnp.arange(S, dtype=jnp.int32)
    pos_s = P + jnp.arange(Ts, dtype=jnp.int32)
    past_pos_s = jnp.arange(P, dtype=jnp.int32)
    empty_pos = jnp.zeros((0,), jnp.int32)
    empty_kv = jnp.zeros((B, 0, N_ATTN_HEADS, ATTN_HEAD_DIM), dt)
    empty_ik = jnp.zeros((B, 0, IDX_DIM), dt)
    gla0 = jnp.zeros((B, GLA_HEADS, GLA_DK, GLA_DV), dt)
    conv0 = jnp.zeros((B, CONV_W - 1, 2 * D_FF), dt)

    y_p, y_s = x_prompt, x_sample
    kp, vp, ikp, sp, cp = [], [], [], [], []
    ksm, vsm, iks, ssm, csm = [], [], [], [], []
    for l in range(DEPTH):
        lw = (w_ada[l], b_ada[l], g_mix[l], g_ffn[l], w_in[l], g_q[l], g_k[l], w_gate2[l], b_gate2[l],
              g_gla[l], w_out[l], w_up[l], w_conv[l], b_conv[l], w_down[l])
        y_p, k1, v1, ik1, s1, c1 = layer_forward(y_p, c_prompt, pos_p, empty_kv, empty_kv, empty_ik,
                                                 empty_pos, gla0, conv0, lw)
        y_s, k2, v2, ik2, s2, c2 = layer_forward(y_s, c_sample, pos_s, cache_k[l], cache_v[l], cache_idx_k[l],
                                                 past_pos_s, state_gla[l], state_ffn_conv[l], lw)
        kp.append(k1); vp.append(v1); ikp.append(ik1); sp.append(s1); cp.append(c1)
        ksm.append(k2); vsm.append(v2); iks.append(ik2); ssm.append(s2); csm.append(c2)

    return (y_p, y_s,
            jnp.stack(kp), jnp.stack(vp), jnp.stack(ikp), jnp.stack(sp), jnp.stack(cp),
            jnp.stack(ksm), jnp.stack(vsm), jnp.stack(iks), jnp.stack(ssm), jnp.stack(csm))
```

```python
import numpy as np
import concourse.bass as bass
import concourse.mybir as mybir
from concourse.bass_utils import run_bass_kernel_spmd

F32 = mybir.dt.float32
BF16 = mybir.dt.bfloat16
AF = mybir.ActivationFunctionType
ALU = mybir.AluOpType
AX = mybir.AxisListType

D = 2048
NCORE = 8
NREL = 135
NT1 = NREL + 1
NKT = 140
KT_S0 = 128
EPS = 1e-6
DFF = 5632
W1C = 4176
C_AK, C_AV, C_GK, C_GV, C_GQ, C_IK, C_GLR = 0, 1024, 2048, 2560, 3584, 4096, 4160
BIG = 1.0e30


class Sync:
    def __init__(self, nc, ndma=(16, 4, 10)):
        self.nc = nc
        self.eng = {"v": nc.vector, "a": nc.scalar, "p": nc.gpsimd, "t": nc.tensor, "s": nc.sync}
        self.csem = {k: nc.alloc_semaphore("c_" + k) for k in ("v", "a", "p", "t")}
        self.ccount = {k: 0 for k in self.csem}
        self.dpool = {}
        for q, n in zip(("s", "a", "p"), ndma):
            self.dpool[q] = [[nc.alloc_semaphore(f"d_{q}{i}"), 0] for i in range(n)]
        self.dnext = {q: 0 for q in self.dpool}
        self.waited = {}
        self.lastw = {}
        self.readers = {}
        self.out_tokens = []
        self.nwait = 0

    def _wait(self, e, tok):
        sem, val, src = tok[:3]
        key = (e, id(sem))
        if self.waited.get(key, 0) >= val:
            return
        self.eng[e].wait_ge(sem, val)
        self.nwait += 1
        self.waited[key] = val

    def _deps(self, e, reads, writes):
        for k in reads:
            w = self.lastw.get(k)
            if w is not None:
                self._wait(e, w)
        for k in writes:
            for r in self.readers.get(k, ()):
                if r[2] != e or r[3]:
                    self._wait(e, r[:3])
            w = self.lastw.get(k)
            if w is not None and (w[2] != e or w[3]):
                self._wait(e, w[:3])

    def _commit(self, tok, reads, writes):
        for k in reads:
            self.readers.setdefault(k, []).append(tok)
        for k in writes:
            self.lastw[k] = tok
            self.readers[k] = []

    def op(self, e, fn, r=(), w=()):
        self._deps(e, r, w)
        inst = fn(self.eng[e])
        self.ccount[e] += 1
        inst.then_inc(self.csem[e], 1)
        tok = (self.csem[e], self.ccount[e], e, False)
        self._commit(tok, r, w)
        return tok

    def dma(self, q, out, in_, r=(), w=(), is_output=False, **kw):
        self._deps(q, r, w)
        pool = self.dpool[q]
        i = self.dnext[q]
        self.dnext[q] = (i + 1) % len(pool)
        sem, cnt = pool[i]
        if cnt:
            self._wait(q, (sem, 16 * cnt, q))
        inst = self.eng[q].dma_start(out=out, in_=in_, **kw)
        inst.then_inc(sem, 16)
        pool[i][1] = cnt + 1
        tok = (sem, 16 * (cnt + 1), q, True)
        self._commit(tok, r, w)
        if is_output:
            self.out_tokens.append(tok)
        return tok

    def barrier(self):
        toks = [(self.csem[k], self.ccount[k], k) for k in self.csem if self.ccount[k]]
        for q, pool in self.dpool.items():
            for sem, cnt in pool:
                if cnt:
                    toks.append((sem, 16 * cnt, q))
        for e in ("v", "a", "p", "t", "s"):
            for t in toks:
                if t[2] != e or t[0] not in self.csem.values():
                    self._wait(e, t)
        self.lastw.clear()
        self.readers.clear()

    def finish(self):
        for t in self.out_tokens:
            self._wait("s", t[:3])
        for q, pool in self.dpool.items():
            for sem, cnt in pool:
                if cnt:
                    self._wait("s", (sem, 16 * cnt, q))


class PsumPool:
    def __init__(self, nc, n=8):
        self.banks = [nc.alloc_psum_tensor(f"psb{i}", [128, 512], F32) for i in range(n)]
        self.i = 0
        self.n = n

    def get(self):
        b = self.banks[self.i]
        k = ("ps", self.i)
        self.i = (self.i + 1) % self.n
        return b, k


def build_program(phases=("p0", "p1", "pq")):
    nc = bass.Bass("TRN2", target_bir_lowering=False)
    dt = nc.dram_tensor

    def din(name, shape, dtype=F32):
        return dt(name, list(shape), dtype, kind="ExternalInput").ap()

    def dout(name, shape, dtype=F32):
        return dt(name, list(shape), dtype, kind="ExternalOutput").ap()

    xp = din("xp", [NREL * 128, D])
    xs = din("xs", [64, D])
    cT = din("cT", [128, 16, 2])
    badaT = din("badaT", [128, 96])
    badaR = din("badaR", [2, 4096])
    gmixT = din("gmixT", [128, 16])
    gffnT = din("gffnT", [128, 16])
    wada = din("wada", [24, 128, 16, 512])
    w1 = din("w1", [128, 16, W1C])
    gk_b = din("gk_b", [128, 1024])
    gq_b = din("gq_b", [128, 1024])
    ggla_b = din("ggla_b", [128, 1024])
    wg2 = din("wg2", [17, 512])
    flagt = din("flagt", [128, 2 * NT1])
    ropeK = din("ropeK", [128, NT1, 32])
    ropeI = din("ropeI", [128, NT1, 16])
    cmat = din("cmat", [128, 6, 128])
    ckT = din("ckT", [8, 128, 8, 128])
    cv = din("cv", [1024, 1024])
    cikT = din("cikT", [64, 1024])
    sgla = din("sgla", [128, 1024])

    o_k = dout("o_k", [16 * 128, 1024])
    o_v = dout("o_v", [16 * 128, 1024])
    o_ik = dout("o_ik", [16 * 128, 64])
    o_ks = dout("o_ks", [64, 1024])
    o_vs = dout("o_vs", [64, 1024])
    o_iks = dout("o_iks", [64, 64])
    o_glap = dout("o_glap", [128, 1024])
    o_glas = dout("o_glas", [128, 1024])
    o_y = dout("o_y", [16 * 128, D])
    import os as _os
    DBG = _os.environ.get("DBG", "0") == "1"
    if DBG:
        dbg_mix = dout("dbg_mix", [128, D]); dbg_xm = dout("dbg_xm", [128, D]); dbg_q = dout("dbg_q", [128, 1024])
    o_ys = dout("o_ys", [64, D])
    o_convp = dout("o_convp", [128, 88, 2])
    o_convs = dout("o_convs", [128, 88, 2])
    w2_d = din("w2", [12, 128, 16, 256])
    wiw_d = din("wiw", [128, 16, 16])
    wout_d = din("wout", [8, 128, 16, 256])
    wup_d = din("wup", [44, 128, 16, 256])
    wdn_d = din("wdn", [4, 44, 128, 512])
    convw_d = din("convw", [128, 88, 4])
    sconv_d = din("sconv", [128, 88, 2])
    hflag_d = din("hflag", [128, 32])
    ropeKh_d = din("ropeKh", [128, 32])
    ropeIh_d = din("ropeIh", [128, 16])
    biasO_d = din("biasO", [4, 128, 512])
    biasS_d = din("biasS", [128, 128])
    biasH_d = din("biasH", [128, 16384])

    KTs = dt("KTs", [NKT, 128, 8, 128], BF16).ap()
    Vs = dt("Vs", [NKT * 128, 8 * 130], BF16).ap()
    IKTs = dt("IKTs", [64, NKT * 128], BF16).ap()
    OGs = dt("OGs", [17 * 128, 1024], F32).ap()
    GTd = dt("GTd", [2, 4096], F32).ap()
    w2b = dt("w2b", [12, 128, 16, 256], BF16).ap()
    wiwb = dt("wiwb", [128, 16, 16], BF16).ap()
    woutb = dt("woutb", [8, 128, 16, 256], BF16).ap()
    wupb = dt("wupb", [44, 128, 16, 256], BF16).ap()
    wdnb = dt("wdnb", [4, 44, 128, 512], BF16).ap()

    sy = Sync(nc)
    pp = PsumPool(nc)
    V, A, P, T = (lambda fn, r=(), w=(): sy.op("v", fn, r, w)), (lambda fn, r=(), w=(): sy.op("a", fn, r, w)), \
        (lambda fn, r=(), w=(): sy.op("p", fn, r, w)), (lambda fn, r=(), w=(): sy.op("t", fn, r, w))

    import contextlib
    stack_holder = [None]

    def sb(name, shape, dtype=F32):
        sb.n = getattr(sb, "n", 0) + 1
        name = f"sb{sb.n}_{name}"
        if stack_holder[0] is None:
            return nc.alloc_sbuf_tensor(name, list(shape), dtype)
        return stack_holder[0].enter_context(nc.sbuf_tensor(name, list(shape), dtype))

    cm = sb("cm", [128, 6, 128])
    cmb = sb("cmb", [128, 128], BF16)
    modT = sb("modT", [128, 96, 2])
    g1 = sb("g1", [128, 2, 16]); shm = sb("shm", [128, 2, 16])
    g2 = sb("g2", [128, 2, 16]); shf = sb("shf", [128, 2, 16])
    tmpc = sb("tmpc", [128, 16, 2])
    negh = sb("negh", [128, 1])

    sy.dma("s", cm[:], cmat, w=["cm"])
    V(lambda e: e.tensor_copy(out=cmb[:], in_=cm[:, 0, :]), r=["cm"], w=["cmb"])
    V(lambda e: e.memset(negh[:], -0.5), w=["negh"])
    IDF = cm[:, 0, :]

    def rsqrt_small(dst, src, n, mul, key_src, key_dst):
        V(lambda e: e.tensor_scalar(out=dst, in0=src, scalar1=mul, scalar2=EPS, op0=ALU.mult, op1=ALU.add),
          r=[key_src], w=[key_dst])
        P(lambda e: e.tensor_tensor(out=dst, in0=dst, in1=negh[:, 0:1].to_broadcast([128, n]), op=ALU.pow),
          r=[key_dst, "negh"], w=[key_dst])

    if "p0" in phases:
        stack_holder[0] = contextlib.ExitStack()
        gtrow = sb("gtrow", [2, 4096])
        cTt = sb("cTt", [128, 16, 2]); scb = sb("scb", [128, 16, 2], BF16)
        bT = sb("bT", [128, 96]); gmT = sb("gmT", [128, 16]); gfT = sb("gfT", [128, 16])
        brow = sb("brow", [2, 4096])
        wab = [sb(f"wab{i}", [128, 16, 512], BF16) for i in range(2)]
        sy.dma("s", cTt[:], cT, w=["cTt"])
        sy.dma("s", bT[:], badaT, w=["bT"])
        sy.dma("s", gmT[:], gmixT, w=["gmT"])
        sy.dma("s", gfT[:], gffnT, w=["gfT"])
        sy.dma("s", brow[:], badaR, w=["brow"])
        A(lambda e: e.activation(out=scb[:], in_=cTt[:], func=AF.Silu), r=["cTt"], w=["scb"])
        import os
        P0STOP = int(os.environ.get("P0STOP", "99"))
        for blk in range(24 if P0STOP > 2 else (1 if P0STOP > 0 else 0)):
            wb = wab[blk % 2]; wk = ("wab", blk % 2)
            sy.dma("p", wb[:], wada[blk], w=[wk])
            ps, pk = pp.get()
            for cc in range(4):
                for k in range(16):
                    T(lambda e, cc=cc, k=k: e.matmul(ps[:, cc * 2:cc * 2 + 2], lhsT=wb[:, k, cc * 128:(cc + 1) * 128],
                                                     rhs=scb[:, k, :], start=(k == 0), stop=(k == 15)),
                      r=[wk, "scb"], w=[pk])
            if P0STOP == 1:
                break
            V(lambda e, blk=blk: e.tensor_tensor(
                out=modT[:, blk * 4:(blk + 1) * 4, :],
                in0=ps[:, 0:8].rearrange("p (c t) -> p c t", t=2),
                in1=bT[:, blk * 4:(blk + 1) * 4].rearrange("p (c o) -> p c o", o=1).to_broadcast([128, 4, 2]),
                op=ALU.add), r=[pk, "bT"], w=["modT"])
            if blk in (8, 9, 10, 11, 20, 21, 22, 23):
                ro = (blk - 8) * 512 if blk < 12 else 2048 + (blk - 20) * 512
                ps2, pk2 = pp.get()
                for k in range(16):
                    T(lambda e, k=k: e.matmul(ps2[0:2, :], lhsT=scb[:, k, :], rhs=wb[:, k, :],
                                              start=(k == 0), stop=(k == 15)), r=[wk, "scb"], w=[pk2])
                V(lambda e, ro=ro: e.tensor_tensor(out=gtrow[:, ro:ro + 512], in0=ps2[0:2, :], in1=brow[:, ro:ro + 512],
                                                   op=ALU.add), r=[pk2, "brow"], w=["gtrow"])
        def modview(j):
            return modT[:, j * 16:(j + 1) * 16, :].rearrange("p k t -> p t k")
        for (gd, gsrc, jsc, sd, jsh, nm) in ((g1, gmT, 1, shm, 0, "m"), (g2, gfT, 4, shf, 3, "f")):
            for t in range(2):
                V(lambda e, gd=gd, jsc=jsc, t=t: e.tensor_scalar(out=gd[:, t, :], in0=modT[:, jsc * 16:(jsc + 1) * 16, t],
                                                                 scalar1=1.0, scalar2=None, op0=ALU.add),
                  r=["modT"], w=["g" + nm])
                V(lambda e, gd=gd, gsrc=gsrc, t=t: e.tensor_tensor(out=gd[:, t, :], in0=gd[:, t, :], in1=gsrc[:], op=ALU.mult),
                  r=["g" + nm, "gmT", "gfT"], w=["g" + nm])
                V(lambda e, sd=sd, jsh=jsh, t=t: e.tensor_copy(out=sd[:, t, :], in_=modT[:, jsh * 16:(jsh + 1) * 16, t]),
                  r=["modT"], w=["sh" + nm])
        sy.dma("s", GTd, gtrow[:], r=["gtrow"], w=["GTd"])
        sy.barrier()
        stack_holder[0].close()
        stack_holder[0] = None

    if "p1" in phases:
        stack_holder[0] = contextlib.ExitStack()
        flg = sb("flg", [128, 2 * NT1])
        gkb = sb("gkb", [128, 1024]); gglab = sb("gglab", [128, 1024])
        sy.dma("s", flg[:], flagt, w=["flg"])
        sy.dma("s", gkb[:], gk_b, w=["gkb"])
        sy.dma("s", gglab[:], ggla_b, w=["gglab"])
        w1b = sb("w1b", [128, 16, W1C], BF16)
        for k in range(16):
            sy.dma("p", w1b[:, k, :], w1[:, k, :], w=["w1b"])
        wg2f = sb("wg2f", [17, 512]); wg2b = sb("wg2b", [17, 512], BF16)
        sy.dma("s", wg2f[:], wg2, w=["wg2f"])
        V(lambda e: e.tensor_copy(out=wg2b[:], in_=wg2f[:]), r=["wg2f"], w=["wg2b"])
        rKs = [sb(f"rK{i}", [128, 32]) for i in range(2)]; rIs = [sb(f"rI{i}", [128, 16]) for i in range(2)]
        xt = [sb(f"xt{i}", [128, D]) for i in range(1)]
        hT = sb("hT", [128, 16, 128], BF16)
        kf = sb("kf", [128, 1024]); sq = sb("sq", [128, 1024]); vf = sq
        kb = sb("kb", [128, 1024], BF16)
        ktb = sb("ktb", [128, 8, 128], BF16)
        vb = sb("vb", [128, 8, 130], BF16)
        ikf = sb("ikf", [128, 64]); ikb = sb("ikb", [128, 64], BF16); iktb = sb("iktb", [64, 128], BF16)
        gkfs = [sb(f"gkf{i}", [128, 512]) for i in range(2)]; gqfs = [sb(f"gqf{i}", [128, 512]) for i in range(2)]
        gvbs = [sb(f"gvb{i}", [128, 1024], BF16) for i in range(2)]
        glras = [sb(f"glra{i}", [17, 128], BF16) for i in range(2)]
        lg = sb("lg", [128, 512]); ee = sb("ee", [128, 512])
        kpb = sb("kpb", [128, 512], BF16)
        qtl = sb("qtl", [128, 512], BF16); ktl = sb("ktl", [128, 512], BF16)
        qkT = sb("qkT", [128, 8, 128], BF16)
        ATb = sb("ATb", [128, 4, 128], BF16)
        S = sb("S", [128, 1024]); Sb = kb
        dec = sb("dec", [128, 4])
        ogf = sb("ogf", [128, 1024])
        st8 = sb("st8", [128, 16]); rt8 = sb("rt8", [128, 16])
        rtmp = sb("rtmp", [128, 4, 8, 16])
        V(lambda e: e.memset(vb[:], 1.0), w=["vb"])
        for i_ in range(2):
            V(lambda e, i_=i_: e.memset(glras[i_][:], 1.0), w=[("glra", i_)])
        V(lambda e: e.memset(S[:], 0.0), w=["S"])

        def ln_transpose(xtile, xkey, which, hdst, hkey, gsc, gsh, gkeys):
            V(lambda e: e.scalar_tensor_tensor(out=sq[:].bitcast(BF16), in0=xtile, scalar=1.0, in1=xtile, op0=ALU.mult,
                                               op1=ALU.mult, accum_out=st8[:, 0:1]), r=[xkey], w=["sq", "st8"])
            rsqrt_small(rt8[:, 0:1], st8[:, 0:1], 1, 1.0 / D, "st8", "rt8")
            A(lambda e: e.activation(out=xtile, in_=xtile, func=AF.Copy, scale=rt8[:, 0:1]),
              r=[xkey, "rt8"], w=[xkey])
            for half in range(4):
                ps, pk = pp.get()
                for kk in range(4):
                    k = half * 4 + kk
                    T(lambda e, k=k, kk=kk: e.transpose(ps[:, kk * 128:(kk + 1) * 128], xtile[:, k * 128:(k + 1) * 128], IDF),
                      r=[xkey, "cm"], w=[pk])
                for kk in range(4):
                    k = half * 4 + kk
                    if isinstance(which, int):
                        A(lambda e, k=k, kk=kk: e.activation(out=hdst[:, k, :], in_=ps[:, kk * 128:(kk + 1) * 128],
                                                             func=AF.Identity, scale=gsc[:, which, k:k + 1],
                                                             bias=gsh[:, which, k:k + 1]),
                          r=[pk] + gkeys, w=[hkey])
                    else:
                        for (c0, c1, wh) in ((0, 64, 1), (64, 128, 0)):
                            A(lambda e, k=k, kk=kk, c0=c0, c1=c1, wh=wh: e.activation(
                                out=hdst[:, k, c0:c1], in_=ps[:, kk * 128 + c0:kk * 128 + c1], func=AF.Identity,
                                scale=gsc[:, wh, k:k + 1], bias=gsh[:, wh, k:k + 1]), r=[pk] + gkeys, w=[hkey])

        def proj(hsrc, hkey, wtile, wkey, c0, ncols, evac):
            ps, pk = pp.get()
            for k in range(16):
                T(lambda e, k=k: e.matmul(ps[:, 0:ncols], lhsT=hsrc[:, k, :], rhs=wtile[:, k, c0:c0 + ncols],
                                          start=(k == 0), stop=(k == 15)), r=[hkey, wkey], w=[pk])
            evac(ps, pk)

        def headnorm(src, skey, nh, hd, gb, gbkey, dst_keys):
            V(lambda e: e.tensor_tensor(out=sq[:, 0:nh * hd], in0=src, in1=src, op=ALU.mult), r=[skey], w=["sq"])
            V(lambda e: e.tensor_reduce(out=st8[:, 0:nh], in_=sq[:, 0:nh * hd].rearrange("p (h d) -> p h d", h=nh),
                                        axis=AX.X, op=ALU.add), r=["sq"], w=["st8"])
            rsqrt_small(rt8[:, 0:nh], st8[:, 0:nh], nh, 1.0 / hd, "st8", "rt8")
            V(lambda e: e.tensor_tensor(out=src.rearrange("p (h d) -> p h d", h=nh),
                                        in0=src.rearrange("p (h d) -> p h d", h=nh),
                                        in1=rt8[:, 0:nh].rearrange("p (h o) -> p h o", o=1).to_broadcast([128, nh, hd]),
                                        op=ALU.mult), r=[skey, "rt8"], w=[skey])
            V(lambda e: e.tensor_tensor(out=src, in0=src, in1=gb, op=ALU.mult), r=[skey, gbkey], w=[skey])

        def rope(src, skey, nh, hd, half, tab, tkey):
            v3 = src.rearrange("p (h d) -> p h d", h=nh)
            x1 = v3[:, :, 0:half]; x2 = v3[:, :, half:2 * half]
            cb = tab[:, 0:half].rearrange("p (o d) -> p o d", o=1).to_broadcast([128, nh, half])
            sn = tab[:, half:2 * half].rearrange("p (o d) -> p o d", o=1).to_broadcast([128, nh, half])
            t = [rtmp[:, i, 0:nh, 0:half] for i in range(4)]
            V(lambda e: e.tensor_tensor(out=t[0], in0=x1, in1=cb, op=ALU.mult), r=[skey, tkey], w=["rtmp"])
            V(lambda e: e.tensor_tensor(out=t[1], in0=x2, in1=sn, op=ALU.mult), r=[skey, tkey], w=["rtmp"])
            V(lambda e: e.tensor_tensor(out=t[2], in0=x2, in1=cb, op=ALU.mult), r=[skey, tkey], w=["rtmp"])
            V(lambda e: e.tensor_tensor(out=t[3], in0=x1, in1=sn, op=ALU.mult), r=[skey, tkey], w=["rtmp"])
            V(lambda e: e.tensor_tensor(out=x1, in0=t[0], in1=t[1], op=ALU.subtract), r=["rtmp"], w=[skey])
            V(lambda e: e.tensor_tensor(out=x2, in0=t[2], in1=t[3], op=ALU.add), r=["rtmp"], w=[skey])

        def k_store(kt):
            P(lambda e: e.tensor_copy(out=kb[:], in_=kf[:]), r=["kf"], w=["kb"])
            ps, pk = pp.get()
            pv = ps[:].bitcast(BF16) if hasattr(ps[:], "bitcast") else None
            for h in range(8):
                T(lambda e, h=h: e.transpose(pv[:, h * 128:(h + 1) * 128], kb[:, h * 128:(h + 1) * 128], cmb[:]),
                  r=["kb", "cmb"], w=[pk])
            A(lambda e: e.copy(out=ktb[:].rearrange("p h t -> p (h t)"), in_=pv[:, 0:1024]), r=[pk], w=["ktb"])
            sy.dma("s", KTs[kt], ktb[:], r=["ktb"], w=[("KT", kt)])

        def ik_store(kt):
            P(lambda e: e.tensor_copy(out=ikb[:], in_=ikf[:]), r=["ikf"], w=["ikb"])
            ps, pk = pp.get()
            pv = ps[:].bitcast(BF16)
            T(lambda e: e.transpose(pv[0:64, 0:128], ikb[:], cmb[:]), r=["ikb", "cmb"], w=[pk])
            A(lambda e: e.copy(out=iktb[:], in_=pv[0:64, 0:128]), r=[pk], w=["iktb"])
            sy.dma("s", IKTs[:, kt * 128:(kt + 1) * 128], iktb[:], r=["iktb"], w=[("IKT", kt)])

        def v_store(kt):
            sy.dma("s", Vs[kt * 128:(kt + 1) * 128, :], vb[:].rearrange("p h d -> p (h d)"), r=["vb"], w=[("V", kt)])

        import os
        P1STOP = int(os.environ.get("P1STOP", "99"))

        def kv_tile(xsrc, nrows, which, tcol, kt, own_j, og_dst, outs):
            slot = kv_tile.n % 2
            kv_tile.n += 1
            xtile = xt[0]; xkey = ("xt", 0)
            rKt = rKs[slot]; rIt = rIs[slot]
            gkf = gkfs[slot]; gqf = gqfs[slot]; gvb = gvbs[slot]; glra = glras[slot]
            gkk = ("gkf", slot); gqk = ("gqf", slot); gvk = ("gvb", slot); glk = ("glra", slot)
            sy.dma("p", rKt[:], ropeK[:, tcol, :], w=[("rK", slot)])
            sy.dma("p", rIt[:], ropeI[:, tcol, :], w=[("rI", slot)])
            if nrows < 128:
                V(lambda e: e.memset(xtile[:], 0.0), w=[xkey])
            sy.dma("p", xtile[0:nrows, :], xsrc, w=[xkey])
            if P1STOP <= 1:
                return
            ln_transpose(xtile[:], xkey, which, hT, "hT", g1, shm, ["gm"])
            if P1STOP <= 2:
                return
            f01 = flg[:, tcol:tcol + 1]; fn16 = flg[:, NT1 + tcol:NT1 + tcol + 1]
            for b in range(2):
                proj(hT, "hT", w1b, "w1b", C_AK + b * 512, 512,
                     lambda ps, pk, b=b: V(lambda e: e.tensor_copy(out=kf[:, b * 512:(b + 1) * 512], in_=ps[:, :]),
                                           r=[pk], w=["kf"]))
            headnorm(kf[:], "kf", 8, 128, gkb[:], "gkb", None)
            if P1STOP <= 3:
                return
            rope(kf[:], "kf", 8, 128, 16, rKt[:], ("rK", slot))
            if outs is not None and "k" in os.environ.get("P1SEL", "kvi"):
                sy.dma("s", outs["k"], kf[0:nrows, :], r=["kf"], w=["o_k"], is_output=True)
            k_store(kt)
            if P1STOP <= 4:
                return
            for b in range(2):
                def ev(ps, pk, b=b):
                    A(lambda e: e.copy(out=vf[:, b * 512:(b + 1) * 512], in_=ps[:, :]), r=[pk], w=["sq"])
                    P(lambda e: e.tensor_copy(out=vb[:, b * 4:(b + 1) * 4, 0:128],
                                              in_=vf[:, b * 512:(b + 1) * 512].rearrange("p (h d) -> p h d", h=4)),
                      r=["sq"], w=["vb"])
                proj(hT, "hT", w1b, "w1b", C_AV + b * 512, 512, ev)
            if outs is not None and "v" in os.environ.get("P1SEL", "kvi"):
                sy.dma("s", outs["v"], vf[0:nrows, :], r=["sq"], w=["o_v"], is_output=True)
            v_store(kt)
            if P1STOP <= 5:
                return
            proj(hT, "hT", w1b, "w1b", C_IK, 64,
                 lambda ps, pk: V(lambda e: e.tensor_copy(out=ikf[:], in_=ps[:, 0:64]), r=[pk], w=["ikf"]))
            rope(ikf[:], "ikf", 1, 64, 8, rIt[:], ("rI", slot))
            if outs is not None and "i" in os.environ.get("P1SEL", "kvi"):
                sy.dma("s", outs["ik"], ikf[0:nrows, :], r=["ikf"], w=["o_ik"], is_output=True)
            ik_store(kt)
            if P1STOP <= 6:
                return
            proj(hT, "hT", w1b, "w1b", C_GK, 512,
                 lambda ps, pk: V(lambda e: e.tensor_copy(out=gkf[:], in_=ps[:, :]), r=[pk], w=[gkk]))
            for b in range(2):
                proj(hT, "hT", w1b, "w1b", C_GV + b * 512, 512,
                     lambda ps, pk, b=b: A(lambda e: e.activation(out=gvb[:, b * 512:(b + 1) * 512], in_=ps[:, :],
                                                                  func=AF.Copy, scale=f01), r=[pk, "flg"], w=[gvk]))
            ps, pk = pp.get()
            for k in range(16):
                T(lambda e, k=k: e.matmul(ps[0:16, 0:128], lhsT=w1b[:, k, C_GLR:C_GLR + 16], rhs=hT[:, k, :],
                                          start=(k == 0), stop=(k == 15)), r=["hT", "w1b"], w=[pk])
            V(lambda e: e.tensor_copy(out=glra[0:16, :], in_=ps[0:16, 0:128]), r=[pk], w=[glk])
            if og_dst is not None:
                proj(hT, "hT", w1b, "w1b", C_GQ, 512,
                     lambda ps, pk: V(lambda e: e.tensor_copy(out=gqf[:], in_=ps[:, :]), r=[pk], w=[gqk]))
            yield
            ps, pk = pp.get()
            T(lambda e: e.matmul(ps[:, :], lhsT=glra[:], rhs=wg2b[:], start=True, stop=True), r=[glk, "wg2b"], w=[pk])
            A(lambda e: e.activation(out=ee[:], in_=ps[:, :], func=AF.Exp, scale=-1.0), r=[pk], w=["ee"])
            A(lambda e: e.activation(out=ee[:], in_=ee[:], func=AF.Ln, bias=1.0), r=["ee"], w=["ee"])
            V(lambda e: e.tensor_scalar(out=lg[:], in0=ee[:], scalar1=fn16, scalar2=None, op0=ALU.mult),
              r=["ee", "flg"], w=["lg"])
            if P1STOP <= 7:
                return
            need_out = og_dst is not None
            if need_out:
                ps, pk = pp.get()
                T(lambda e: e.matmul(ps[:, :], lhsT=cm[:, 1, :], rhs=lg[:], start=True, stop=True), r=["cm", "lg"], w=[pk])
                A(lambda e: e.activation(out=ee[:], in_=ps[:, :], func=AF.Exp), r=[pk], w=["ee"])
                V(lambda e: e.scalar_tensor_tensor(out=qtl[:], in0=gqf[:], scalar=128.0 ** -0.5, in1=ee[:], op0=ALU.mult,
                                                   op1=ALU.mult), r=[gqk, "ee"], w=["qtl"])
                A(lambda e: e.activation(out=ee[:], in_=ps[:, :], func=AF.Exp, scale=-1.0), r=[pk, "qtl"], w=["ee"])
                V(lambda e: e.tensor_tensor(out=ktl[:], in0=gkf[:], in1=ee[:], op=ALU.mult), r=[gkk, "ee"], w=["ktl"])
                ps, pk = pp.get()
                pv = ps[:].bitcast(BF16)
                for h in range(4):
                    T(lambda e, h=h: e.transpose(pv[:, h * 128:(h + 1) * 128], qtl[:, h * 128:(h + 1) * 128], cmb[:]),
                      r=["qtl", "cmb"], w=[pk])
                    T(lambda e, h=h: e.transpose(pv[:, (4 + h) * 128:(5 + h) * 128], ktl[:, h * 128:(h + 1) * 128], cmb[:]),
                      r=["ktl", "cmb"], w=[pk])
                A(lambda e: e.copy(out=qkT[:].rearrange("p h t -> p (h t)"), in_=pv[:, 0:1024]), r=[pk], w=["qkT"])
                ps, pk = pp.get()
                for h in range(4):
                    T(lambda e, h=h: e.matmul(ps[:, h * 128:(h + 1) * 128], lhsT=qkT[:, 4 + h, :], rhs=qkT[:, h, :],
                                              start=True, stop=True), r=["qkT"], w=[pk])
                V(lambda e: e.tensor_tensor(out=ATb[:], in0=ps[:, :].rearrange("p (h t) -> p h t", h=4),
                                            in1=cm[:, 1, :].rearrange("p (o t) -> p o t", o=1).to_broadcast([128, 4, 128]),
                                            op=ALU.mult), r=[pk, "cm"], w=["ATb"])
                P(lambda e: e.tensor_copy(out=Sb[:], in_=S[:]), r=["S"], w=["kb"])
                pso = [pp.get(), pp.get()]
                for h in range(4):
                    po, pok = pso[h // 2]
                    oc = (h % 2) * 256
                    T(lambda e, h=h, po=po, oc=oc: e.matmul(po[:, oc:oc + 256], lhsT=qkT[:, h, :], rhs=Sb[:, h * 256:(h + 1) * 256],
                                                            start=True, stop=False), r=["qkT", "kb"], w=[pok])
                    T(lambda e, h=h, po=po, oc=oc: e.matmul(po[:, oc:oc + 256], lhsT=ATb[:, h, :], rhs=gvb[:, h * 256:(h + 1) * 256],
                                                            start=False, stop=True), r=["ATb", gvk], w=[pok])
                for i2 in range(2):
                    po, pok = pso[i2]
                    V(lambda e, po=po, i2=i2: e.tensor_copy(out=ogf[:, i2 * 512:(i2 + 1) * 512], in_=po[:, :]), r=[pok], w=["ogf"])
                headnorm(ogf[:], "ogf", 4, 256, gglab[:], "gglab", None)
                lo, hi, drow = og_dst
                sy.dma("s", OGs[drow:drow + (hi - lo), :], ogf[lo:hi, :], r=["ogf"], w=[("OG", drow)])
            if P1STOP <= 8:
                return
            ps, pk = pp.get()
            T(lambda e: e.matmul(ps[:, :], lhsT=cm[:, 2, :], rhs=lg[:], start=True, stop=True), r=["cm", "lg"], w=[pk])
            A(lambda e: e.activation(out=ee[:], in_=ps[:, :], func=AF.Exp), r=[pk, "ktl"], w=["ee"])
            V(lambda e: e.tensor_tensor(out=kpb[:], in0=gkf[:], in1=ee[:], op=ALU.mult), r=[gkk, "ee"], w=["kpb"])
            ps, pk = pp.get()
            for h in range(4):
                T(lambda e, h=h: e.matmul(ps[:, h:h + 1], lhsT=lg[:, h * 128:(h + 1) * 128], rhs=cm[:, 3, 0:1],
                                          start=True, stop=True), r=["lg", "cm"], w=[pk])
            A(lambda e: e.activation(out=dec[:], in_=ps[:, 0:4], func=AF.Exp), r=[pk], w=["dec"])
            psu = [pp.get(), pp.get()]
            for h in range(4):
                pu, puk = psu[h // 2]
                oc = (h % 2) * 256
                T(lambda e, h=h, pu=pu, oc=oc: e.matmul(pu[:, oc:oc + 256], lhsT=kpb[:, h * 128:(h + 1) * 128],
                                                        rhs=gvb[:, h * 256:(h + 1) * 256], start=True, stop=True),
                  r=["kpb", gvk], w=[puk])
            for h in range(4):
                pu, puk = psu[h // 2]
                oc = (h % 2) * 256
                V(lambda e, h=h, pu=pu, oc=oc: e.scalar_tensor_tensor(
                    out=S[:, h * 256:(h + 1) * 256], in0=S[:, h * 256:(h + 1) * 256], scalar=dec[:, h:h + 1],
                    in1=pu[:, oc:oc + 256], op0=ALU.mult, op1=ALU.add), r=["S", "dec", puk], w=["S"])
        kv_tile.n = 0

        ntiles = build_program.ntiles_p1 if hasattr(build_program, "ntiles_p1") else NREL
        conv_jobs = [(w2b[2 * i:2 * i + 2], w2_d[2 * i:2 * i + 2]) for i in range(6)] + [(wiwb, wiw_d)]
        conv_jobs += [(woutb[2 * i:2 * i + 2], wout_d[2 * i:2 * i + 2]) for i in range(4)]
        conv_jobs += [(wupb[2 * i:2 * i + 2], wup_d[2 * i:2 * i + 2]) for i in range(22)]
        conv_jobs += [(wdnb[nb, 11 * i:11 * i + 11], wdn_d[nb, 11 * i:11 * i + 11]) for nb in range(4) for i in range(4)]
        if "pq" not in phases:
            conv_jobs = []
        prev_g = None
        for r in range(ntiles):
            if conv_jobs:
                dst_, src_ = conv_jobs.pop(0)
                sy.dma("p", dst_, src_, w=["wconv"])
            own_j = r // 8 if r % 8 == 7 else None
            halo_j = r // 8 if r % 8 == 6 else None
            outs = None
            og = None
            if own_j is not None:
                outs = {"k": o_k[own_j * 128:(own_j + 1) * 128, :], "v": o_v[own_j * 128:(own_j + 1) * 128, :],
                        "ik": o_ik[own_j * 128:(own_j + 1) * 128, :]}
                og = (0, 128, (1 + own_j) * 128)
            if halo_j is not None:
                og = (126, 128, 64 + 2 * halo_j)
            if os.environ.get("P1OUTS", "1") == "0":
                outs = None
            if os.environ.get("P1OG", "1") == "0" and own_j is not None:
                og = None
            g_ = kv_tile(xp[r * 128:(r + 1) * 128, :], 128, 0, r, r if r < 128 else NKT - 1, own_j, og, outs)
            next(g_, None)
            if prev_g is not None:
                next(prev_g, None)
            prev_g = g_
        if prev_g is not None:
            next(prev_g, None)
        while conv_jobs:
            dst_, src_ = conv_jobs.pop(0)
            sy.dma("p", dst_, src_, w=["wconv"])
        sy.dma("s", o_glap, S[:], r=["S"], w=["o_glap"], is_output=True)
        P1POST = int(os.environ.get("P1POST", "1"))
        for i in range(8 if P1POST else 0):
            ktf = xt[0]; xkey = ("xt", 0)
            sy.dma("s", ktf[:, 0:1024].rearrange("p (h t) -> p h t", h=8), ckT[i], w=[xkey])
            P(lambda e, ktf=ktf: e.tensor_copy(out=ktb[:].rearrange("p h t -> p (h t)"), in_=ktf[:, 0:1024]), r=[xkey], w=["ktb"])
            sy.dma("s", KTs[KT_S0 + i], ktb[:], r=["ktb"], w=[("KT", KT_S0 + i)])
            sy.dma("s", ktf[:, 1024:2048], cv[i * 128:(i + 1) * 128, :], w=[xkey])
            A(lambda e, ktf=ktf: e.copy(out=vb[:, :, 0:128], in_=ktf[:, 1024:2048].rearrange("p (h d) -> p h d", h=8)),
              r=[xkey], w=["vb"])
            v_store(KT_S0 + i)
        ikc = xt[0][0:64, 0:1024]; ikcb = kb[0:64, :]
        if P1POST:
            sy.dma("s", ikc, cikT, w=[("xt", 0)])
            V(lambda e: e.tensor_copy(out=ikcb, in_=ikc), r=[("xt", 0)], w=["kb"])
            sy.dma("s", IKTs[:, KT_S0 * 128:(KT_S0 + 8) * 128], ikcb, r=["kb"], w=[("IKT", "c")])
            sy.dma("s", S[:], sgla, r=["o_glap"], w=["S"])
            outs = {"k": o_ks, "v": o_vs, "ik": o_iks}
            for _ in kv_tile(xs, 64, 1, NT1 - 1, KT_S0 + 8, None, (0, 64, 0), outs):
                pass
        sy.dma("s", o_glas, S[:], r=["S"], w=["o_glas"], is_output=True)
        sy.barrier()
        stack_holder[0].close()
        stack_holder[0] = None

    if "pq" in phases:
        import os
        stack_holder[0] = contextlib.ExitStack()
        pp.n = 5
        pp.i = 0
        OB = [(pp.banks[5 + i], ("ps", 5 + i)) for i in range(3)]
        NITER = int(os.environ.get("NITER", "26"))
        gqb = sb("gqb", [128, 1024])
        sy.dma("s", gqb[:], gq_b, w=["gqb"])
        gtc = [sb(f"gtc{i}", [128, 512]) for i in range(2)]
        uhalo = sb("uhalo", [128, 88, 32])
        convw = sb("convw", [128, 88, 4]); sconv = sb("sconv", [128, 88, 2]); hfl = sb("hfl", [128, 32])
        oconv = sb("oconv", [128, 88, 2])
        sy.dma("s", convw[:], convw_d, w=["convw"])
        sy.dma("s", sconv[:], sconv_d, w=["sconv"])
        sy.dma("s", hfl[:], hflag_d, w=["hfl"])
        xq = sb("xq", [128, D])
        hT = sb("hT", [128, 16, 128], BF16)
        wblk = [sb(f"wblk{i}", [128, 16, 256], BF16) for i in range(2)]
        wiw = sb("wiw", [128, 16, 16], BF16)
        qf = sb("qf", [128, 1024]); sq = sb("sq", [128, 1024]); ogt = sq
        QT = sb("QT", [128, 8, 128], BF16); iqT = sb("iqT", [128, 8, 128], BF16)
        iwf = sb("iwf", [128, 16])
        mixb = sb("mixb", [128, D], BF16); mixT = hT; qb = mixb[:, 0:1024]
        score = sb("score", [128, 16384]); xsq = score[:, 0:2048]
        hidT = score[:, 2048:4864].bitcast(BF16).rearrange("p (i t) -> p i t", i=44)
        IKc = [sb(f"IKc{i}", [128, 512], BF16) for i in range(2)]
        rl = [sb(f"rl{i}", [128, 512]) for i in range(2)]
        bch = [sb(f"bch{i}", [128, 512]) for i in range(2)]
        junk = qf[:].bitcast(BF16)
        bs = sb("bs", [128, 48]); cnts = sb("cnts", [128, 16]); hmx = sb("hmx", [128, 40]); lmx = sb("lmx", [128, 40])
        m01 = [sb(f"m01{i}", [128, 512], BF16) for i in range(2)]
        mT = [sb(f"mT{i}", [128, 512], BF16) for i in range(2)]
        KTc = [sb(f"KTc{i}", [128, 4, 8, 128], BF16) for i in range(2)]
        Vc = [sb(f"Vc{i}", [128, 4, 8 * 130], BF16) for i in range(2)]
        eT = [sb(f"eT{i}", [128, 512], BF16) for i in range(2)]
        PT = [sb(f"PT{i}", [128, 512], BF16) for i in range(2)]
        oa = sb("oa", [128, 8, 129]); rden = sb("rden", [128, 8])
        uexts = [sb(f"uext{i}", [128, 2, 130]) for i in range(2)]; tas = [sb(f"ta{i}", [128, 2, 128]) for i in range(2)]
        sas = [sb(f"sa{i}", [128, 128]) for i in range(2)]
        wdp = [sb(f"wdp{i}", [128, 2, 512], BF16) for i in range(3)]
        tmpy = rl[1]
        rKq = sb("rKq", [128, 32]); rIq = sb("rIq", [128, 16])
        st8 = sb("st8q", [128, 16]); rt8 = sb("rt8q", [128, 16])
        rtmp = sb("rtmpq", [128, 4, 16, 16])

        cntq = {"w": 0, "d": 0, "g": 0}

        def ln_transpose_q(which, gsc, gsh):
            V(lambda e: e.scalar_tensor_tensor(out=xsq, in0=xq[:], scalar=1.0, in1=xq[:], op0=ALU.mult,
                                               op1=ALU.mult, accum_out=st8[:, 0:1]), r=["xq"], w=["score", "st8"])
            rsqrt_small(rt8[:, 0:1], st8[:, 0:1], 1, 1.0 / D, "st8", "rt8")
            A(lambda e: e.activation(out=xsq, in_=xq[:], func=AF.Copy, scale=rt8[:, 0:1]), r=["xq", "rt8"], w=["score"])
            for half in range(4):
                ps, pk = pp.get()
                for kk in range(4):
                    k = half * 4 + kk
                    T(lambda e, k=k, kk=kk: e.transpose(ps[:, kk * 128:(kk + 1) * 128], xsq[:, k * 128:(k + 1) * 128], IDF),
                      r=["score"], w=[pk])
                for kk in range(4):
                    k = half * 4 + kk
                    A(lambda e, k=k, kk=kk: e.activation(out=hT[:, k, :], in_=ps[:, kk * 128:(kk + 1) * 128], func=AF.Identity,
                                                         scale=gsc[:, which, k:k + 1], bias=gsh[:, which, k:k + 1]),
                      r=[pk], w=["hT"])

        def wload(src):
            i = cntq["w"] % 2
            cntq["w"] += 1
            sy.dma("s", wblk[i][:], src, w=[("wblk", i)])
            return wblk[i], ("wblk", i)

        def projq(lhs, lkey, wt, wkey, c0, ncols, evac):
            ps, pk = pp.get()
            for k in range(16):
                T(lambda e, k=k: e.matmul(ps[:, 0:ncols], lhsT=lhs[:, k, :], rhs=wt[:, k, c0:c0 + ncols],
                                          start=(k == 0), stop=(k == 15)), r=[lkey, wkey], w=[pk])
            evac(ps, pk)

        def headnorm_q(src, skey, nh, hd, gb, gbkey):
            V(lambda e: e.tensor_tensor(out=sq[:, 0:nh * hd], in0=src, in1=src, op=ALU.mult), r=[skey], w=["sq"])
            V(lambda e: e.tensor_reduce(out=st8[:, 0:nh], in_=sq[:, 0:nh * hd].rearrange("p (h d) -> p h d", h=nh),
                                        axis=AX.X, op=ALU.add), r=["sq"], w=["st8"])
            rsqrt_small(rt8[:, 0:nh], st8[:, 0:nh], nh, 1.0 / hd, "st8", "rt8")
            V(lambda e: e.tensor_tensor(out=src.rearrange("p (h d) -> p h d", h=nh),
                                        in0=src.rearrange("p (h d) -> p h d", h=nh),
                                        in1=rt8[:, 0:nh].rearrange("p (h o) -> p h o", o=1).to_broadcast([128, nh, hd]),
                                        op=ALU.mult), r=[skey, "rt8"], w=[skey])
            V(lambda e: e.tensor_tensor(out=src, in0=src, in1=gb, op=ALU.mult), r=[skey, gbkey], w=[skey])

        def rope_q(src, skey, nh, hd, half, tab, tkey):
            v3 = src.rearrange("p (h d) -> p h d", h=nh)
            x1 = v3[:, :, 0:half]; x2 = v3[:, :, half:2 * half]
            cb = tab[:, 0:half].rearrange("p (o d) -> p o d", o=1).to_broadcast([128, nh, half])
            sn = tab[:, half:2 * half].rearrange("p (o d) -> p o d", o=1).to_broadcast([128, nh, half])
            t = [rtmp[:, i, 0:nh, 0:half] for i in range(4)]
            V(lambda e: e.tensor_tensor(out=t[0], in0=x1, in1=cb, op=ALU.mult), r=[skey, tkey], w=["rtmp"])
            V(lambda e: e.tensor_tensor(out=t[1], in0=x2, in1=sn, op=ALU.mult), r=[skey, tkey], w=["rtmp"])
            V(lambda e: e.tensor_tensor(out=t[2], in0=x2, in1=cb, op=ALU.mult), r=[skey, tkey], w=["rtmp"])
            V(lambda e: e.tensor_tensor(out=t[3], in0=x1, in1=sn, op=ALU.mult), r=[skey, tkey], w=["rtmp"])
            V(lambda e: e.tensor_tensor(out=x1, in0=t[0], in1=t[1], op=ALU.subtract), r=["rtmp"], w=[skey])
            V(lambda e: e.tensor_tensor(out=x2, in0=t[2], in1=t[3], op=ALU.add), r=["rtmp"], w=[skey])

        def q_tile(kind, j):
            which = 1 if kind == "sample" else 0
            if kind == "own":
                r0 = (8 * j + 7) * 128
                sy.dma("s", xq[:], xp[r0:r0 + 128, :], w=["xq"])
                sy.dma("s", rKq[:], ropeK[:, 8 * j + 7, :], w=["rKq"])
                sy.dma("s", rIq[:], ropeI[:, 8 * j + 7, :], w=["rIq"])
                nrows = 128
            elif kind == "sample":
                V(lambda e: e.memset(xq[:], 0.0), w=["xq"])
                sy.dma("s", xq[0:64, :], xs, w=["xq"])
                sy.dma("s", rKq[:], ropeK[:, NT1 - 1, :], w=["rKq"])
                sy.dma("s", rIq[:], ropeI[:, NT1 - 1, :], w=["rIq"])
                nrows = 64
            else:
                V(lambda e: e.memset(xq[:], 0.0), w=["xq"])
                for jj in range(16):
                    r0 = (8 * jj + 6) * 128 + 126
                    sy.dma("s", xq[2 * jj:2 * jj + 2, :], xp[r0:r0 + 2, :], w=["xq"])
                sy.dma("s", rKq[:], ropeKh_d, w=["rKq"])
                sy.dma("s", rIq[:], ropeIh_d, w=["rIq"])
                nrows = 32
            ln_transpose_q(which, g1, shm)
            for b in range(4):
                wt, wk = wload(w2b[b])
                projq(hT, "hT", wt, wk, 0, 256,
                      lambda ps, pk, b=b: A(lambda e: e.copy(out=qf[:, b * 256:(b + 1) * 256], in_=ps[:, 0:256]), r=[pk], w=["qf"]))
            headnorm_q(qf[:], "qf", 8, 128, gqb[:], "gqb")
            rope_q(qf[:], "qf", 8, 128, 16, rKq[:], "rKq")
            V(lambda e: e.tensor_copy(out=qb, in_=qf[:]), r=["qf"], w=["mixb"])
            ps, pk = pp.get()
            pv = ps[:].bitcast(BF16)
            for h in range(8):
                T(lambda e, h=h: e.transpose(pv[:, h * 128:(h + 1) * 128], qb[:, h * 128:(h + 1) * 128], cmb[:]), r=["mixb"], w=[pk])
            A(lambda e: e.copy(out=QT[:].rearrange("p h t -> p (h t)"), in_=pv[:, 0:1024]), r=[pk], w=["QT"])
            for b in range(4):
                wt, wk = wload(w2b[4 + b])
                projq(hT, "hT", wt, wk, 0, 256,
                      lambda ps, pk, b=b: A(lambda e: e.copy(out=qf[:, b * 256:(b + 1) * 256], in_=ps[:, 0:256]), r=[pk], w=["qf"]))
            rope_q(qf[:], "qf", 16, 64, 8, rIq[:], "rIq")
            V(lambda e: e.tensor_copy(out=qb, in_=qf[:]), r=["qf"], w=["mixb"])
            ps, pk = pp.get()
            pv = ps[:].bitcast(BF16)
            for h in range(8):
                T(lambda e, h=h: e.transpose(pv[:, h * 128:(h + 1) * 128], qb[:, h * 128:(h + 1) * 128], cmb[:]), r=["mixb"], w=[pk])
            A(lambda e: e.copy(out=iqT[:].rearrange("p h t -> p (h t)"), in_=pv[:, 0:1024]), r=[pk], w=["iqT"])
            sy.dma("s", wiw[:], wiwb, w=["wiw"])
            ps, pk = pp.get()
            for k in range(16):
                T(lambda e, k=k: e.matmul(ps[:, 0:16], lhsT=hT[:, k, :], rhs=wiw[:, k, :], start=(k == 0), stop=(k == 15)),
                  r=["hT", "wiw"], w=[pk])
            V(lambda e: e.tensor_copy(out=iwf[:], in_=ps[:, 0:16]), r=[pk], w=["iwf"])
            if nrows < 128:
                V(lambda e: e.memset(ogt[:], 0.0), w=["sq"])
            if kind == "own":
                sy.dma("s", ogt[:], OGs[(1 + j) * 128:(2 + j) * 128, :], w=["sq"])
            elif kind == "sample":
                sy.dma("s", ogt[0:64, :], OGs[0:64, :], w=["sq"])
            else:
                sy.dma("s", ogt[0:32, :], OGs[64:96, :], w=["sq"])
            for b in range(4):
                wt, wk = wload(w2b[8 + b])
                def evg(ps, pk, b=b):
                    A(lambda e: e.activation(out=qf[:, b * 256:(b + 1) * 256], in_=ps[:, 0:256], func=AF.Silu), r=[pk], w=["qf"])
                    V(lambda e: e.tensor_tensor(out=mixb[:, 1024 + b * 256:1024 + (b + 1) * 256], in0=qf[:, b * 256:(b + 1) * 256],
                                                in1=ogt[:, b * 256:(b + 1) * 256], op=ALU.mult), r=["qf", "sq"], w=["mixb"])
                projq(hT, "hT", wt, wk, 0, 256, evg)

            if kind == "own":
                nkt = 8 * j + 8
                chunks = [(4 * ci, 4) for ci in range(nkt // 4)]
            elif kind == "sample":
                chunks = [(KT_S0, 4), (KT_S0 + 4, 4), (KT_S0 + 8, 1)]
            else:
                chunks = [(4 * ci, 4) for ci in range(32)]
            nch = len(chunks)
            L = sum(n for _, n in chunks) * 128
            coff = [sum(n for _, n in chunks[:i]) * 128 for i in range(nch)]

            def bias_src(ci):
                if kind == "own":
                    last = (ci == nch - 1)
                    if ci == 0:
                        return biasO_d[0]
                    if ci == 1:
                        return biasO_d[2] if last else biasO_d[1]
                    return biasO_d[3] if last else None
                if kind == "sample":
                    return biasS_d if ci == nch - 1 else None
                return biasH_d[:, ci * 512:(ci + 1) * 512]

            for ci, (kt0, nk) in enumerate(chunks):
                nc_ = nk * 128
                ik = IKc[ci % 2]; ikk = ("IKc", ci % 2)
                sy.dma("s", ik[0:64, 0:nc_], IKTs[:, kt0 * 128:kt0 * 128 + nc_], w=[ikk])
                sy.dma("s", ik[64:128, 0:nc_], IKTs[:, kt0 * 128:kt0 * 128 + nc_], w=[ikk])
                bsrc = bias_src(ci)
                bt = bch[ci % 2]; bk = ("bch", ci % 2)
                if bsrc is not None:
                    sy.dma("s", bt[:, 0:nc_], bsrc if kind != "sample" else bsrc, w=[bk])
                sc = score[:, coff[ci]:coff[ci] + nc_]
                for h in range(16):
                    hp, lo = h // 2, (h % 2) * 64
                    ps, pk = pp.get()
                    T(lambda e, hp=hp, lo=lo, ps=ps: e.matmul(ps[:, 0:nc_], lhsT=iqT[lo:lo + 64, hp, :], rhs=ik[lo:lo + 64, 0:nc_],
                                                              start=True, stop=True), r=["iqT", ikk], w=[pk])
                    rt = rl[h % 2]; rk = ("rl", h % 2)
                    A(lambda e, ps=ps, rt=rt: e.activation(out=rt[:, 0:nc_], in_=ps[:, 0:nc_], func=AF.Relu), r=[pk], w=[rk])
                    if h == 0:
                        if bsrc is not None:
                            V(lambda e, rt=rt: e.scalar_tensor_tensor(out=sc, in0=rt[:, 0:nc_], scalar=iwf[:, 0:1], in1=bt[:, 0:nc_],
                                                                      op0=ALU.mult, op1=ALU.add), r=[rk, "iwf", bk], w=["score"])
                        else:
                            V(lambda e, rt=rt: e.tensor_scalar(out=sc, in0=rt[:, 0:nc_], scalar1=iwf[:, 0:1], scalar2=None, op0=ALU.mult),
                              r=[rk, "iwf"], w=["score"])
                    else:
                        V(lambda e, rt=rt, h=h: e.scalar_tensor_tensor(out=sc, in0=rt[:, 0:nc_], scalar=iwf[:, h:h + 1], in1=sc,
                                                                       op0=ALU.mult, op1=ALU.add), r=[rk, "iwf", "score"], w=["score"])
                V(lambda e, ci=ci: e.tensor_reduce(out=hmx[:, ci:ci + 1], in_=sc, axis=AX.X, op=ALU.max), r=["score"], w=["hmx"])
                rt = rl[0]; rk = ("rl", 0)
                V(lambda e, rt=rt: e.tensor_scalar(out=rt[:, 0:nc_], in0=sc, scalar1=-1.0e29, scalar2=-3.0e30, op0=ALU.is_lt, op1=ALU.mult),
                  r=["score"], w=[rk])
                V(lambda e, rt=rt: e.scalar_tensor_tensor(out=rt[:, 0:nc_], in0=sc, scalar=-1.0, in1=rt[:, 0:nc_], op0=ALU.mult, op1=ALU.add),
                  r=["score", rk], w=[rk])
                V(lambda e, rt=rt, ci=ci: e.tensor_reduce(out=lmx[:, ci:ci + 1], in_=rt[:, 0:nc_], axis=AX.X, op=ALU.max), r=[rk], w=["lmx"])
            V(lambda e: e.tensor_reduce(out=bs[:, 0:1], in_=hmx[:, 0:nch], axis=AX.X, op=ALU.max), r=["hmx"], w=["bs"])
            V(lambda e: e.tensor_reduce(out=bs[:, 1:2], in_=lmx[:, 0:nch], axis=AX.X, op=ALU.max), r=["lmx"], w=["bs"])
            V(lambda e: e.tensor_scalar(out=bs[:, 2:3], in0=bs[:, 1:2], scalar1=-1.0, scalar2=-1.0, op0=ALU.mult, op1=ALU.add),
              r=["bs"], w=["bs"])
            V(lambda e: e.tensor_tensor(out=bs[:, 3:4], in0=bs[:, 0:1], in1=bs[:, 2:3], op=ALU.subtract), r=["bs"], w=["bs"])
            nseg = (L + 2047) // 2048
            for it in range(NITER):
                fac = 0.5 ** (it + 1)
                V(lambda e, fac=fac: e.scalar_tensor_tensor(out=bs[:, 4:5], in0=bs[:, 3:4], scalar=fac, in1=bs[:, 2:3],
                                                            op0=ALU.mult, op1=ALU.add), r=["bs"], w=["bs"])
                for sg in range(nseg):
                    c0 = sg * 2048; c1 = min(L, c0 + 2048)
                    V(lambda e, c0=c0, c1=c1, sg=sg: e.tensor_scalar(out=junk[:, 0:c1 - c0], in0=score[:, c0:c1], scalar1=bs[:, 4:5],
                                                                     scalar2=0.0, op0=ALU.is_gt, op1=ALU.add,
                                                                     accum_out=cnts[:, sg:sg + 1]), r=["score", "bs"], w=["qf", "cnts"])
                V(lambda e: e.tensor_reduce(out=bs[:, 5:6], in_=cnts[:, 0:nseg], axis=AX.X, op=ALU.add), r=["cnts"], w=["bs"])
                V(lambda e, fac=fac: e.tensor_scalar(out=bs[:, 6:7], in0=bs[:, 5:6], scalar1=255.5, scalar2=fac, op0=ALU.is_gt, op1=ALU.mult),
                  r=["bs"], w=["bs"])
                V(lambda e: e.scalar_tensor_tensor(out=bs[:, 2:3], in0=bs[:, 3:4], scalar=bs[:, 6:7], in1=bs[:, 2:3],
                                                   op0=ALU.mult, op1=ALU.add), r=["bs"], w=["bs"])
            nob = 0
            for ci, (kt0, nk) in enumerate(chunks):
                nc_ = nk * 128
                kc = KTc[ci % 2]; kck = ("KTc", ci % 2)
                vc = Vc[ci % 2]; vck = ("Vc", ci % 2)
                sy.dma("s", kc[:, 0:nk], KTs[kt0:kt0 + nk].rearrange("t p h s -> p t h s"), w=[kck])
                sy.dma("s", vc[:, 0:nk, :], Vs[kt0 * 128:(kt0 + nk) * 128, :].rearrange("(t p) c -> p t c", p=128), w=[vck])
                mm = m01[ci % 2]; mmk = ("m01", ci % 2)
                V(lambda e, mm=mm: e.tensor_scalar(out=mm[:, 0:nc_], in0=score[:, coff[ci]:coff[ci] + nc_], scalar1=bs[:, 2:3],
                                                   scalar2=None, op0=ALU.is_gt), r=["score", "bs"], w=[mmk])
                ps, pk = pp.get()
                pv = ps[:].bitcast(BF16)
                for t in range(nk):
                    T(lambda e, t=t, mm=mm: e.transpose(pv[:, t * 128:(t + 1) * 128], mm[:, t * 128:(t + 1) * 128], cmb[:]), r=[mmk], w=[pk])
                mt = mT[ci % 2]; mtk = ("mT", ci % 2)
                A(lambda e, mt=mt: e.copy(out=mt[:, 0:nc_], in_=pv[:, 0:nc_]), r=[pk], w=[mtk])
                for h in range(8):
                    ps, pk = pp.get()
                    for t in range(nk):
                        T(lambda e, t=t, h=h, ps=ps: e.matmul(ps[:, t * 128:(t + 1) * 128], lhsT=kc[:, t, h, :], rhs=QT[:, h, :],
                                                              start=True, stop=True), r=[kck, "QT"], w=[pk])
                    et = eT[h % 2]; ek = ("eT", h % 2)
                    A(lambda e, et=et, ps=ps: e.activation(out=et[:, 0:nc_], in_=ps[:, 0:nc_], func=AF.Exp, scale=128.0 ** -0.5),
                      r=[pk], w=[ek])
                    pt = PT[h % 2]; ptk = ("PT", h % 2)
                    V(lambda e, et=et, pt=pt, mt=mt: e.tensor_tensor(out=pt[:, 0:nc_], in0=et[:, 0:nc_], in1=mt[:, 0:nc_], op=ALU.mult),
                      r=[ek, mtk], w=[ptk])
                    ob, obk = OB[h // 3]
                    oc = (h % 3) * 129
                    for t in range(nk):
                        T(lambda e, t=t, h=h, pt=pt, ob=ob, oc=oc: e.matmul(
                            ob[:, oc:oc + 129], lhsT=pt[:, t * 128:(t + 1) * 128], rhs=vc[:, t, h * 130:h * 130 + 129],
                            start=(t == 0), stop=(t == nk - 1)), r=[ptk, vck], w=[obk])
                    if h in (2, 5, 7):
                        bi = h // 3
                        nh_ = 3 if bi < 2 else 2
                        oav = oa[:, bi * 3:bi * 3 + nh_, :].rearrange("p h d -> p (h d)")
                        if ci == 0:
                            V(lambda e, ob=ob, oav=oav, nh_=nh_: e.tensor_copy(out=oav, in_=ob[:, 0:nh_ * 129]), r=[obk], w=["oa"])
                        else:
                            V(lambda e, ob=ob, oav=oav, nh_=nh_: e.tensor_tensor(out=oav, in0=oav, in1=ob[:, 0:nh_ * 129], op=ALU.add),
                              r=[obk, "oa"], w=["oa"])
            V(lambda e: e.tensor_scalar(out=rden[:], in0=oa[:, :, 128:129].rearrange("p h o -> p (h o)"), scalar1=1.0e-30, scalar2=None, op0=ALU.max), r=["oa"], w=["rden"])
            V(lambda e: e.reciprocal(out=rden[:], in_=rden[:]), r=["rden"], w=["rden"])
            V(lambda e: e.tensor_tensor(out=mixb[:, 0:1024].rearrange("p (h d) -> p h d", h=8), in0=oa[:, :, 0:128],
                                        in1=rden[:].rearrange("p (h o) -> p h o", o=1).to_broadcast([128, 8, 128]), op=ALU.mult),
              r=["oa", "rden"], w=["mixb"])
            if DBG:
                V(lambda e: e.tensor_copy(out=xsq, in_=mixb[:]), r=["mixb"], w=["score"])
                sy.dma("s", dbg_mix, xsq, r=["score"], w=["dbg_mix"], is_output=True)
            for half in range(2):
                ps, pk = pp.get()
                pv = ps[:].bitcast(BF16)
                for kk in range(8):
                    k = half * 8 + kk
                    T(lambda e, k=k, kk=kk: e.transpose(pv[:, kk * 128:(kk + 1) * 128], mixb[:, k * 128:(k + 1) * 128], cmb[:]), r=["mixb"], w=[pk])
                A(lambda e, half=half: e.copy(out=mixT[:, half * 8:(half + 1) * 8, :].rearrange("p h t -> p (h t)"), in_=pv[:, 0:1024]),
                  r=[pk], w=["hT"])
            for b in range(8):
                wt, wk = wload(woutb[b])
                gi = cntq["g"] % 2
                cntq["g"] += 1
                sy.dma("s", gtc[gi][:, 0:256], GTd[which:which + 1, b * 256:(b + 1) * 256].to_broadcast([128, 256]), w=[("gtc", gi)])
                def evo(ps, pk, b=b, gi=gi):
                    V(lambda e: e.tensor_tensor(out=tmpy[:, 0:256], in0=ps[:, 0:256], in1=gtc[gi][:, 0:256], op=ALU.mult),
                      r=[pk, ("gtc", gi)], w=[("rl", 1)])
                    V(lambda e: e.tensor_tensor(out=xq[:, b * 256:(b + 1) * 256], in0=xq[:, b * 256:(b + 1) * 256], in1=tmpy[:, 0:256], op=ALU.add),
                      r=[("rl", 1), "xq"], w=["xq"])
                projq(mixT, "hT", wt, wk, 0, 256, evo)
            if DBG:
                sy.dma("s", dbg_xm, xq[:], r=["xq"], w=["dbg_xm"], is_output=True)
            ln_transpose_q(which, g2, shf)
            for i in range(44):
                wt, wk = wload(wupb[i])
                uext = uexts[i % 2]; ta = tas[i % 2]; sa = sas[i % 2]
                uk = ("uext", i % 2); tk = ("ta", i % 2); sk = ("sa", i % 2)
                ps, pk = pp.get()
                for ab in range(2):
                    for k in range(16):
                        T(lambda e, k=k, ab=ab, ps=ps: e.matmul(ps[:, ab * 128:(ab + 1) * 128], lhsT=wt[:, k, ab * 128:(ab + 1) * 128],
                                                                rhs=hT[:, k, :], start=(k == 0), stop=(k == 15)), r=["hT", wk], w=[pk])
                A(lambda e, ps=ps: e.copy(out=uext[:, :, 2:130], in_=ps[:, 0:256].rearrange("p (a t) -> p a t", a=2)), r=[pk], w=[uk])
                for ab in range(2):
                    ch = ab * 44 + i
                    if kind == "own":
                        V(lambda e, ab=ab, ch=ch: e.tensor_copy(out=uext[:, ab, 0:2], in_=uhalo[:, ch, 2 * j:2 * j + 2]), r=["uhalo"], w=[uk])
                    elif kind == "sample":
                        V(lambda e, ab=ab, ch=ch: e.tensor_copy(out=uext[:, ab, 0:2], in_=sconv[:, ch, :]), r=["sconv"], w=[uk])
                    else:
                        V(lambda e, ab=ab: e.memset(uext[:, ab, 0:2], 0.0), w=[uk])
                    if kind == "halo":
                        V(lambda e, ab=ab, ch=ch: e.tensor_tensor(out=uhalo[:, ch, :], in0=uext[:, ab, 2:34], in1=hfl[:], op=ALU.mult),
                          r=[uk, "hfl"], w=["uhalo"])
                    if kind == "sample":
                        V(lambda e, ab=ab, ch=ch: e.tensor_copy(out=oconv[:, ch, :], in_=uext[:, ab, 64:66]), r=[uk], w=["oconv"])
                    if kind == "own" and j == 15:
                        V(lambda e, ab=ab, ch=ch: e.tensor_copy(out=oconv[:, ch, :], in_=uext[:, ab, 128:130]), r=[uk], w=["oconv"])
                    A(lambda e, ab=ab, ch=ch: e.activation(out=ta[:, ab, :], in_=uext[:, ab, 2:130], func=AF.Identity,
                                                           scale=convw[:, ch, 2:3], bias=convw[:, ch, 3:4]), r=[uk, "convw"], w=[tk])
                    V(lambda e, ab=ab, ch=ch: e.scalar_tensor_tensor(out=ta[:, ab, :], in0=uext[:, ab, 1:129], scalar=convw[:, ch, 1:2],
                                                                     in1=ta[:, ab, :], op0=ALU.mult, op1=ALU.add), r=[uk, tk, "convw"], w=[tk])
                    V(lambda e, ab=ab, ch=ch: e.scalar_tensor_tensor(out=ta[:, ab, :], in0=uext[:, ab, 0:128], scalar=convw[:, ch, 0:1],
                                                                     in1=ta[:, ab, :], op0=ALU.mult, op1=ALU.add), r=[uk, tk, "convw"], w=[tk])
                A(lambda e: e.activation(out=sa[:], in_=ta[:, 0, :], func=AF.Silu), r=[tk], w=[sk])
                V(lambda e, i=i: e.tensor_tensor(out=hidT[:, i, :], in0=sa[:], in1=ta[:, 1, :], op=ALU.mult), r=[sk, tk], w=["score"])
            if kind == "sample":
                sy.dma("s", o_convs, oconv[:], r=["oconv"], w=["o_convs"], is_output=True)
            if kind == "own" and j == 15:
                sy.dma("s", o_convp, oconv[:], r=["oconv"], w=["o_convp"], is_output=True)
            if kind != "halo":
                for nb in range(4):
                    ps, pk = pp.get()
                    for i2 in range(22):
                        di = cntq["d"] % 3
                        cntq["d"] += 1
                        sy.dma("s", wdp[di][:], wdnb[nb, 2 * i2:2 * i2 + 2].rearrange("i p c -> p i c"), w=[("wdp", di)])
                        for ii in range(2):
                            i = 2 * i2 + ii
                            T(lambda e, i=i, ii=ii, di=di, ps=ps: e.matmul(ps[:, :], lhsT=hidT[:, i, :], rhs=wdp[di][:, ii, :], start=(i == 0), stop=(i == 43)),
                              r=["score", ("wdp", di)], w=[pk])
                    gi = cntq["g"] % 2
                    cntq["g"] += 1
                    sy.dma("s", gtc[gi][:], GTd[which:which + 1, 2048 + nb * 512:2048 + (nb + 1) * 512].to_broadcast([128, 512]), w=[("gtc", gi)])
                    V(lambda e, ps=ps, nb=nb, gi=gi: e.tensor_tensor(out=tmpy[:, 0:512], in0=ps[:, :], in1=gtc[gi][:], op=ALU.mult),
                      r=[pk, ("gtc", gi)], w=[("rl", 1)])
                    V(lambda e, nb=nb: e.tensor_tensor(out=xq[:, nb * 512:(nb + 1) * 512], in0=xq[:, nb * 512:(nb + 1) * 512], in1=tmpy[:, 0:512], op=ALU.add),
                      r=[("rl", 1), "xq"], w=["xq"])
                if kind == "own":
                    sy.dma("s", o_y[j * 128:(j + 1) * 128, :], xq[:], r=["xq"], w=["o_y"], is_output=True)
                else:
                    sy.dma("s", o_ys, xq[0:64, :], r=["xq"], w=["o_ys"], is_output=True)

        QT_LIST = os.environ.get("QTILES", "all")
        tl = [("sample", 0), ("halo", 0)] + [("own", j) for j in range(16)]
        if QT_LIST != "all":
            tl = [tl[int(x)] for x in QT_LIST.split(",")]
        for kind, j in tl:
            q_tile(kind, j)
        sy.barrier()
        stack_holder[0].close()
        stack_holder[0] = None

    sy.finish()
    return nc


def _rope_tab(pos, half, theta=500000.0):
    inv = (np.float32(theta) ** (-(np.arange(half, dtype=np.float32) / np.float32(half)))).astype(np.float32)
    ang = (pos.astype(np.float32)[:, None] * inv[None, :]).astype(np.float32)
    return np.concatenate([np.cos(ang.astype(np.float64)), np.sin(ang.astype(np.float64))], axis=1).astype(np.float32)


def _host_inputs(inp):
    f = np.float32
    x_prompt = np.asarray(inp["x_prompt"], f)[0]
    xpad = np.zeros(((128 + 14) * 128, D), f)
    xpad[7 * 128:(7 + 128) * 128] = x_prompt
    w_in = np.asarray(inp["w_in"], f)[0]
    offs = np.cumsum([0, 1024, 1024, 1024, 1024, 64, 16, 512, 512, 1024, 1024, 16])
    aq, ak, av, iq, ik, iw, gq, gk, gv, gr, glr = [w_in[:, offs[i]:offs[i + 1]] for i in range(11)]

    def kmaj(w):
        return np.ascontiguousarray(w.reshape(16, 128, -1).transpose(1, 0, 2))
    w1 = kmaj(np.concatenate([ak, av, gk, gv, gq, ik, glr], axis=1))
    w_ada = np.asarray(inp["w_ada"], f)[0]
    wada = np.ascontiguousarray(w_ada.reshape(16, 128, 24, 512).transpose(2, 1, 0, 3))
    b_ada = np.asarray(inp["b_ada"], f)[0]
    badaT = np.ascontiguousarray(b_ada.reshape(96, 128).T)
    brow = np.concatenate([b_ada[4096:6144], b_ada[10240:12288]])
    badaR = np.ascontiguousarray(np.stack([brow, brow]))
    cmat = np.zeros((128, 6, 128), f)
    ii = np.arange(128)
    cmat[:, 0, :] = np.eye(128)
    cmat[:, 1, :] = (ii[:, None] <= ii[None, :])
    cmat[:, 2, :] = (ii[:, None] > ii[None, :])
    cmat[:, 3, :] = 1.0
    cmat[0, 4, :] = 1.0
    cmat[0, 5, 64:] = 1.0
    cmat[1, 5, :64] = 1.0
    common = {
        "badaT": badaT, "badaR": badaR,
        "gmixT": np.ascontiguousarray(np.asarray(inp["g_mix"], f)[0].reshape(16, 128).T),
        "gffnT": np.ascontiguousarray(np.asarray(inp["g_ffn"], f)[0].reshape(16, 128).T),
        "wada": wada, "w1": w1,
        "gk_b": np.ascontiguousarray(np.broadcast_to(np.tile(np.asarray(inp["g_k"], f)[0], 8)[None, :], (128, 1024))),
        "gq_b": np.ascontiguousarray(np.broadcast_to(np.tile(np.asarray(inp["g_q"], f)[0], 8)[None, :], (128, 1024))),
        "ggla_b": np.ascontiguousarray(np.broadcast_to(np.tile(np.asarray(inp["g_gla"], f)[0], 4)[None, :], (128, 1024))),
        "wg2": np.ascontiguousarray(np.concatenate([np.asarray(inp["w_gate2"], f)[0], np.asarray(inp["b_gate2"], f)[0][None, :]], 0)),
        "cmat": cmat,
    }
    w_out = np.asarray(inp["w_out"], f)[0]
    w_up = np.asarray(inp["w_up"], f)[0]
    w_down = np.asarray(inp["w_down"], f)[0]
    k2 = kmaj(np.concatenate([aq, iq, gr], axis=1))
    common["w2"] = np.ascontiguousarray(k2.reshape(128, 16, 12, 256).transpose(2, 0, 1, 3))
    common["wiw"] = kmaj(iw)
    common["wout"] = np.ascontiguousarray(kmaj(w_out).reshape(128, 16, 8, 256).transpose(2, 0, 1, 3))
    ku = kmaj(w_up)
    common["wup"] = np.ascontiguousarray(np.concatenate(
        [ku[:, :, :DFF].reshape(128, 16, 44, 128), ku[:, :, DFF:].reshape(128, 16, 44, 128)], axis=3).transpose(2, 0, 1, 3))
    common["wdn"] = np.ascontiguousarray(w_down.reshape(44, 128, 4, 512).transpose(2, 0, 1, 3))
    cw = np.concatenate([np.asarray(inp["w_conv"], f)[0], np.asarray(inp["b_conv"], f)[0][None, :]], axis=0)
    common["convw"] = np.ascontiguousarray(cw.reshape(4, 88, 128).transpose(2, 1, 0))
    bS = np.zeros((128, 128), f); bS[:, 64:] = -BIG
    common["biasS"] = bS
    maps = []
    for c in range(NCORE):
        m = dict(common)
        sc_ = np.asarray(inp["state_ffn_conv"], f)[0, c]
        m["sconv"] = np.ascontiguousarray(sc_.reshape(2, 88, 128).transpose(2, 1, 0))
        npad = (7 - c) * 128
        bO = np.zeros((4, 128, 512), f)
        padrow = np.zeros(1024, f); padrow[:npad] = -BIG
        bO[0] = padrow[None, 0:512]; bO[1] = padrow[None, 512:1024]
        diag = np.zeros((128, 512), f); diag[0:64, 448:512] = -BIG
        bO[2] = bO[1] + diag; bO[3] = diag
        m["biasO"] = bO
        bH = np.full((128, 16384), -BIG, f)
        hfl = np.zeros((128, 32), f)
        posh = np.zeros(128, np.int64)
        for jj in range(16):
            a = 8 * jj + c - 1
            if a >= 0:
                for e_ in range(2):
                    bH[2 * jj + e_, npad:(8 * jj + 7) * 128] = 0.0
                    hfl[:, 2 * jj + e_] = 1.0
                    posh[2 * jj + e_] = a * 128 + 126 + e_
        m["biasH"] = bH
        m["hflag"] = hfl
        m["ropeKh"] = _rope_tab(posh, 16)
        m["ropeIh"] = _rope_tab(posh, 8)
        m["xp"] = xpad[c * 128:(c + NREL) * 128]
        m["xs"] = np.ascontiguousarray(np.asarray(inp["x_sample"], f)[c])
        cT = np.stack([np.asarray(inp["c_prompt"], f)[0], np.asarray(inp["c_sample"], f)[c]], axis=1)
        m["cT"] = np.ascontiguousarray(cT.reshape(16, 128, 2).transpose(1, 0, 2))
        flag = np.zeros((128, 2 * NT1), f)
        posK = np.zeros((NT1, 128), np.int64)
        for r in range(NREL):
            a = r - (7 - c)
            if 0 <= a < 128:
                flag[:, r] = 1.0
                posK[r] = a * 128 + np.arange(128)
        flag[:64, NT1 - 1] = 1.0
        posK[NT1 - 1, :64] = 1024 + np.arange(64)
        flag[:, NT1:] = -flag[:, :NT1] / 16.0
        m["flagt"] = flag
        m["ropeK"] = np.ascontiguousarray(_rope_tab(posK.reshape(-1), 16).reshape(NT1, 128, 32).transpose(1, 0, 2))
        m["ropeI"] = np.ascontiguousarray(_rope_tab(posK.reshape(-1), 8).reshape(NT1, 128, 16).transpose(1, 0, 2))
        ck = np.asarray(inp["cache_k"], f)[0, c]
        m["ckT"] = np.ascontiguousarray(ck.reshape(8, 128, 8, 128).transpose(0, 3, 2, 1))
        m["cv"] = np.ascontiguousarray(np.asarray(inp["cache_v"], f)[0, c].reshape(1024, 1024))
        m["cikT"] = np.ascontiguousarray(np.asarray(inp["cache_idx_k"], f)[0, c].T)
        m["sgla"] = np.ascontiguousarray(np.asarray(inp["state_gla"], f)[0, c].transpose(1, 0, 2).reshape(128, 1024))
        maps.append(m)
    return maps


def kernel(**inp):
    maps = _host_inputs(inp)
    nc = build_program()
    res = run_bass_kernel_spmd(nc, maps, core_ids=list(range(NCORE)))
    R = res.results
    f = np.float32
    kp = np.zeros((128, 128, 1024), f); vp = np.zeros((128, 128, 1024), f); ikp = np.zeros((128, 128, 64), f)
    for c in range(NCORE):
        for j in range(16):
            kp[8 * j + c] = R[c]["o_k"][j * 128:(j + 1) * 128]
            vp[8 * j + c] = R[c]["o_v"][j * 128:(j + 1) * 128]
            ikp[8 * j + c] = R[c]["o_ik"][j * 128:(j + 1) * 128]
    k_prompt = kp.reshape(1, 1, 16384, 8, 128)
    v_prompt = vp.reshape(1, 1, 16384, 8, 128)
    idx_k_prompt = ikp.reshape(1, 1, 16384, 64)
    gla_p = np.ascontiguousarray(R[0]["o_glap"].reshape(128, 4, 256).transpose(1, 0, 2)).reshape(1, 1, 4, 128, 256)
    k_sample = np.stack([R[c]["o_ks"] for c in range(NCORE)]).reshape(1, 8, 64, 8, 128)
    v_sample = np.stack([R[c]["o_vs"] for c in range(NCORE)]).reshape(1, 8, 64, 8, 128)
    idx_k_sample = np.stack([R[c]["o_iks"] for c in range(NCORE)]).reshape(1, 8, 64, 64)
    gla_s = np.stack([R[c]["o_glas"].reshape(128, 4, 256).transpose(1, 0, 2) for c in range(NCORE)]).reshape(1, 8, 4, 128, 256)
    yp = np.zeros((128, 128, D), f)
    for c in range(NCORE):
        for j in range(16):
            yp[8 * j + c] = R[c]["o_y"][j * 128:(j + 1) * 128]
    y_prompt = yp.reshape(1, 16384, D)
    y_sample = np.stack([R[c]["o_ys"] for c in range(NCORE)])
    ffn_conv_prompt = np.ascontiguousarray(R[7]["o_convp"].transpose(2, 1, 0)).reshape(1, 1, 2, 2 * DFF)
    ffn_conv_sample = np.stack([R[c]["o_convs"].transpose(2, 1, 0).reshape(2, 2 * DFF) for c in range(NCORE)]).reshape(1, 8, 2, 2 * DFF)
    return (y_prompt, y_sample, k_prompt, v_prompt, idx_k_prompt, np.ascontiguousarray(gla_p), ffn_conv_prompt,
            k_sample, v_sample, idx_k_sample, np.ascontiguousarray(gla_s), ffn_conv_sample)
```

```python
import numpy as np
import concourse.bass as bass
import concourse.mybir as mybir
from concourse.bass_utils import run_bass_kernel_spmd

F32 = mybir.dt.float32
BF16 = mybir.dt.bfloat16
AF = mybir.ActivationFunctionType
ALU = mybir.AluOpType
AX = mybir.AxisListType

D = 2048
NCORE = 8
NREL = 135
NT1 = NREL + 1
NKT = 140
KT_S0 = 128
EPS = 1e-6
DFF = 5632
W1C = 4176
C_AK, C_AV, C_GK, C_GV, C_GQ, C_IK, C_GLR = 0, 1024, 2048, 2560, 3584, 4096, 4160
BIG = 1.0e30


class Sync:
    def __init__(self, nc, ndma=(16, 4, 10)):
        self.nc = nc
        self.eng = {"v": nc.vector, "a": nc.scalar, "p": nc.gpsimd, "t": nc.tensor, "s": nc.sync}
        self.csem = {k: nc.alloc_semaphore("c_" + k) for k in ("v", "a", "p", "t")}
        self.ccount = {k: 0 for k in self.csem}
        self.dpool = {}
        for q, n in zip(("s", "a", "p"), ndma):
            self.dpool[q] = [[nc.alloc_semaphore(f"d_{q}{i}"), 0] for i in range(n)]
        self.dnext = {q: 0 for q in self.dpool}
        self.waited = {}
        self.lastw = {}
        self.readers = {}
        self.out_tokens = []
        self.nwait = 0

    def _wait(self, e, tok):
        sem, val, src = tok[:3]
        key = (e, id(sem))
        if self.waited.get(key, 0) >= val:
            return
        self.eng[e].wait_ge(sem, val)
        self.nwait += 1
        self.waited[key] = val

    def _deps(self, e, reads, writes):
        for k in reads:
            w = self.lastw.get(k)
            if w is not None:
                self._wait(e, w)
        for k in writes:
            for r in self.readers.get(k, ()):
                if r[2] != e or r[3]:
                    self._wait(e, r[:3])
            w = self.lastw.get(k)
            if w is not None and (w[2] != e or w[3]):
                self._wait(e, w[:3])

    def _commit(self, tok, reads, writes):
        for k in reads:
            self.readers.setdefault(k, []).append(tok)
        for k in writes:
            self.lastw[k] = tok
            self.readers[k] = []

    def op(self, e, fn, r=(), w=()):
        self._deps(e, r, w)
        inst = fn(self.eng[e])
        self.ccount[e] += 1
        inst.then_inc(self.csem[e], 1)
        tok = (self.csem[e], self.ccount[e], e, False)
        self._commit(tok, r, w)
        return tok

    def dma(self, q, out, in_, r=(), w=(), is_output=False, **kw):
        self._deps(q, r, w)
        pool = self.dpool[q]
        i = self.dnext[q]
        self.dnext[q] = (i + 1) % len(pool)
        sem, cnt = pool[i]
        if cnt:
            self._wait(q, (sem, 16 * cnt, q))
        inst = self.eng[q].dma_start(out=out, in_=in_, **kw)
        inst.then_inc(sem, 16)
        pool[i][1] = cnt + 1
        tok = (sem, 16 * (cnt + 1), q, True)
        self._commit(tok, r, w)
        if is_output:
            self.out_tokens.append(tok)
        return tok

    def barrier(self):
        toks = [(self.csem[k], self.ccount[k], k) for k in self.csem if self.ccount[k]]
        for q, pool in self.dpool.items():
            for sem, cnt in pool:
                if cnt:
                    toks.append((sem, 16 * cnt, q))
        for e in ("v", "a", "p", "t", "s"):
            for t in toks:
                if t[2] != e or t[0] not in self.csem.values():
                    self._wait(e, t)
        self.lastw.clear()
        self.readers.clear()

    def finish(self):
        for t in self.out_tokens:
            self._wait("s", t[:3])
        for q, pool in self.dpool.items():
            for sem, cnt in pool:
                if cnt:
                    self._wait("s", (sem, 16 * cnt, q))


class PsumPool:
    def __init__(self, nc, n=8):
        self.banks = [nc.alloc_psum_tensor(f"psb{i}", [128, 512], F32) for i in range(n)]
        self.i = 0
        self.n = n

    def get(self):
        b = self.banks[self.i]
        k = ("ps", self.i)
        self.i = (self.i + 1) % self.n
        return b, k


def build_program(phases=("p0", "p1", "pq")):
    nc = bass.Bass("TRN2", target_bir_lowering=False)
    dt = nc.dram_tensor

    def din(name, shape, dtype=F32):
        return dt(name, list(shape), dtype, kind="ExternalInput").ap()

    def dout(name, shape, dtype=F32):
        return dt(name, list(shape), dtype, kind="ExternalOutput").ap()

    xp = din("xp", [NREL * 128, D])
    xs = din("xs", [64, D])
    cT = din("cT", [128, 16, 2])
    badaT = din("badaT", [128, 96])
    badaR = din("badaR", [2, 4096])
    gmixT = din("gmixT", [128, 16])
    gffnT = din("gffnT", [128, 16])
    wada = din("wada", [24, 128, 16, 512])
    w1 = din("w1", [128, 16, W1C])
    gk_b = din("gk_b", [128, 1024])
    gq_b = din("gq_b", [128, 1024])
    ggla_b = din("ggla_b", [128, 1024])
    wg2 = din("wg2", [17, 512])
    flagt = din("flagt", [128, 2 * NT1])
    ropeK = din("ropeK", [128, NT1, 32])
    ropeI = din("ropeI", [128, NT1, 16])
    cmat = din("cmat", [128, 6, 128])
    ckT = din("ckT", [8, 128, 8, 128])
    cv = din("cv", [1024, 1024])
    cikT = din("cikT", [64, 1024])
    sgla = din("sgla", [128, 1024])

    o_k = dout("o_k", [16 * 128, 1024])
    o_v = dout("o_v", [16 * 128, 1024])
    o_ik = dout("o_ik", [16 * 128, 64])
    o_ks = dout("o_ks", [64, 1024])
    o_vs = dout("o_vs", [64, 1024])
    o_iks = dout("o_iks", [64, 64])
    o_glap = dout("o_glap", [128, 1024])
    o_glas = dout("o_glas", [128, 1024])
    o_y = dout("o_y", [16 * 128, D])
    import os as _os
    DBG = _os.environ.get("DBG", "0") == "1"
    if DBG:
        dbg_mix = dout("dbg_mix", [128, D]); dbg_xm = dout("dbg_xm", [128, D]); dbg_q = dout("dbg_q", [128, 1024])
    o_ys = dout("o_ys", [64, D])
    o_convp = dout("o_convp", [128, 88, 2])
    o_convs = dout("o_convs", [128, 88, 2])
    w2_d = din("w2", [12, 128, 16, 256])
    wiw_d = din("wiw", [128, 16, 16])
    wout_d = din("wout", [8, 128, 16, 256])
    wup_d = din("wup", [44, 128, 16, 256])
    wdn_d = din("wdn", [4, 44, 128, 512])
    convw_d = din("convw", [128, 88, 4])
    sconv_d = din("sconv", [128, 88, 2])
    hflag_d = din("hflag", [128, 32])
    ropeKh_d = din("ropeKh", [128, 32])
    ropeIh_d = din("ropeIh", [128, 16])
    biasO_d = din("biasO", [4, 128, 512])
    biasS_d = din("biasS", [128, 128])
    biasH_d = din("biasH", [128, 16384])

    KTs = dt("KTs", [NKT, 128, 8, 128], BF16).ap()
    Vs = dt("Vs", [NKT * 128, 8 * 130], BF16).ap()
    IKTs = dt("IKTs", [64, NKT * 128], BF16).ap()
    OGs = dt("OGs", [17 * 128, 1024], F32).ap()
    GTd = dt("GTd", [2, 4096], F32).ap()
    w2b = dt("w2b", [12, 128, 16, 256], BF16).ap()
    wiwb = dt("wiwb", [128, 16, 16], BF16).ap()
    woutb = dt("woutb", [8, 128, 16, 256], BF16).ap()
    wupb = dt("wupb", [44, 128, 16, 256], BF16).ap()
    wdnb = dt("wdnb", [4, 44, 128, 512], BF16).ap()

    sy = Sync(nc)
    pp = PsumPool(nc)
    V, A, P, T = (lambda fn, r=(), w=(): sy.op("v", fn, r, w)), (lambda fn, r=(), w=(): sy.op("a", fn, r, w)), \
        (lambda fn, r=(), w=(): sy.op("p", fn, r, w)), (lambda fn, r=(), w=(): sy.op("t", fn, r, w))

    import contextlib
    stack_holder = [None]

    def sb(name, shape, dtype=F32):
        sb.n = getattr(sb, "n", 0) + 1
        name = f"sb{sb.n}_{name}"
        if stack_holder[0] is None:
            return nc.alloc_sbuf_tensor(name, list(shape), dtype)
        return stack_holder[0].enter_context(nc.sbuf_tensor(name, list(shape), dtype))

    cm = sb("cm", [128, 6, 128])
    cmb = sb("cmb", [128, 128], BF16)
    modT = sb("modT", [128, 96, 2])
    g1 = sb("g1", [128, 2, 16]); shm = sb("shm", [128, 2, 16])
    g2 = sb("g2", [128, 2, 16]); shf = sb("shf", [128, 2, 16])
    tmpc = sb("tmpc", [128, 16, 2])
    negh = sb("negh", [128, 1])

    sy.dma("s", cm[:], cmat, w=["cm"])
    V(lambda e: e.tensor_copy(out=cmb[:], in_=cm[:, 0, :]), r=["cm"], w=["cmb"])
    V(lambda e: e.memset(negh[:], -0.5), w=["negh"])
    IDF = cm[:, 0, :]

    def rsqrt_small(dst, src, n, mul, key_src, key_dst):
        V(lambda e: e.tensor_scalar(out=dst, in0=src, scalar1=mul, scalar2=EPS, op0=ALU.mult, op1=ALU.add),
          r=[key_src], w=[key_dst])
        P(lambda e: e.tensor_tensor(out=dst, in0=dst, in1=negh[:, 0:1].to_broadcast([128, n]), op=ALU.pow),
          r=[key_dst, "negh"], w=[key_dst])

    if "p0" in phases:
        stack_holder[0] = contextlib.ExitStack()
        gtrow = sb("gtrow", [2, 4096])
        cTt = sb("cTt", [128, 16, 2]); scb = sb("scb", [128, 16, 2], BF16)
        bT = sb("bT", [128, 96]); gmT = sb("gmT", [128, 16]); gfT = sb("gfT", [128, 16])
        brow = sb("brow", [2, 4096])
        wab = [sb(f"wab{i}", [128, 16, 512], BF16) for i in range(2)]
        sy.dma("s", cTt[:], cT, w=["cTt"])
        sy.dma("s", bT[:], badaT, w=["bT"])
        sy.dma("s", gmT[:], gmixT, w=["gmT"])
        sy.dma("s", gfT[:], gffnT, w=["gfT"])
        sy.dma("s", brow[:], badaR, w=["brow"])
        A(lambda e: e.activation(out=scb[:], in_=cTt[:], func=AF.Silu), r=["cTt"], w=["scb"])
        import os
        P0STOP = int(os.environ.get("P0STOP", "99"))
        for blk in range(24 if P0STOP > 2 else (1 if P0STOP > 0 else 0)):
            wb = wab[blk % 2]; wk = ("wab", blk % 2)
            sy.dma("p", wb[:], wada[blk], w=[wk])
            ps, pk = pp.get()
            for cc in range(4):
                for k in range(16):
                    T(lambda e, cc=cc, k=k: e.matmul(ps[:, cc * 2:cc * 2 + 2], lhsT=wb[:, k, cc * 128:(cc + 1) * 128],
                                                     rhs=scb[:, k, :], start=(k == 0), stop=(k == 15)),
                      r=[wk, "scb"], w=[pk])
            if P0STOP == 1:
                break
            V(lambda e, blk=blk: e.tensor_tensor(
                out=modT[:, blk * 4:(blk + 1) * 4, :],
                in0=ps[:, 0:8].rearrange("p (c t) -> p c t", t=2),
                in1=bT[:, blk * 4:(blk + 1) * 4].rearrange("p (c o) -> p c o", o=1).to_broadcast([128, 4, 2]),
                op=ALU.add), r=[pk, "bT"], w=["modT"])
            if blk in (8, 9, 10, 11, 20, 21, 22, 23):
                ro = (blk - 8) * 512 if blk < 12 else 2048 + (blk - 20) * 512
                ps2, pk2 = pp.get()
                for k in range(16):
                    T(lambda e, k=k: e.matmul(ps2[0:2, :], lhsT=scb[:, k, :], rhs=wb[:, k, :],
                                              start=(k == 0), stop=(k == 15)), r=[wk, "scb"], w=[pk2])
                V(lambda e, ro=ro: e.tensor_tensor(out=gtrow[:, ro:ro + 512], in0=ps2[0:2, :], in1=brow[:, ro:ro + 512],
                                                   op=ALU.add), r=[pk2, "brow"], w=["gtrow"])
        def modview(j):
            return modT[:, j * 16:(j + 1) * 16, :].rearrange("p k t -> p t k")
        for (gd, gsrc, jsc, sd, jsh, nm) in ((g1, gmT, 1, shm, 0, "m"), (g2, gfT, 4, shf, 3, "f")):
            for t in range(2):
                V(lambda e, gd=gd, jsc=jsc, t=t: e.tensor_scalar(out=gd[:, t, :], in0=modT[:, jsc * 16:(jsc + 1) * 16, t],
                                                                 scalar1=1.0, scalar2=None, op0=ALU.add),
                  r=["modT"], w=["g" + nm])
                V(lambda e, gd=gd, gsrc=gsrc, t=t: e.tensor_tensor(out=gd[:, t, :], in0=gd[:, t, :], in1=gsrc[:], op=ALU.mult),
                  r=["g" + nm, "gmT", "gfT"], w=["g" + nm])
                V(lambda e, sd=sd, jsh=jsh, t=t: e.tensor_copy(out=sd[:, t, :], in_=modT[:, jsh * 16:(jsh + 1) * 16, t]),
                  r=["modT"], w=["sh" + nm])
        sy.dma("s", GTd, gtrow[:], r=["gtrow"], w=["GTd"])
        sy.barrier()
        stack_holder[0].close()
        stack_holder[0] = None

    if "p1" in phases:
        stack_holder[0] = contextlib.ExitStack()
        flg = sb("flg", [128, 2 * NT1])
        gkb = sb("gkb", [128, 1024]); gglab = sb("gglab", [128, 1024])
        sy.dma("s", flg[:], flagt, w=["flg"])
        sy.dma("s", gkb[:], gk_b, w=["gkb"])
        sy.dma("s", gglab[:], ggla_b, w=["gglab"])
        w1b = sb("w1b", [128, 16, W1C], BF16)
        for k in range(16):
            sy.dma("p", w1b[:, k, :], w1[:, k, :], w=["w1b"])
        wg2f = sb("wg2f", [17, 512]); wg2b = sb("wg2b", [17, 512], BF16)
        sy.dma("s", wg2f[:], wg2, w=["wg2f"])
        V(lambda e: e.tensor_copy(out=wg2b[:], in_=wg2f[:]), r=["wg2f"], w=["wg2b"])
        rKs = [sb(f"rK{i}", [128, 32]) for i in range(2)]; rIs = [sb(f"rI{i}", [128, 16]) for i in range(2)]
        xt = [sb(f"xt{i}", [128, D]) for i in range(1)]
        hT = sb("hT", [128, 16, 128], BF16)
        kf = sb("kf", [128, 1024]); sq = sb("sq", [128, 1024]); vf = sq
        kb = sb("kb", [128, 1024], BF16)
        ktb = sb("ktb", [128, 8, 128], BF16)
        vb = sb("vb", [128, 8, 130], BF16)
        ikf = sb("ikf", [128, 64]); ikb = sb("ikb", [128, 64], BF16); iktb = sb("iktb", [64, 128], BF16)
        gkfs = [sb(f"gkf{i}", [128, 512]) for i in range(2)]; gqfs = [sb(f"gqf{i}", [128, 512]) for i in range(2)]
        gvbs = [sb(f"gvb{i}", [128, 1024], BF16) for i in range(2)]
        glras = [sb(f"glra{i}", [17, 128], BF16) for i in range(2)]
        lg = sb("lg", [128, 512]); ee = sb("ee", [128, 512])
        kpb = sb("kpb", [128, 512], BF16)
        qtl = sb("qtl", [128, 512], BF16); ktl = sb("ktl", [128, 512], BF16)
        qkT = sb("qkT", [128, 8, 128], BF16)
        ATb = sb("ATb", [128, 4, 128], BF16)
        S = sb("S", [128, 1024]); Sb = kb
        dec = sb("dec", [128, 4])
        ogf = sb("ogf", [128, 1024])
        st8 = sb("st8", [128, 16]); rt8 = sb("rt8", [128, 16])
        rtmp = sb("rtmp", [128, 4, 8, 16])
        V(lambda e: e.memset(vb[:], 1.0), w=["vb"])
        for i_ in range(2):
            V(lambda e, i_=i_: e.memset(glras[i_][:], 1.0), w=[("glra", i_)])
        V(lambda e: e.memset(S[:], 0.0), w=["S"])

        def ln_transpose(xtile, xkey, which, hdst, hkey, gsc, gsh, gkeys):
            V(lambda e: e.scalar_tensor_tensor(out=sq[:].bitcast(BF16), in0=xtile, scalar=1.0, in1=xtile, op0=ALU.mult,
                                               op1=ALU.mult, accum_out=st8[:, 0:1]), r=[xkey], w=["sq", "st8"])
            rsqrt_small(rt8[:, 0:1], st8[:, 0:1], 1, 1.0 / D, "st8", "rt8")
            A(lambda e: e.activation(out=xtile, in_=xtile, func=AF.Copy, scale=rt8[:, 0:1]),
              r=[xkey, "rt8"], w=[xkey])
            for half in range(4):
                ps, pk = pp.get()
                for kk in range(4):
                    k = half * 4 + kk
                    T(lambda e, k=k, kk=kk: e.transpose(ps[:, kk * 128:(kk + 1) * 128], xtile[:, k * 128:(k + 1) * 128], IDF),
                      r=[xkey, "cm"], w=[pk])
                for kk in range(4):
                    k = half * 4 + kk
                    if isinstance(which, int):
                        A(lambda e, k=k, kk=kk: e.activation(out=hdst[:, k, :], in_=ps[:, kk * 128:(kk + 1) * 128],
                                                             func=AF.Identity, scale=gsc[:, which, k:k + 1],
                                                             bias=gsh[:, which, k:k + 1]),
                          r=[pk] + gkeys, w=[hkey])
                    else:
                        for (c0, c1, wh) in ((0, 64, 1), (64, 128, 0)):
                            A(lambda e, k=k, kk=kk, c0=c0, c1=c1, wh=wh: e.activation(
                                out=hdst[:, k, c0:c1], in_=ps[:, kk * 128 + c0:kk * 128 + c1], func=AF.Identity,
                                scale=gsc[:, wh, k:k + 1], bias=gsh[:, wh, k:k + 1]), r=[pk] + gkeys, w=[hkey])

        def proj(hsrc, hkey, wtile, wkey, c0, ncols, evac):
            ps, pk = pp.get()
            for k in range(16):
                T(lambda e, k=k: e.matmul(ps[:, 0:ncols], lhsT=hsrc[:, k, :], rhs=wtile[:, k, c0:c0 + ncols],
                                          start=(k == 0), stop=(k == 15)), r=[hkey, wkey], w=[pk])
            evac(ps, pk)

        def headnorm(src, skey, nh, hd, gb, gbkey, dst_keys):
            V(lambda e: e.tensor_tensor(out=sq[:, 0:nh * hd], in0=src, in1=src, op=ALU.mult), r=[skey], w=["sq"])
            V(lambda e: e.tensor_reduce(out=st8[:, 0:nh], in_=sq[:, 0:nh * hd].rearrange("p (h d) -> p h d", h=nh),
                                        axis=AX.X, op=ALU.add), r=["sq"], w=["st8"])
            rsqrt_small(rt8[:, 0:nh], st8[:, 0:nh], nh, 1.0 / hd, "st8", "rt8")
            V(lambda e: e.tensor_tensor(out=src.rearrange("p (h d) -> p h d", h=nh),
                                        in0=src.rearrange("p (h d) -> p h d", h=nh),
                                        in1=rt8[:, 0:nh].rearrange("p (h o) -> p h o", o=1).to_broadcast([128, nh, hd]),
                                        op=ALU.mult), r=[skey, "rt8"], w=[skey])
            V(lambda e: e.tensor_tensor(out=src, in0=src, in1=gb, op=ALU.mult), r=[skey, gbkey], w=[skey])

        def rope(src, skey, nh, hd, half, tab, tkey):
            v3 = src.rearrange("p (h d) -> p h d", h=nh)
            x1 = v3[:, :, 0:half]; x2 = v3[:, :, half:2 * half]
            cb = tab[:, 0:half].rearrange("p (o d) -> p o d", o=1).to_broadcast([128, nh, half])
            sn = tab[:, half:2 * half].rearrange("p (o d) -> p o d", o=1).to_broadcast([128, nh, half])
            t = [rtmp[:, i, 0:nh, 0:half] for i in range(4)]
            V(lambda e: e.tensor_tensor(out=t[0], in0=x1, in1=cb, op=ALU.mult), r=[skey, tkey], w=["rtmp"])
            V(lambda e: e.tensor_tensor(out=t[1], in0=x2, in1=sn, op=ALU.mult), r=[skey, tkey], w=["rtmp"])
            V(lambda e: e.tensor_tensor(out=t[2], in0=x2, in1=cb, op=ALU.mult), r=[skey, tkey], w=["rtmp"])
            V(lambda e: e.tensor_tensor(out=t[3], in0=x1, in1=sn, op=ALU.mult), r=[skey, tkey], w=["rtmp"])
            V(lambda e: e.tensor_tensor(out=x1, in0=t[0], in1=t[1], op=ALU.subtract), r=["rtmp"], w=[skey])
            V(lambda e: e.tensor_tensor(out=x2, in0=t[2], in1=t[3], op=ALU.add), r=["rtmp"], w=[skey])

        def k_store(kt):
            A(lambda e: e.copy(out=kb[:], in_=kf[:]), r=["kf"], w=["kb"])
            ps, pk = pp.get()
            pv = ps[:].bitcast(BF16) if hasattr(ps[:], "bitcast") else None
            for h in range(8):
                T(lambda e, h=h: e.transpose(pv[:, h * 128:(h + 1) * 128], kb[:, h * 128:(h + 1) * 128], cmb[:]),
                  r=["kb", "cmb"], w=[pk])
            A(lambda e: e.copy(out=ktb[:].rearrange("p h t -> p (h t)"), in_=pv[:, 0:1024]), r=[pk], w=["ktb"])
            sy.dma("s", KTs[kt], ktb[:], r=["ktb"], w=[("KT", kt)])

        def ik_store(kt):
            P(lambda e: e.tensor_copy(out=ikb[:], in_=ikf[:]), r=["ikf"], w=["ikb"])
            ps, pk = pp.get()
            pv = ps[:].bitcast(BF16)
            T(lambda e: e.transpose(pv[0:64, 0:128], ikb[:], cmb[:]), r=["ikb", "cmb"], w=[pk])
            A(lambda e: e.copy(out=iktb[:], in_=pv[0:64, 0:128]), r=[pk], w=["iktb"])
            sy.dma("s", IKTs[:, kt * 128:(kt + 1) * 128], iktb[:], r=["iktb"], w=[("IKT", kt)])

        def v_store(kt):
            sy.dma("s", Vs[kt * 128:(kt + 1) * 128, :], vb[:].rearrange("p h d -> p (h d)"), r=["vb"], w=[("V", kt)])

        import os
        P1STOP = int(os.environ.get("P1STOP", "99"))

        def kv_tile(xsrc, nrows, which, tcol, kt, own_j, og_dst, outs):
            slot = kv_tile.n % 2
            kv_tile.n += 1
            xtile = xt[0]; xkey = ("xt", 0)
            rKt = rKs[slot]; rIt = rIs[slot]
            gkf = gkfs[slot]; gqf = gqfs[slot]; gvb = gvbs[slot]; glra = glras[slot]
            gkk = ("gkf", slot); gqk = ("gqf", slot); gvk = ("gvb", slot); glk = ("glra", slot)
            sy.dma("p", rKt[:], ropeK[:, tcol, :], w=[("rK", slot)])
            sy.dma("p", rIt[:], ropeI[:, tcol, :], w=[("rI", slot)])
            if nrows < 128:
                V(lambda e: e.memset(xtile[:], 0.0), w=[xkey])
            sy.dma("p", xtile[0:nrows, :], xsrc, w=[xkey])
            if P1STOP <= 1:
                return
            ln_transpose(xtile[:], xkey, which, hT, "hT", g1, shm, ["gm"])
            if P1STOP <= 2:
                return
            f01 = flg[:, tcol:tcol + 1]; fn16 = flg[:, NT1 + tcol:NT1 + tcol + 1]
            for b in range(2):
                proj(hT, "hT", w1b, "w1b", C_AK + b * 512, 512,
                     lambda ps, pk, b=b: V(lambda e: e.tensor_copy(out=kf[:, b * 512:(b + 1) * 512], in_=ps[:, :]),
                                           r=[pk], w=["kf"]))
            yield
            for b in range(2):
                def ev(ps, pk, b=b):
                    if outs is not None:
                        A(lambda e: e.copy(out=vf[:, b * 512:(b + 1) * 512], in_=ps[:, :]), r=[pk], w=["sq"])
                        P(lambda e: e.tensor_copy(out=vb[:, b * 4:(b + 1) * 4, 0:128],
                                                  in_=vf[:, b * 512:(b + 1) * 512].rearrange("p (h d) -> p h d", h=4)),
                          r=["sq"], w=["vb"])
                    else:
                        A(lambda e: e.copy(out=vb[:, b * 4:(b + 1) * 4, 0:128], in_=ps[:, :].rearrange("p (h d) -> p h d", h=4)),
                          r=[pk], w=["vb"])
                proj(hT, "hT", w1b, "w1b", C_AV + b * 512, 512, ev)
            if outs is not None:
                sy.dma("s", outs["v"], vf[0:nrows, :], r=["sq"], w=["o_v"], is_output=True)
            yield
            proj(hT, "hT", w1b, "w1b", C_IK, 64,
                 lambda ps, pk: V(lambda e: e.tensor_copy(out=ikf[:], in_=ps[:, 0:64]), r=[pk], w=["ikf"]))
            proj(hT, "hT", w1b, "w1b", C_GK, 512,
                 lambda ps, pk: V(lambda e: e.tensor_copy(out=gkf[:], in_=ps[:, :]), r=[pk], w=[gkk]))
            yield
            for b in range(2):
                proj(hT, "hT", w1b, "w1b", C_GV + b * 512, 512,
                     lambda ps, pk, b=b: A(lambda e: e.activation(out=gvb[:, b * 512:(b + 1) * 512], in_=ps[:, :],
                                                                  func=AF.Copy, scale=f01), r=[pk, "flg"], w=[gvk]))
            yield
            ps, pk = pp.get()
            for k in range(16):
                T(lambda e, k=k: e.matmul(ps[0:16, 0:128], lhsT=w1b[:, k, C_GLR:C_GLR + 16], rhs=hT[:, k, :],
                                          start=(k == 0), stop=(k == 15)), r=["hT", "w1b"], w=[pk])
            V(lambda e: e.tensor_copy(out=glra[0:16, :], in_=ps[0:16, 0:128]), r=[pk], w=[glk])
            if og_dst is not None:
                proj(hT, "hT", w1b, "w1b", C_GQ, 512,
                     lambda ps, pk: V(lambda e: e.tensor_copy(out=gqf[:], in_=ps[:, :]), r=[pk], w=[gqk]))
            yield
            headnorm(kf[:], "kf", 8, 128, gkb[:], "gkb", None)
            rope(kf[:], "kf", 8, 128, 16, rKt[:], ("rK", slot))
            if outs is not None:
                sy.dma("s", outs["k"], kf[0:nrows, :], r=["kf"], w=["o_k"], is_output=True)
            yield
            k_store(kt)
            v_store(kt)
            yield
            rope(ikf[:], "ikf", 1, 64, 8, rIt[:], ("rI", slot))
            if outs is not None:
                sy.dma("s", outs["ik"], ikf[0:nrows, :], r=["ikf"], w=["o_ik"], is_output=True)
            ik_store(kt)
            yield "S2"
            ps, pk = pp.get()
            T(lambda e: e.matmul(ps[:, :], lhsT=glra[:], rhs=wg2b[:], start=True, stop=True), r=[glk, "wg2b"], w=[pk])
            A(lambda e: e.activation(out=ee[:], in_=ps[:, :], func=AF.Exp, scale=-1.0), r=[pk], w=["ee"])
            A(lambda e: e.activation(out=ee[:], in_=ee[:], func=AF.Ln, bias=1.0), r=["ee"], w=["ee"])
            V(lambda e: e.tensor_scalar(out=lg[:], in0=ee[:], scalar1=fn16, scalar2=None, op0=ALU.mult),
              r=["ee", "flg"], w=["lg"])
            if P1STOP <= 7:
                return
            yield
            need_out = og_dst is not None
            if need_out:
                ps, pk = pp.get()
                T(lambda e: e.matmul(ps[:, :], lhsT=cm[:, 1, :], rhs=lg[:], start=True, stop=True), r=["cm", "lg"], w=[pk])
                A(lambda e: e.activation(out=ee[:], in_=ps[:, :], func=AF.Exp), r=[pk], w=["ee"])
                V(lambda e: e.scalar_tensor_tensor(out=qtl[:], in0=gqf[:], scalar=128.0 ** -0.5, in1=ee[:], op0=ALU.mult,
                                                   op1=ALU.mult), r=[gqk, "ee"], w=["qtl"])
                A(lambda e: e.activation(out=ee[:], in_=ps[:, :], func=AF.Exp, scale=-1.0), r=[pk, "qtl"], w=["ee"])
                V(lambda e: e.tensor_tensor(out=ktl[:], in0=gkf[:], in1=ee[:], op=ALU.mult), r=[gkk, "ee"], w=["ktl"])
                yield
                ps, pk = pp.get()
                pv = ps[:].bitcast(BF16)
                for h in range(4):
                    T(lambda e, h=h: e.transpose(pv[:, h * 128:(h + 1) * 128], qtl[:, h * 128:(h + 1) * 128], cmb[:]),
                      r=["qtl", "cmb"], w=[pk])
                    T(lambda e, h=h: e.transpose(pv[:, (4 + h) * 128:(5 + h) * 128], ktl[:, h * 128:(h + 1) * 128], cmb[:]),
                      r=["ktl", "cmb"], w=[pk])
                A(lambda e: e.copy(out=qkT[:].rearrange("p h t -> p (h t)"), in_=pv[:, 0:1024]), r=[pk], w=["qkT"])
                ps, pk = pp.get()
                for h in range(4):
                    T(lambda e, h=h: e.matmul(ps[:, h * 128:(h + 1) * 128], lhsT=qkT[:, 4 + h, :], rhs=qkT[:, h, :],
                                              start=True, stop=True), r=["qkT"], w=[pk])
                V(lambda e: e.tensor_tensor(out=ATb[:], in0=ps[:, :].rearrange("p (h t) -> p h t", h=4),
                                            in1=cm[:, 1, :].rearrange("p (o t) -> p o t", o=1).to_broadcast([128, 4, 128]),
                                            op=ALU.mult), r=[pk, "cm"], w=["ATb"])
                yield
                P(lambda e: e.tensor_copy(out=Sb[:], in_=S[:]), r=["S"], w=["kb"])
                pso = [pp.get(), pp.get()]
                for h in range(4):
                    po, pok = pso[h // 2]
                    oc = (h % 2) * 256
                    T(lambda e, h=h, po=po, oc=oc: e.matmul(po[:, oc:oc + 256], lhsT=qkT[:, h, :], rhs=Sb[:, h * 256:(h + 1) * 256],
                                                            start=True, stop=False), r=["qkT", "kb"], w=[pok])
                    T(lambda e, h=h, po=po, oc=oc: e.matmul(po[:, oc:oc + 256], lhsT=ATb[:, h, :], rhs=gvb[:, h * 256:(h + 1) * 256],
                                                            start=False, stop=True), r=["ATb", gvk], w=[pok])
                for i2 in range(2):
                    po, pok = pso[i2]
                    V(lambda e, po=po, i2=i2: e.tensor_copy(out=ogf[:, i2 * 512:(i2 + 1) * 512], in_=po[:, :]), r=[pok], w=["ogf"])
                headnorm(ogf[:], "ogf", 4, 256, gglab[:], "gglab", None)
                lo, hi, drow = og_dst
                sy.dma("s", OGs[drow:drow + (hi - lo), :], ogf[lo:hi, :], r=["ogf"], w=[("OG", drow)])
            if P1STOP <= 8:
                return
            yield
            ps, pk = pp.get()
            T(lambda e: e.matmul(ps[:, :], lhsT=cm[:, 2, :], rhs=lg[:], start=True, stop=True), r=["cm", "lg"], w=[pk])
            A(lambda e: e.activation(out=ee[:], in_=ps[:, :], func=AF.Exp), r=[pk, "ktl"], w=["ee"])
            V(lambda e: e.tensor_tensor(out=kpb[:], in0=gkf[:], in1=ee[:], op=ALU.mult), r=[gkk, "ee"], w=["kpb"])
            ps, pk = pp.get()
            for h in range(4):
                T(lambda e, h=h: e.matmul(ps[:, h:h + 1], lhsT=lg[:, h * 128:(h + 1) * 128], rhs=cm[:, 3, 0:1],
                                          start=True, stop=True), r=["lg", "cm"], w=[pk])
            A(lambda e: e.activation(out=dec[:], in_=ps[:, 0:4], func=AF.Exp), r=[pk], w=["dec"])
            yield
            psu = [pp.get(), pp.get()]
            for h in range(4):
                pu, puk = psu[h // 2]
                oc = (h % 2) * 256
                T(lambda e, h=h, pu=pu, oc=oc: e.matmul(pu[:, oc:oc + 256], lhsT=kpb[:, h * 128:(h + 1) * 128],
                                                        rhs=gvb[:, h * 256:(h + 1) * 256], start=True, stop=True),
                  r=["kpb", gvk], w=[puk])
            for h in range(4):
                pu, puk = psu[h // 2]
                oc = (h % 2) * 256
                V(lambda e, h=h, pu=pu, oc=oc: e.scalar_tensor_tensor(
                    out=S[:, h * 256:(h + 1) * 256], in0=S[:, h * 256:(h + 1) * 256], scalar=dec[:, h:h + 1],
                    in1=pu[:, oc:oc + 256], op0=ALU.mult, op1=ALU.add), r=["S", "dec", puk], w=["S"])
        kv_tile.n = 0

        ntiles = build_program.ntiles_p1 if hasattr(build_program, "ntiles_p1") else NREL
        conv_jobs = [(w2b[2 * i:2 * i + 2], w2_d[2 * i:2 * i + 2]) for i in range(6)] + [(wiwb, wiw_d)]
        conv_jobs += [(woutb[2 * i:2 * i + 2], wout_d[2 * i:2 * i + 2]) for i in range(4)]
        conv_jobs += [(wupb[2 * i:2 * i + 2], wup_d[2 * i:2 * i + 2]) for i in range(22)]
        conv_jobs += [(wdnb[nb, 11 * i:11 * i + 11], wdn_d[nb, 11 * i:11 * i + 11]) for nb in range(4) for i in range(4)]
        if "pq" not in phases:
            conv_jobs = []
        prev_g = None
        for r in range(ntiles):
            if conv_jobs:
                dst_, src_ = conv_jobs.pop(0)
                sy.dma("p", dst_, src_, w=["wconv"])
            own_j = r // 8 if r % 8 == 7 else None
            halo_j = r // 8 if r % 8 == 6 else None
            outs = None
            og = None
            if own_j is not None:
                outs = {"k": o_k[own_j * 128:(own_j + 1) * 128, :], "v": o_v[own_j * 128:(own_j + 1) * 128, :],
                        "ik": o_ik[own_j * 128:(own_j + 1) * 128, :]}
                og = (0, 128, (1 + own_j) * 128)
            if halo_j is not None:
                og = (126, 128, 64 + 2 * halo_j)
            if os.environ.get("P1OUTS", "1") == "0":
                outs = None
            if os.environ.get("P1OG", "1") == "0" and own_j is not None:
                og = None
            g_ = kv_tile(xp[r * 128:(r + 1) * 128, :], 128, 0, r, r if r < 128 else NKT - 1, own_j, og, outs)
            dn = False; do = prev_g is None
            while not (dn and do):
                if not dn:
                    dn = next(g_, "END") in ("S2", "END")
                if not do:
                    do = next(prev_g, "END") == "END"
            prev_g = g_
        if prev_g is not None:
            for _ in prev_g:
                pass
        while conv_jobs:
            dst_, src_ = conv_jobs.pop(0)
            sy.dma("p", dst_, src_, w=["wconv"])
        sy.dma("s", o_glap, S[:], r=["S"], w=["o_glap"], is_output=True)
        P1POST = int(os.environ.get("P1POST", "1"))
        for i in range(8 if P1POST else 0):
            ktf = xt[0]; xkey = ("xt", 0)
            sy.dma("s", ktf[:, 0:1024].rearrange("p (h t) -> p h t", h=8), ckT[i], w=[xkey])
            P(lambda e, ktf=ktf: e.tensor_copy(out=ktb[:].rearrange("p h t -> p (h t)"), in_=ktf[:, 0:1024]), r=[xkey], w=["ktb"])
            sy.dma("s", KTs[KT_S0 + i], ktb[:], r=["ktb"], w=[("KT", KT_S0 + i)])
            sy.dma("s", ktf[:, 1024:2048], cv[i * 128:(i + 1) * 128, :], w=[xkey])
            A(lambda e, ktf=ktf: e.copy(out=vb[:, :, 0:128], in_=ktf[:, 1024:2048].rearrange("p (h d) -> p h d", h=8)),
              r=[xkey], w=["vb"])
            v_store(KT_S0 + i)
        ikc = xt[0][0:64, 0:1024]; ikcb = kb[0:64, :]
        if P1POST:
            sy.dma("s", ikc, cikT, w=[("xt", 0)])
            V(lambda e: e.tensor_copy(out=ikcb, in_=ikc), r=[("xt", 0)], w=["kb"])
            sy.dma("s", IKTs[:, KT_S0 * 128:(KT_S0 + 8) * 128], ikcb, r=["kb"], w=[("IKT", "c")])
            sy.dma("s", S[:], sgla, r=["o_glap"], w=["S"])
            outs = {"k": o_ks, "v": o_vs, "ik": o_iks}
            for _ in kv_tile(xs, 64, 1, NT1 - 1, KT_S0 + 8, None, (0, 64, 0), outs):
                pass
        sy.dma("s", o_glas, S[:], r=["S"], w=["o_glas"], is_output=True)
        sy.barrier()
        stack_holder[0].close()
        stack_holder[0] = None

    if "pq" in phases:
        import os
        stack_holder[0] = contextlib.ExitStack()
        pp.n = 5
        pp.i = 0
        OB = [(pp.banks[5 + i], ("ps", 5 + i)) for i in range(3)]
        NITER = int(os.environ.get("NITER", "26"))
        gqb = sb("gqb", [128, 1024])
        sy.dma("s", gqb[:], gq_b, w=["gqb"])
        gtc = [sb(f"gtc{i}", [128, 512]) for i in range(2)]
        uhalo = sb("uhalo", [128, 88, 32])
        convw = sb("convw", [128, 88, 4]); sconv = sb("sconv", [128, 88, 2]); hfl = sb("hfl", [128, 32])
        oconv = sb("oconv", [128, 88, 2])
        sy.dma("s", convw[:], convw_d, w=["convw"])
        sy.dma("s", sconv[:], sconv_d, w=["sconv"])
        sy.dma("s", hfl[:], hflag_d, w=["hfl"])
        xq = sb("xq", [128, D])
        hT = sb("hT", [128, 16, 128], BF16)
        wblk = [sb(f"wblk{i}", [128, 16, 256], BF16) for i in range(2)]
        wiw = sb("wiw", [128, 16, 16], BF16)
        qf = sb("qf", [128, 1024]); sq = sb("sq", [128, 1024]); ogt = sq
        QT = sb("QT", [128, 8, 128], BF16); iqT = sb("iqT", [128, 8, 128], BF16)
        iwf = sb("iwf", [128, 16])
        mixb = sb("mixb", [128, D], BF16); mixT = hT; qb = mixb[:, 0:1024]
        score = sb("score", [128, 16384]); xsq = score[:, 0:2048]
        hidT = score[:, 2048:4864].bitcast(BF16).rearrange("p (i t) -> p i t", i=44)
        IKc = [sb(f"IKc{i}", [128, 512], BF16) for i in range(2)]
        rl = [sb(f"rl{i}", [128, 512]) for i in range(2)]
        bch = [sb(f"bch{i}", [128, 512]) for i in range(2)]
        junk = qf[:].bitcast(BF16)
        bs = sb("bs", [128, 48]); cnts = sb("cnts", [128, 16]); hmx = sb("hmx", [128, 40]); lmx = sb("lmx", [128, 40])
        m01 = [sb(f"m01{i}", [128, 512], BF16) for i in range(2)]
        mT = [sb(f"mT{i}", [128, 512], BF16) for i in range(2)]
        KTc = [sb(f"KTc{i}", [128, 4, 8, 128], BF16) for i in range(2)]
        Vc = [sb(f"Vc{i}", [128, 4, 8 * 130], BF16) for i in range(2)]
        eT = [sb(f"eT{i}", [128, 512], BF16) for i in range(2)]
        PT = [sb(f"PT{i}", [128, 512], BF16) for i in range(2)]
        oa = sb("oa", [128, 8, 129]); rden = sb("rden", [128, 8])
        uexts = [sb(f"uext{i}", [128, 2, 130]) for i in range(2)]; tas = [sb(f"ta{i}", [128, 2, 128]) for i in range(2)]
        sas = [sb(f"sa{i}", [128, 128]) for i in range(2)]
        wdp = [sb(f"wdp{i}", [128, 2, 512], BF16) for i in range(3)]
        tmpy = rl[1]
        rKq = sb("rKq", [128, 32]); rIq = sb("rIq", [128, 16])
        st8 = sb("st8q", [128, 16]); rt8 = sb("rt8q", [128, 16])
        rtmp = sb("rtmpq", [128, 4, 16, 16])

        cntq = {"w": 0, "d": 0, "g": 0}

        def ln_transpose_q(which, gsc, gsh):
            V(lambda e: e.scalar_tensor_tensor(out=xsq, in0=xq[:], scalar=1.0, in1=xq[:], op0=ALU.mult,
                                               op1=ALU.mult, accum_out=st8[:, 0:1]), r=["xq"], w=["score", "st8"])
            rsqrt_small(rt8[:, 0:1], st8[:, 0:1], 1, 1.0 / D, "st8", "rt8")
            A(lambda e: e.activation(out=xsq, in_=xq[:], func=AF.Copy, scale=rt8[:, 0:1]), r=["xq", "rt8"], w=["score"])
            for half in range(4):
                ps, pk = pp.get()
                for kk in range(4):
                    k = half * 4 + kk
                    T(lambda e, k=k, kk=kk: e.transpose(ps[:, kk * 128:(kk + 1) * 128], xsq[:, k * 128:(k + 1) * 128], IDF),
                      r=["score"], w=[pk])
                for kk in range(4):
                    k = half * 4 + kk
                    A(lambda e, k=k, kk=kk: e.activation(out=hT[:, k, :], in_=ps[:, kk * 128:(kk + 1) * 128], func=AF.Identity,
                                                         scale=gsc[:, which, k:k + 1], bias=gsh[:, which, k:k + 1]),
                      r=[pk], w=["hT"])

        def wload(src):
            i = cntq["w"] % 2
            cntq["w"] += 1
            sy.dma("s", wblk[i][:], src, w=[("wblk", i)])
            return wblk[i], ("wblk", i)

        def projq(lhs, lkey, wt, wkey, c0, ncols, evac):
            ps, pk = pp.get()
            for k in range(16):
                T(lambda e, k=k: e.matmul(ps[:, 0:ncols], lhsT=lhs[:, k, :], rhs=wt[:, k, c0:c0 + ncols],
                                          start=(k == 0), stop=(k == 15)), r=[lkey, wkey], w=[pk])
            evac(ps, pk)

        def headnorm_q(src, skey, nh, hd, gb, gbkey):
            V(lambda e: e.tensor_tensor(out=sq[:, 0:nh * hd], in0=src, in1=src, op=ALU.mult), r=[skey], w=["sq"])
            V(lambda e: e.tensor_reduce(out=st8[:, 0:nh], in_=sq[:, 0:nh * hd].rearrange("p (h d) -> p h d", h=nh),
                                        axis=AX.X, op=ALU.add), r=["sq"], w=["st8"])
            rsqrt_small(rt8[:, 0:nh], st8[:, 0:nh], nh, 1.0 / hd, "st8", "rt8")
            V(lambda e: e.tensor_tensor(out=src.rearrange("p (h d) -> p h d", h=nh),
                                        in0=src.rearrange("p (h d) -> p h d", h=nh),
                                        in1=rt8[:, 0:nh].rearrange("p (h o) -> p h o", o=1).to_broadcast([128, nh, hd]),
                                        op=ALU.mult), r=[skey, "rt8"], w=[skey])
            V(lambda e: e.tensor_tensor(out=src, in0=src, in1=gb, op=ALU.mult), r=[skey, gbkey], w=[skey])

        def rope_q(src, skey, nh, hd, half, tab, tkey):
            v3 = src.rearrange("p (h d) -> p h d", h=nh)
            x1 = v3[:, :, 0:half]; x2 = v3[:, :, half:2 * half]
            cb = tab[:, 0:half].rearrange("p (o d) -> p o d", o=1).to_broadcast([128, nh, half])
            sn = tab[:, half:2 * half].rearrange("p (o d) -> p o d", o=1).to_broadcast([128, nh, half])
            t = [rtmp[:, i, 0:nh, 0:half] for i in range(4)]
            V(lambda e: e.tensor_tensor(out=t[0], in0=x1, in1=cb, op=ALU.mult), r=[skey, tkey], w=["rtmp"])
            V(lambda e: e.tensor_tensor(out=t[1], in0=x2, in1=sn, op=ALU.mult), r=[skey, tkey], w=["rtmp"])
            V(lambda e: e.tensor_tensor(out=t[2], in0=x2, in1=cb, op=ALU.mult), r=[skey, tkey], w=["rtmp"])
            V(lambda e: e.tensor_tensor(out=t[3], in0=x1, in1=sn, op=ALU.mult), r=[skey, tkey], w=["rtmp"])
            V(lambda e: e.tensor_tensor(out=x1, in0=t[0], in1=t[1], op=ALU.subtract), r=["rtmp"], w=[skey])
            V(lambda e: e.tensor_tensor(out=x2, in0=t[2], in1=t[3], op=ALU.add), r=["rtmp"], w=[skey])

        def q_tile(kind, j):
            which = 1 if kind == "sample" else 0
            if kind == "own":
                r0 = (8 * j + 7) * 128
                sy.dma("s", xq[:], xp[r0:r0 + 128, :], w=["xq"])
                sy.dma("s", rKq[:], ropeK[:, 8 * j + 7, :], w=["rKq"])
                sy.dma("s", rIq[:], ropeI[:, 8 * j + 7, :], w=["rIq"])
                nrows = 128
            elif kind == "sample":
                V(lambda e: e.memset(xq[:], 0.0), w=["xq"])
                sy.dma("s", xq[0:64, :], xs, w=["xq"])
                sy.dma("s", rKq[:], ropeK[:, NT1 - 1, :], w=["rKq"])
                sy.dma("s", rIq[:], ropeI[:, NT1 - 1, :], w=["rIq"])
                nrows = 64
            else:
                V(lambda e: e.memset(xq[:], 0.0), w=["xq"])
                for jj in range(16):
                    r0 = (8 * jj + 6) * 128 + 126
                    sy.dma("s", xq[2 * jj:2 * jj + 2, :], xp[r0:r0 + 2, :], w=["xq"])
                sy.dma("s", rKq[:], ropeKh_d, w=["rKq"])
                sy.dma("s", rIq[:], ropeIh_d, w=["rIq"])
                nrows = 32
            ln_transpose_q(which, g1, shm)
            for b in range(4):
                wt, wk = wload(w2b[b])
                projq(hT, "hT", wt, wk, 0, 256,
                      lambda ps, pk, b=b: A(lambda e: e.copy(out=qf[:, b * 256:(b + 1) * 256], in_=ps[:, 0:256]), r=[pk], w=["qf"]))
            headnorm_q(qf[:], "qf", 8, 128, gqb[:], "gqb")
            rope_q(qf[:], "qf", 8, 128, 16, rKq[:], "rKq")
            V(lambda e: e.tensor_copy(out=qb, in_=qf[:]), r=["qf"], w=["mixb"])
            ps, pk = pp.get()
            pv = ps[:].bitcast(BF16)
            for h in range(8):
                T(lambda e, h=h: e.transpose(pv[:, h * 128:(h + 1) * 128], qb[:, h * 128:(h + 1) * 128], cmb[:]), r=["mixb"], w=[pk])
            A(lambda e: e.copy(out=QT[:].rearrange("p h t -> p (h t)"), in_=pv[:, 0:1024]), r=[pk], w=["QT"])
            for b in range(4):
                wt, wk = wload(w2b[4 + b])
                projq(hT, "hT", wt, wk, 0, 256,
                      lambda ps, pk, b=b: A(lambda e: e.copy(out=qf[:, b * 256:(b + 1) * 256], in_=ps[:, 0:256]), r=[pk], w=["qf"]))
            rope_q(qf[:], "qf", 16, 64, 8, rIq[:], "rIq")
            V(lambda e: e.tensor_copy(out=qb, in_=qf[:]), r=["qf"], w=["mixb"])
            ps, pk = pp.get()
            pv = ps[:].bitcast(BF16)
            for h in range(8):
                T(lambda e, h=h: e.transpose(pv[:, h * 128:(h + 1) * 128], qb[:, h * 128:(h + 1) * 128], cmb[:]), r=["mixb"], w=[pk])
            A(lambda e: e.copy(out=iqT[:].rearrange("p h t -> p (h t)"), in_=pv[:, 0:1024]), r=[pk], w=["iqT"])
            sy.dma("s", wiw[:], wiwb, w=["wiw"])
            ps, pk = pp.get()
            for k in range(16):
                T(lambda e, k=k: e.matmul(ps[:, 0:16], lhsT=hT[:, k, :], rhs=wiw[:, k, :], start=(k == 0), stop=(k == 15)),
                  r=["hT", "wiw"], w=[pk])
            V(lambda e: e.tensor_copy(out=iwf[:], in_=ps[:, 0:16]), r=[pk], w=["iwf"])
            if nrows < 128:
                V(lambda e: e.memset(ogt[:], 0.0), w=["sq"])
            if kind == "own":
                sy.dma("s", ogt[:], OGs[(1 + j) * 128:(2 + j) * 128, :], w=["sq"])
            elif kind == "sample":
                sy.dma("s", ogt[0:64, :], OGs[0:64, :], w=["sq"])
            else:
                sy.dma("s", ogt[0:32, :], OGs[64:96, :], w=["sq"])
            for b in range(4):
                wt, wk = wload(w2b[8 + b])
                def evg(ps, pk, b=b):
                    A(lambda e: e.activation(out=qf[:, b * 256:(b + 1) * 256], in_=ps[:, 0:256], func=AF.Silu), r=[pk], w=["qf"])
                    V(lambda e: e.tensor_tensor(out=mixb[:, 1024 + b * 256:1024 + (b + 1) * 256], in0=qf[:, b * 256:(b + 1) * 256],
                                                in1=ogt[:, b * 256:(b + 1) * 256], op=ALU.mult), r=["qf", "sq"], w=["mixb"])
                projq(hT, "hT", wt, wk, 0, 256, evg)

            if kind == "own":
                nkt = 8 * j + 8
                chunks = [(4 * ci, 4) for ci in range(nkt // 4)]
            elif kind == "sample":
                chunks = [(KT_S0, 4), (KT_S0 + 4, 4), (KT_S0 + 8, 1)]
            else:
                chunks = [(4 * ci, 4) for ci in range(32)]
            nch = len(chunks)
            L = sum(n for _, n in chunks) * 128
            coff = [sum(n for _, n in chunks[:i]) * 128 for i in range(nch)]

            def bias_src(ci):
                if kind == "own":
                    last = (ci == nch - 1)
                    if ci == 0:
                        return biasO_d[0]
                    if ci == 1:
                        return biasO_d[2] if last else biasO_d[1]
                    return biasO_d[3] if last else None
                if kind == "sample":
                    return biasS_d if ci == nch - 1 else None
                return biasH_d[:, ci * 512:(ci + 1) * 512]

            for ci, (kt0, nk) in enumerate(chunks):
                nc_ = nk * 128
                ik = IKc[ci % 2]; ikk = ("IKc", ci % 2)
                sy.dma("s", ik[0:64, 0:nc_], IKTs[:, kt0 * 128:kt0 * 128 + nc_], w=[ikk])
                sy.dma("s", ik[64:128, 0:nc_], IKTs[:, kt0 * 128:kt0 * 128 + nc_], w=[ikk])
                bsrc = bias_src(ci)
                bt = bch[ci % 2]; bk = ("bch", ci % 2)
                if bsrc is not None:
                    sy.dma("s", bt[:, 0:nc_], bsrc if kind != "sample" else bsrc, w=[bk])
                sc = score[:, coff[ci]:coff[ci] + nc_]
                for h in range(16):
                    hp, lo = h // 2, (h % 2) * 64
                    ps, pk = pp.get()
                    T(lambda e, hp=hp, lo=lo, ps=ps: e.matmul(ps[:, 0:nc_], lhsT=iqT[lo:lo + 64, hp, :], rhs=ik[lo:lo + 64, 0:nc_],
                                                              start=True, stop=True), r=["iqT", ikk], w=[pk])
                    rt = rl[h % 2]; rk = ("rl", h % 2)
                    A(lambda e, ps=ps, rt=rt: e.activation(out=rt[:, 0:nc_], in_=ps[:, 0:nc_], func=AF.Relu), r=[pk], w=[rk])
                    if h == 0:
                        if bsrc is not None:
                            V(lambda e, rt=rt: e.scalar_tensor_tensor(out=sc, in0=rt[:, 0:nc_], scalar=iwf[:, 0:1], in1=bt[:, 0:nc_],
                                                                      op0=ALU.mult, op1=ALU.add), r=[rk, "iwf", bk], w=["score"])
                        else:
                            V(lambda e, rt=rt: e.tensor_scalar(out=sc, in0=rt[:, 0:nc_], scalar1=iwf[:, 0:1], scalar2=None, op0=ALU.mult),
                              r=[rk, "iwf"], w=["score"])
                    else:
                        V(lambda e, rt=rt, h=h: e.scalar_tensor_tensor(out=sc, in0=rt[:, 0:nc_], scalar=iwf[:, h:h + 1], in1=sc,
                                                                       op0=ALU.mult, op1=ALU.add), r=[rk, "iwf", "score"], w=["score"])
                V(lambda e, ci=ci: e.tensor_reduce(out=hmx[:, ci:ci + 1], in_=sc, axis=AX.X, op=ALU.max), r=["score"], w=["hmx"])
                rt = rl[0]; rk = ("rl", 0)
                V(lambda e, rt=rt: e.tensor_scalar(out=rt[:, 0:nc_], in0=sc, scalar1=-1.0e29, scalar2=-3.0e30, op0=ALU.is_lt, op1=ALU.mult),
                  r=["score"], w=[rk])
                V(lambda e, rt=rt: e.scalar_tensor_tensor(out=rt[:, 0:nc_], in0=sc, scalar=-1.0, in1=rt[:, 0:nc_], op0=ALU.mult, op1=ALU.add),
                  r=["score", rk], w=[rk])
                V(lambda e, rt=rt, ci=ci: e.tensor_reduce(out=lmx[:, ci:ci + 1], in_=rt[:, 0:nc_], axis=AX.X, op=ALU.max), r=[rk], w=["lmx"])
            V(lambda e: e.tensor_reduce(out=bs[:, 0:1], in_=hmx[:, 0:nch], axis=AX.X, op=ALU.max), r=["hmx"], w=["bs"])
            V(lambda e: e.tensor_reduce(out=bs[:, 1:2], in_=lmx[:, 0:nch], axis=AX.X, op=ALU.max), r=["lmx"], w=["bs"])
            V(lambda e: e.tensor_scalar(out=bs[:, 2:3], in0=bs[:, 1:2], scalar1=-1.0, scalar2=-1.0, op0=ALU.mult, op1=ALU.add),
              r=["bs"], w=["bs"])
            V(lambda e: e.tensor_tensor(out=bs[:, 3:4], in0=bs[:, 0:1], in1=bs[:, 2:3], op=ALU.subtract), r=["bs"], w=["bs"])
            nseg = (L + 2047) // 2048
            for it in range(NITER):
                fac = 0.5 ** (it + 1)
                V(lambda e, fac=fac: e.scalar_tensor_tensor(out=bs[:, 4:5], in0=bs[:, 3:4], scalar=fac, in1=bs[:, 2:3],
                                                            op0=ALU.mult, op1=ALU.add), r=["bs"], w=["bs"])
                for sg in range(nseg):
                    c0 = sg * 2048; c1 = min(L, c0 + 2048)
                    V(lambda e, c0=c0, c1=c1, sg=sg: e.tensor_scalar(out=junk[:, 0:c1 - c0], in0=score[:, c0:c1], scalar1=bs[:, 4:5],
                                                                     scalar2=0.0, op0=ALU.is_gt, op1=ALU.add,
                                                                     accum_out=cnts[:, sg:sg + 1]), r=["score", "bs"], w=["qf", "cnts"])
                V(lambda e: e.tensor_reduce(out=bs[:, 5:6], in_=cnts[:, 0:nseg], axis=AX.X, op=ALU.add), r=["cnts"], w=["bs"])
                V(lambda e, fac=fac: e.tensor_scalar(out=bs[:, 6:7], in0=bs[:, 5:6], scalar1=255.5, scalar2=fac, op0=ALU.is_gt, op1=ALU.mult),
                  r=["bs"], w=["bs"])
                V(lambda e: e.scalar_tensor_tensor(out=bs[:, 2:3], in0=bs[:, 3:4], scalar=bs[:, 6:7], in1=bs[:, 2:3],
                                                   op0=ALU.mult, op1=ALU.add), r=["bs"], w=["bs"])
            nob = 0
            for ci, (kt0, nk) in enumerate(chunks):
                nc_ = nk * 128
                kc = KTc[ci % 2]; kck = ("KTc", ci % 2)
                vc = Vc[ci % 2]; vck = ("Vc", ci % 2)
                sy.dma("s", kc[:, 0:nk], KTs[kt0:kt0 + nk].rearrange("t p h s -> p t h s"), w=[kck])
                sy.dma("s", vc[:, 0:nk, :], Vs[kt0 * 128:(kt0 + nk) * 128, :].rearrange("(t p) c -> p t c", p=128), w=[vck])
                mm = m01[ci % 2]; mmk = ("m01", ci % 2)
                V(lambda e, mm=mm: e.tensor_scalar(out=mm[:, 0:nc_], in0=score[:, coff[ci]:coff[ci] + nc_], scalar1=bs[:, 2:3],
                                                   scalar2=None, op0=ALU.is_gt), r=["score", "bs"], w=[mmk])
                ps, pk = pp.get()
                pv = ps[:].bitcast(BF16)
                for t in range(nk):
                    T(lambda e, t=t, mm=mm: e.transpose(pv[:, t * 128:(t + 1) * 128], mm[:, t * 128:(t + 1) * 128], cmb[:]), r=[mmk], w=[pk])
                mt = mT[ci % 2]; mtk = ("mT", ci % 2)
                A(lambda e, mt=mt: e.copy(out=mt[:, 0:nc_], in_=pv[:, 0:nc_]), r=[pk], w=[mtk])
                for h in range(8):
                    ps, pk = pp.get()
                    for t in range(nk):
                        T(lambda e, t=t, h=h, ps=ps: e.matmul(ps[:, t * 128:(t + 1) * 128], lhsT=kc[:, t, h, :], rhs=QT[:, h, :],
                                                              start=True, stop=True), r=[kck, "QT"], w=[pk])
                    et = eT[h % 2]; ek = ("eT", h % 2)
                    A(lambda e, et=et, ps=ps: e.activation(out=et[:, 0:nc_], in_=ps[:, 0:nc_], func=AF.Exp, scale=128.0 ** -0.5),
                      r=[pk], w=[ek])
                    pt = PT[h % 2]; ptk = ("PT", h % 2)
                    V(lambda e, et=et, pt=pt, mt=mt: e.tensor_tensor(out=pt[:, 0:nc_], in0=et[:, 0:nc_], in1=mt[:, 0:nc_], op=ALU.mult),
                      r=[ek, mtk], w=[ptk])
                    ob, obk = OB[h // 3]
                    oc = (h % 3) * 129
                    for t in range(nk):
                        T(lambda e, t=t, h=h, pt=pt, ob=ob, oc=oc: e.matmul(
                            ob[:, oc:oc + 129], lhsT=pt[:, t * 128:(t + 1) * 128], rhs=vc[:, t, h * 130:h * 130 + 129],
                            start=(t == 0), stop=(t == nk - 1)), r=[ptk, vck], w=[obk])
                    if h in (2, 5, 7):
                        bi = h // 3
                        nh_ = 3 if bi < 2 else 2
                        oav = oa[:, bi * 3:bi * 3 + nh_, :].rearrange("p h d -> p (h d)")
                        if ci == 0:
                            V(lambda e, ob=ob, oav=oav, nh_=nh_: e.tensor_copy(out=oav, in_=ob[:, 0:nh_ * 129]), r=[obk], w=["oa"])
                        else:
                            V(lambda e, ob=ob, oav=oav, nh_=nh_: e.tensor_tensor(out=oav, in0=oav, in1=ob[:, 0:nh_ * 129], op=ALU.add),
                              r=[obk, "oa"], w=["oa"])
            V(lambda e: e.tensor_scalar(out=rden[:], in0=oa[:, :, 128:129].rearrange("p h o -> p (h o)"), scalar1=1.0e-30, scalar2=None, op0=ALU.max), r=["oa"], w=["rden"])
            V(lambda e: e.reciprocal(out=rden[:], in_=rden[:]), r=["rden"], w=["rden"])
            V(lambda e: e.tensor_tensor(out=mixb[:, 0:1024].rearrange("p (h d) -> p h d", h=8), in0=oa[:, :, 0:128],
                                        in1=rden[:].rearrange("p (h o) -> p h o", o=1).to_broadcast([128, 8, 128]), op=ALU.mult),
              r=["oa", "rden"], w=["mixb"])
            if DBG:
                V(lambda e: e.tensor_copy(out=xsq, in_=mixb[:]), r=["mixb"], w=["score"])
                sy.dma("s", dbg_mix, xsq, r=["score"], w=["dbg_mix"], is_output=True)
            for half in range(2):
                ps, pk = pp.get()
                pv = ps[:].bitcast(BF16)
                for kk in range(8):
                    k = half * 8 + kk
                    T(lambda e, k=k, kk=kk: e.transpose(pv[:, kk * 128:(kk + 1) * 128], mixb[:, k * 128:(k + 1) * 128], cmb[:]), r=["mixb"], w=[pk])
                A(lambda e, half=half: e.copy(out=mixT[:, half * 8:(half + 1) * 8, :].rearrange("p h t -> p (h t)"), in_=pv[:, 0:1024]),
                  r=[pk], w=["hT"])
            for b in range(8):
                wt, wk = wload(woutb[b])
                gi = cntq["g"] % 2
                cntq["g"] += 1
                sy.dma("s", gtc[gi][:, 0:256], GTd[which:which + 1, b * 256:(b + 1) * 256].to_broadcast([128, 256]), w=[("gtc", gi)])
                def evo(ps, pk, b=b, gi=gi):
                    V(lambda e: e.tensor_tensor(out=tmpy[:, 0:256], in0=ps[:, 0:256], in1=gtc[gi][:, 0:256], op=ALU.mult),
                      r=[pk, ("gtc", gi)], w=[("rl", 1)])
                    V(lambda e: e.tensor_tensor(out=xq[:, b * 256:(b + 1) * 256], in0=xq[:, b * 256:(b + 1) * 256], in1=tmpy[:, 0:256], op=ALU.add),
                      r=[("rl", 1), "xq"], w=["xq"])
                projq(mixT, "hT", wt, wk, 0, 256, evo)
            if DBG:
                sy.dma("s", dbg_xm, xq[:], r=["xq"], w=["dbg_xm"], is_output=True)
            ln_transpose_q(which, g2, shf)
            for i in range(44):
                wt, wk = wload(wupb[i])
                uext = uexts[i % 2]; ta = tas[i % 2]; sa = sas[i % 2]
                uk = ("uext", i % 2); tk = ("ta", i % 2); sk = ("sa", i % 2)
                ps, pk = pp.get()
                for ab in range(2):
                    for k in range(16):
                        T(lambda e, k=k, ab=ab, ps=ps: e.matmul(ps[:, ab * 128:(ab + 1) * 128], lhsT=wt[:, k, ab * 128:(ab + 1) * 128],
                                                                rhs=hT[:, k, :], start=(k == 0), stop=(k == 15)), r=["hT", wk], w=[pk])
                A(lambda e, ps=ps: e.copy(out=uext[:, :, 2:130], in_=ps[:, 0:256].rearrange("p (a t) -> p a t", a=2)), r=[pk], w=[uk])
                for ab in range(2):
                    ch = ab * 44 + i
                    if kind == "own":
                        V(lambda e, ab=ab, ch=ch: e.tensor_copy(out=uext[:, ab, 0:2], in_=uhalo[:, ch, 2 * j:2 * j + 2]), r=["uhalo"], w=[uk])
                    elif kind == "sample":
                        V(lambda e, ab=ab, ch=ch: e.tensor_copy(out=uext[:, ab, 0:2], in_=sconv[:, ch, :]), r=["sconv"], w=[uk])
                    else:
                        V(lambda e, ab=ab: e.memset(uext[:, ab, 0:2], 0.0), w=[uk])
                    if kind == "halo":
                        V(lambda e, ab=ab, ch=ch: e.tensor_tensor(out=uhalo[:, ch, :], in0=uext[:, ab, 2:34], in1=hfl[:], op=ALU.mult),
                          r=[uk, "hfl"], w=["uhalo"])
                    if kind == "sample":
                        V(lambda e, ab=ab, ch=ch: e.tensor_copy(out=oconv[:, ch, :], in_=uext[:, ab, 64:66]), r=[uk], w=["oconv"])
                    if kind == "own" and j == 15:
                        V(lambda e, ab=ab, ch=ch: e.tensor_copy(out=oconv[:, ch, :], in_=uext[:, ab, 128:130]), r=[uk], w=["oconv"])
                    A(lambda e, ab=ab, ch=ch: e.activation(out=ta[:, ab, :], in_=uext[:, ab, 2:130], func=AF.Identity,
                                                           scale=convw[:, ch, 2:3], bias=convw[:, ch, 3:4]), r=[uk, "convw"], w=[tk])
                    V(lambda e, ab=ab, ch=ch: e.scalar_tensor_tensor(out=ta[:, ab, :], in0=uext[:, ab, 1:129], scalar=convw[:, ch, 1:2],
                                                                     in1=ta[:, ab, :], op0=ALU.mult, op1=ALU.add), r=[uk, tk, "convw"], w=[tk])
                    V(lambda e, ab=ab, ch=ch: e.scalar_tensor_tensor(out=ta[:, ab, :], in0=uext[:, ab, 0:128], scalar=convw[:, ch, 0:1],
                                                                     in1=ta[:, ab, :], op0=ALU.mult, op1=ALU.add), r=[uk, tk, "convw"], w=[tk])
                A(lambda e: e.activation(out=sa[:], in_=ta[:, 0, :], func=AF.Silu), r=[tk], w=[sk])
                V(lambda e, i=i: e.tensor_tensor(out=hidT[:, i, :], in0=sa[:], in1=ta[:, 1, :], op=ALU.mult), r=[sk, tk], w=["score"])
            if kind == "sample":
                sy.dma("s", o_convs, oconv[:], r=["oconv"], w=["o_convs"], is_output=True)
            if kind == "own" and j == 15:
                sy.dma("s", o_convp, oconv[:], r=["oconv"], w=["o_convp"], is_output=True)
            if kind != "halo":
                for nb in range(4):
                    ps, pk = pp.get()
                    for i2 in range(22):
                        di = cntq["d"] % 3
                        cntq["d"] += 1
                        sy.dma("s", wdp[di][:], wdnb[nb, 2 * i2:2 * i2 + 2].rearrange("i p c -> p i c"), w=[("wdp", di)])
                        for ii in range(2):
                            i = 2 * i2 + ii
                            T(lambda e, i=i, ii=ii, di=di, ps=ps: e.matmul(ps[:, :], lhsT=hidT[:, i, :], rhs=wdp[di][:, ii, :], start=(i == 0), stop=(i == 43)),
                              r=["score", ("wdp", di)], w=[pk])
                    gi = cntq["g"] % 2
                    cntq["g"] += 1
                    sy.dma("s", gtc[gi][:], GTd[which:which + 1, 2048 + nb * 512:2048 + (nb + 1) * 512].to_broadcast([128, 512]), w=[("gtc", gi)])
                    V(lambda e, ps=ps, nb=nb, gi=gi: e.tensor_tensor(out=tmpy[:, 0:512], in0=ps[:, :], in1=gtc[gi][:], op=ALU.mult),
                      r=[pk, ("gtc", gi)], w=[("rl", 1)])
                    V(lambda e, nb=nb: e.tensor_tensor(out=xq[:, nb * 512:(nb + 1) * 512], in0=xq[:, nb * 512:(nb + 1) * 512], in1=tmpy[:, 0:512], op=ALU.add),
                      r=[("rl", 1), "xq"], w=["xq"])
                if kind == "own":
                    sy.dma("s", o_y[j * 128:(j + 1) * 128, :], xq[:], r=["xq"], w=["o_y"], is_output=True)
                else:
                    sy.dma("s", o_ys, xq[0:64, :], r=["xq"], w=["o_ys"], is_output=True)

        QT_LIST = os.environ.get("QTILES", "all")
        tl = [("sample", 0), ("halo", 0)] + [("own", j) for j in range(16)]
        if QT_LIST != "all":
            tl = [tl[int(x)] for x in QT_LIST.split(",")]
        for kind, j in tl:
            q_tile(kind, j)
        sy.barrier()
        stack_holder[0].close()
        stack_holder[0] = None

    sy.finish()
    return nc


def _rope_tab(pos, half, theta=500000.0):
    inv = (np.float32(theta) ** (-(np.arange(half, dtype=np.float32) / np.float32(half)))).astype(np.float32)
    ang = (pos.astype(np.float32)[:, None] * inv[None, :]).astype(np.float32)
    return np.concatenate([np.cos(ang.astype(np.float64)), np.sin(ang.astype(np.float64))], axis=1).astype(np.float32)


def _host_inputs(inp):
    f = np.float32
    x_prompt = np.asarray(inp["x_prompt"], f)[0]
    xpad = np.zeros(((128 + 14) * 128, D), f)
    xpad[7 * 128:(7 + 128) * 128] = x_prompt
    w_in = np.asarray(inp["w_in"], f)[0]
    offs = np.cumsum([0, 1024, 1024, 1024, 1024, 64, 16, 512, 512, 1024, 1024, 16])
    aq, ak, av, iq, ik, iw, gq, gk, gv, gr, glr = [w_in[:, offs[i]:offs[i + 1]] for i in range(11)]

    def kmaj(w):
        return np.ascontiguousarray(w.reshape(16, 128, -1).transpose(1, 0, 2))
    w1 = kmaj(np.concatenate([ak, av, gk, gv, gq, ik, glr], axis=1))
    w_ada = np.asarray(inp["w_ada"], f)[0]
    wada = np.ascontiguousarray(w_ada.reshape(16, 128, 24, 512).transpose(2, 1, 0, 3))
    b_ada = np.asarray(inp["b_ada"], f)[0]
    badaT = np.ascontiguousarray(b_ada.reshape(96, 128).T)
    brow = np.concatenate([b_ada[4096:6144], b_ada[10240:12288]])
    badaR = np.ascontiguousarray(np.stack([brow, brow]))
    cmat = np.zeros((128, 6, 128), f)
    ii = np.arange(128)
    cmat[:, 0, :] = np.eye(128)
    cmat[:, 1, :] = (ii[:, None] <= ii[None, :])
    cmat[:, 2, :] = (ii[:, None] > ii[None, :])
    cmat[:, 3, :] = 1.0
    cmat[0, 4, :] = 1.0
    cmat[0, 5, 64:] = 1.0
    cmat[1, 5, :64] = 1.0
    common = {
        "badaT": badaT, "badaR": badaR,
        "gmixT": np.ascontiguousarray(np.asarray(inp["g_mix"], f)[0].reshape(16, 128).T),
        "gffnT": np.ascontiguousarray(np.asarray(inp["g_ffn"], f)[0].reshape(16, 128).T),
        "wada": wada, "w1": w1,
        "gk_b": np.ascontiguousarray(np.broadcast_to(np.tile(np.asarray(inp["g_k"], f)[0], 8)[None, :], (128, 1024))),
        "gq_b": np.ascontiguousarray(np.broadcast_to(np.tile(np.asarray(inp["g_q"], f)[0], 8)[None, :], (128, 1024))),
        "ggla_b": np.ascontiguousarray(np.broadcast_to(np.tile(np.asarray(inp["g_gla"], f)[0], 4)[None, :], (128, 1024))),
        "wg2": np.ascontiguousarray(np.concatenate([np.asarray(inp["w_gate2"], f)[0], np.asarray(inp["b_gate2"], f)[0][None, :]], 0)),
        "cmat": cmat,
    }
    w_out = np.asarray(inp["w_out"], f)[0]
    w_up = np.asarray(inp["w_up"], f)[0]
    w_down = np.asarray(inp["w_down"], f)[0]
    k2 = kmaj(np.concatenate([aq, iq, gr], axis=1))
    common["w2"] = np.ascontiguousarray(k2.reshape(128, 16, 12, 256).transpose(2, 0, 1, 3))
    common["wiw"] = kmaj(iw)
    common["wout"] = np.ascontiguousarray(kmaj(w_out).reshape(128, 16, 8, 256).transpose(2, 0, 1, 3))
    ku = kmaj(w_up)
    common["wup"] = np.ascontiguousarray(np.concatenate(
        [ku[:, :, :DFF].reshape(128, 16, 44, 128), ku[:, :, DFF:].reshape(128, 16, 44, 128)], axis=3).transpose(2, 0, 1, 3))
    common["wdn"] = np.ascontiguousarray(w_down.reshape(44, 128, 4, 512).transpose(2, 0, 1, 3))
    cw = np.concatenate([np.asarray(inp["w_conv"], f)[0], np.asarray(inp["b_conv"], f)[0][None, :]], axis=0)
    common["convw"] = np.ascontiguousarray(cw.reshape(4, 88, 128).transpose(2, 1, 0))
    bS = np.zeros((128, 128), f); bS[:, 64:] = -BIG
    common["biasS"] = bS
    maps = []
    for c in range(NCORE):
        m = dict(common)
        sc_ = np.asarray(inp["state_ffn_conv"], f)[0, c]
        m["sconv"] = np.ascontiguousarray(sc_.reshape(2, 88, 128).transpose(2, 1, 0))
        npad = (7 - c) * 128
        bO = np.zeros((4, 128, 512), f)
        padrow = np.zeros(1024, f); padrow[:npad] = -BIG
        bO[0] = padrow[None, 0:512]; bO[1] = padrow[None, 512:1024]
        diag = np.zeros((128, 512), f); diag[0:64, 448:512] = -BIG
        bO[2] = bO[1] + diag; bO[3] = diag
        m["biasO"] = bO
        bH = np.full((128, 16384), -BIG, f)
        hfl = np.zeros((128, 32), f)
        posh = np.zeros(128, np.int64)
        for jj in range(16):
            a = 8 * jj + c - 1
            if a >= 0:
                for e_ in range(2):
                    bH[2 * jj + e_, npad:(8 * jj + 7) * 128] = 0.0
                    hfl[:, 2 * jj + e_] = 1.0
                    posh[2 * jj + e_] = a * 128 + 126 + e_
        m["biasH"] = bH
        m["hflag"] = hfl
        m["ropeKh"] = _rope_tab(posh, 16)
        m["ropeIh"] = _rope_tab(posh, 8)
        m["xp"] = xpad[c * 128:(c + NREL) * 128]
        m["xs"] = np.ascontiguousarray(np.asarray(inp["x_sample"], f)[c])
        cT = np.stack([np.asarray(inp["c_prompt"], f)[0], np.asarray(inp["c_sample"], f)[c]], axis=1)
        m["cT"] = np.ascontiguousarray(cT.reshape(16, 128, 2).transpose(1, 0, 2))
        flag = np.zeros((128, 2 * NT1), f)
        posK = np.zeros((NT1, 128), np.int64)
        for r in range(NREL):
            a = r - (7 - c)
            if 0 <= a < 128:
                flag[:, r] = 1.0
                posK[r] = a * 128 + np.arange(128)
        flag[:64, NT1 - 1] = 1.0
        posK[NT1 - 1, :64] = 1024 + np.arange(64)
        flag[:, NT1:] = -flag[:, :NT1] / 16.0
        m["flagt"] = flag
        m["ropeK"] = np.ascontiguousarray(_rope_tab(posK.reshape(-1), 16).reshape(NT1, 128, 32).transpose(1, 0, 2))
        m["ropeI"] = np.ascontiguousarray(_rope_tab(posK.reshape(-1), 8).reshape(NT1, 128, 16).transpose(1, 0, 2))
        ck = np.asarray(inp["cache_k"], f)[0, c]
        m["ckT"] = np.ascontiguousarray(ck.reshape(8, 128, 8, 128).transpose(0, 3, 2, 1))
        m["cv"] = np.ascontiguousarray(np.asarray(inp["cache_v"], f)[0, c].reshape(1024, 1024))
        m["cikT"] = np.ascontiguousarray(np.asarray(inp["cache_idx_k"], f)[0, c].T)
        m["sgla"] = np.ascontiguousarray(np.asarray(inp["state_gla"], f)[0, c].transpose(1, 0, 2).reshape(128, 1024))
        maps.append(m)
    return maps


def kernel(**inp):
    maps = _host_inputs(inp)
    nc = build_program()
    res = run_bass_kernel_spmd(nc, maps, core_ids=list(range(NCORE)))
    R = res.results
    f = np.float32
    kp = np.zeros((128, 128, 1024), f); vp = np.zeros((128, 128, 1024), f); ikp = np.zeros((128, 128, 64), f)
    for c in range(NCORE):
        for j in range(16):
            kp[8 * j + c] = R[c]["o_k"][j * 128:(j + 1) * 128]
            vp[8 * j + c] = R[c]["o_v"][j * 128:(j + 1) * 128]
            ikp[8 * j + c] = R[c]["o_ik"][j * 128:(j + 1) * 128]
    k_prompt = kp.reshape(1, 1, 16384, 8, 128)
    v_prompt = vp.reshape(1, 1, 16384, 8, 128)
    idx_k_prompt = ikp.reshape(1, 1, 16384, 64)
    gla_p = np.ascontiguousarray(R[0]["o_glap"].reshape(128, 4, 256).transpose(1, 0, 2)).reshape(1, 1, 4, 128, 256)
    k_sample = np.stack([R[c]["o_ks"] for c in range(NCORE)]).reshape(1, 8, 64, 8, 128)
    v_sample = np.stack([R[c]["o_vs"] for c in range(NCORE)]).reshape(1, 8, 64, 8, 128)
    idx_k_sample = np.stack([R[c]["o_iks"] for c in range(NCORE)]).reshape(1, 8, 64, 64)
    gla_s = np.stack([R[c]["o_glas"].reshape(128, 4, 256).transpose(1, 0, 2) for c in range(NCORE)]).reshape(1, 8, 4, 128, 256)
    yp = np.zeros((128, 128, D), f)
    for c in range(NCORE):
        for j in range(16):
            yp[8 * j + c] = R[c]["o_y"][j * 128:(j + 1) * 128]
    y_prompt = yp.reshape(1, 16384, D)
    y_sample = np.stack([R[c]["o_ys"] for c in range(NCORE)])
    ffn_conv_prompt = np.ascontiguousarray(R[7]["o_convp"].transpose(2, 1, 0)).reshape(1, 1, 2, 2 * DFF)
    ffn_conv_sample = np.stack([R[c]["o_convs"].transpose(2, 1, 0).reshape(2, 2 * DFF) for c in range(NCORE)]).reshape(1, 8, 2, 2 * DFF)
    return (y_prompt, y_sample, k_prompt, v_prompt, idx_k_prompt, np.ascontiguousarray(gla_p), ffn_conv_prompt,
            k_sample, v_sample, idx_k_sample, np.ascontiguousarray(gla_s), ffn_conv_sample)
```

```python
import numpy as np
import concourse.bass as bass
import concourse.mybir as mybir
from concourse.bass_utils import run_bass_kernel_spmd

F32 = mybir.dt.float32
BF16 = mybir.dt.bfloat16
AF = mybir.ActivationFunctionType
ALU = mybir.AluOpType
AX = mybir.AxisListType

D = 2048
NCORE = 8
NREL = 135
NT1 = NREL + 1
NKT = 140
KT_S0 = 128
EPS = 1e-6
DFF = 5632
W1C = 4176
C_AK, C_AV, C_GK, C_GV, C_GQ, C_IK, C_GLR = 0, 1024, 2048, 2560, 3584, 4096, 4160
BIG = 1.0e30


class Sync:
    def __init__(self, nc, ndma=(16, 4, 10)):
        self.nc = nc
        self.eng = {"v": nc.vector, "a": nc.scalar, "p": nc.gpsimd, "t": nc.tensor, "s": nc.sync}
        self.csem = {k: nc.alloc_semaphore("c_" + k) for k in ("v", "a", "p", "t")}
        self.ccount = {k: 0 for k in self.csem}
        self.dpool = {}
        for q, n in zip(("s", "a", "p"), ndma):
            self.dpool[q] = [[nc.alloc_semaphore(f"d_{q}{i}"), 0] for i in range(n)]
        self.dnext = {q: 0 for q in self.dpool}
        self.waited = {}
        self.lastw = {}
        self.readers = {}
        self.out_tokens = []
        self.nwait = 0

    def _wait(self, e, tok):
        sem, val, src = tok[:3]
        key = (e, id(sem))
        if self.waited.get(key, 0) >= val:
            return
        self.eng[e].wait_ge(sem, val)
        self.nwait += 1
        self.waited[key] = val

    def _deps(self, e, reads, writes):
        for k in reads:
            w = self.lastw.get(k)
            if w is not None:
                self._wait(e, w)
        for k in writes:
            for r in self.readers.get(k, ()):
                if r[2] != e or r[3]:
                    self._wait(e, r[:3])
            w = self.lastw.get(k)
            if w is not None and (w[2] != e or w[3]):
                self._wait(e, w[:3])

    def _commit(self, tok, reads, writes):
        for k in reads:
            self.readers.setdefault(k, []).append(tok)
        for k in writes:
            self.lastw[k] = tok
            self.readers[k] = []

    def op(self, e, fn, r=(), w=()):
        self._deps(e, r, w)
        inst = fn(self.eng[e])
        self.ccount[e] += 1
        inst.then_inc(self.csem[e], 1)
        tok = (self.csem[e], self.ccount[e], e, False)
        self._commit(tok, r, w)
        return tok

    def dma(self, q, out, in_, r=(), w=(), is_output=False, **kw):
        self._deps(q, r, w)
        pool = self.dpool[q]
        i = self.dnext[q]
        self.dnext[q] = (i + 1) % len(pool)
        sem, cnt = pool[i]
        if cnt:
            self._wait(q, (sem, 16 * cnt, q))
        inst = self.eng[q].dma_start(out=out, in_=in_, **kw)
        inst.then_inc(sem, 16)
        pool[i][1] = cnt + 1
        tok = (sem, 16 * (cnt + 1), q, True)
        self._commit(tok, r, w)
        if is_output:
            self.out_tokens.append(tok)
        return tok

    def barrier(self):
        toks = [(self.csem[k], self.ccount[k], k) for k in self.csem if self.ccount[k]]
        for q, pool in self.dpool.items():
            for sem, cnt in pool:
                if cnt:
                    toks.append((sem, 16 * cnt, q))
        for e in ("v", "a", "p", "t", "s"):
            for t in toks:
                if t[2] != e or t[0] not in self.csem.values():
                    self._wait(e, t)
        self.lastw.clear()
        self.readers.clear()

    def finish(self):
        for t in self.out_tokens:
            self._wait("s", t[:3])
        for q, pool in self.dpool.items():
            for sem, cnt in pool:
                if cnt:
                    self._wait("s", (sem, 16 * cnt, q))


class PsumPool:
    def __init__(self, nc, n=8):
        self.banks = [nc.alloc_psum_tensor(f"psb{i}", [128, 512], F32) for i in range(n)]
        self.i = 0
        self.n = n

    def get(self):
        b = self.banks[self.i]
        k = ("ps", self.i)
        self.i = (self.i + 1) % self.n
        return b, k


def build_program(phases=("p0", "p1", "pq")):
    nc = bass.Bass("TRN2", target_bir_lowering=False)
    dt = nc.dram_tensor

    def din(name, shape, dtype=F32):
        return dt(name, list(shape), dtype, kind="ExternalInput").ap()

    def dout(name, shape, dtype=F32):
        return dt(name, list(shape), dtype, kind="ExternalOutput").ap()

    xp = din("xp", [NREL * 128, D])
    xs = din("xs", [64, D])
    cT = din("cT", [128, 16, 2])
    badaT = din("badaT", [128, 96])
    badaR = din("badaR", [2, 4096])
    gmixT = din("gmixT", [128, 16])
    gffnT = din("gffnT", [128, 16])
    wada = din("wada", [24, 128, 16, 512])
    w1 = din("w1", [128, 16, W1C])
    gk_b = din("gk_b", [128, 1024])
    gq_b = din("gq_b", [128, 1024])
    ggla_b = din("ggla_b", [128, 1024])
    wg2 = din("wg2", [17, 512])
    flagt = din("flagt", [128, 2 * NT1])
    ropeK = din("ropeK", [128, NT1, 32])
    ropeI = din("ropeI", [128, NT1, 16])
    cmat = din("cmat", [128, 6, 128])
    ckT = din("ckT", [8, 128, 8, 128])
    cv = din("cv", [1024, 1024])
    cikT = din("cikT", [64, 1024])
    sgla = din("sgla", [128, 1024])

    o_k = dout("o_k", [16 * 128, 1024])
    o_v = dout("o_v", [16 * 128, 1024])
    o_ik = dout("o_ik", [16 * 128, 64])
    o_ks = dout("o_ks", [64, 1024])
    o_vs = dout("o_vs", [64, 1024])
    o_iks = dout("o_iks", [64, 64])
    o_glap = dout("o_glap", [128, 1024])
    o_glas = dout("o_glas", [128, 1024])
    o_y = dout("o_y", [16 * 128, D])
    import os as _os
    DBG = _os.environ.get("DBG", "0") == "1"
    if DBG:
        dbg_mix = dout("dbg_mix", [128, D]); dbg_xm = dout("dbg_xm", [128, D]); dbg_q = dout("dbg_q", [128, 1024])
    o_ys = dout("o_ys", [64, D])
    o_convp = dout("o_convp", [128, 88, 2])
    o_convs = dout("o_convs", [128, 88, 2])
    w2_d = din("w2", [12, 128, 16, 256])
    wiw_d = din("wiw", [128, 16, 16])
    wout_d = din("wout", [8, 128, 16, 256])
    wup_d = din("wup", [44, 128, 16, 256])
    wdn_d = din("wdn", [4, 44, 128, 512])
    convw_d = din("convw", [128, 88, 4])
    sconv_d = din("sconv", [128, 88, 2])
    hflag_d = din("hflag", [128, 32])
    ropeKh_d = din("ropeKh", [128, 32])
    ropeIh_d = din("ropeIh", [128, 16])
    biasO_d = din("biasO", [4, 128, 512])
    biasS_d = din("biasS", [128, 128])
    biasH_d = din("biasH", [128, 16384])

    KTs = dt("KTs", [NKT, 128, 8, 128], BF16).ap()
    Vs = dt("Vs", [NKT * 128, 8 * 130], BF16).ap()
    IKTs = dt("IKTs", [64, NKT * 128], BF16).ap()
    OGs = dt("OGs", [17 * 128, 1024], F32).ap()
    GTd = dt("GTd", [2, 4096], F32).ap()
    w2b = dt("w2b", [12, 128, 16, 256], BF16).ap()
    wiwb = dt("wiwb", [128, 16, 16], BF16).ap()
    woutb = dt("woutb", [8, 128, 16, 256], BF16).ap()
    wupb = dt("wupb", [44, 128, 16, 256], BF16).ap()
    wdnb = dt("wdnb", [4, 44, 128, 512], BF16).ap()

    sy = Sync(nc)
    pp = PsumPool(nc)
    V, A, P, T = (lambda fn, r=(), w=(): sy.op("v", fn, r, w)), (lambda fn, r=(), w=(): sy.op("a", fn, r, w)), \
        (lambda fn, r=(), w=(): sy.op("p", fn, r, w)), (lambda fn, r=(), w=(): sy.op("t", fn, r, w))

    import contextlib
    stack_holder = [None]

    def sb(name, shape, dtype=F32):
        sb.n = getattr(sb, "n", 0) + 1
        name = f"sb{sb.n}_{name}"
        if stack_holder[0] is None:
            return nc.alloc_sbuf_tensor(name, list(shape), dtype)
        return stack_holder[0].enter_context(nc.sbuf_tensor(name, list(shape), dtype))

    cm = sb("cm", [128, 6, 128])
    cmb = sb("cmb", [128, 128], BF16)
    modT = sb("modT", [128, 96, 2])
    g1 = sb("g1", [128, 2, 16]); shm = sb("shm", [128, 2, 16])
    g2 = sb("g2", [128, 2, 16]); shf = sb("shf", [128, 2, 16])
    tmpc = sb("tmpc", [128, 16, 2])
    negh = sb("negh", [128, 1])

    sy.dma("s", cm[:], cmat, w=["cm"])
    V(lambda e: e.tensor_copy(out=cmb[:], in_=cm[:, 0, :]), r=["cm"], w=["cmb"])
    V(lambda e: e.memset(negh[:], -0.5), w=["negh"])
    IDF = cm[:, 0, :]

    def rsqrt_small(dst, src, n, mul, key_src, key_dst):
        V(lambda e: e.tensor_scalar(out=dst, in0=src, scalar1=mul, scalar2=EPS, op0=ALU.mult, op1=ALU.add),
          r=[key_src], w=[key_dst])
        P(lambda e: e.tensor_tensor(out=dst, in0=dst, in1=negh[:, 0:1].to_broadcast([128, n]), op=ALU.pow),
          r=[key_dst, "negh"], w=[key_dst])

    if "p0" in phases:
        stack_holder[0] = contextlib.ExitStack()
        gtrow = sb("gtrow", [2, 4096])
        cTt = sb("cTt", [128, 16, 2]); scb = sb("scb", [128, 16, 2], BF16)
        bT = sb("bT", [128, 96]); gmT = sb("gmT", [128, 16]); gfT = sb("gfT", [128, 16])
        brow = sb("brow", [2, 4096])
        wab = [sb(f"wab{i}", [128, 16, 512], BF16) for i in range(2)]
        sy.dma("s", cTt[:], cT, w=["cTt"])
        sy.dma("s", bT[:], badaT, w=["bT"])
        sy.dma("s", gmT[:], gmixT, w=["gmT"])
        sy.dma("s", gfT[:], gffnT, w=["gfT"])
        sy.dma("s", brow[:], badaR, w=["brow"])
        A(lambda e: e.activation(out=scb[:], in_=cTt[:], func=AF.Silu), r=["cTt"], w=["scb"])
        import os
        P0STOP = int(os.environ.get("P0STOP", "99"))
        for blk in range(24 if P0STOP > 2 else (1 if P0STOP > 0 else 0)):
            wb = wab[blk % 2]; wk = ("wab", blk % 2)
            sy.dma("p", wb[:], wada[blk], w=[wk])
            ps, pk = pp.get()
            for cc in range(4):
                for k in range(16):
                    T(lambda e, cc=cc, k=k: e.matmul(ps[:, cc * 2:cc * 2 + 2], lhsT=wb[:, k, cc * 128:(cc + 1) * 128],
                                                     rhs=scb[:, k, :], start=(k == 0), stop=(k == 15)),
                      r=[wk, "scb"], w=[pk])
            if P0STOP == 1:
                break
            V(lambda e, blk=blk: e.tensor_tensor(
                out=modT[:, blk * 4:(blk + 1) * 4, :],
                in0=ps[:, 0:8].rearrange("p (c t) -> p c t", t=2),
                in1=bT[:, blk * 4:(blk + 1) * 4].rearrange("p (c o) -> p c o", o=1).to_broadcast([128, 4, 2]),
                op=ALU.add), r=[pk, "bT"], w=["modT"])
            if blk in (8, 9, 10, 11, 20, 21, 22, 23):
                ro = (blk - 8) * 512 if blk < 12 else 2048 + (blk - 20) * 512
                ps2, pk2 = pp.get()
                for k in range(16):
                    T(lambda e, k=k: e.matmul(ps2[0:2, :], lhsT=scb[:, k, :], rhs=wb[:, k, :],
                                              start=(k == 0), stop=(k == 15)), r=[wk, "scb"], w=[pk2])
                V(lambda e, ro=ro: e.tensor_tensor(out=gtrow[:, ro:ro + 512], in0=ps2[0:2, :], in1=brow[:, ro:ro + 512],
                                                   op=ALU.add), r=[pk2, "brow"], w=["gtrow"])
        def modview(j):
            return modT[:, j * 16:(j + 1) * 16, :].rearrange("p k t -> p t k")
        for (gd, gsrc, jsc, sd, jsh, nm) in ((g1, gmT, 1, shm, 0, "m"), (g2, gfT, 4, shf, 3, "f")):
            for t in range(2):
                V(lambda e, gd=gd, jsc=jsc, t=t: e.tensor_scalar(out=gd[:, t, :], in0=modT[:, jsc * 16:(jsc + 1) * 16, t],
                                                                 scalar1=1.0, scalar2=None, op0=ALU.add),
                  r=["modT"], w=["g" + nm])
                V(lambda e, gd=gd, gsrc=gsrc, t=t: e.tensor_tensor(out=gd[:, t, :], in0=gd[:, t, :], in1=gsrc[:], op=ALU.mult),
                  r=["g" + nm, "gmT", "gfT"], w=["g" + nm])
                V(lambda e, sd=sd, jsh=jsh, t=t: e.tensor_copy(out=sd[:, t, :], in_=modT[:, jsh * 16:(jsh + 1) * 16, t]),
                  r=["modT"], w=["sh" + nm])
        sy.dma("s", GTd, gtrow[:], r=["gtrow"], w=["GTd"])
        sy.barrier()
        stack_holder[0].close()
        stack_holder[0] = None

    if "p1" in phases:
        stack_holder[0] = contextlib.ExitStack()
        flg = sb("flg", [128, 2 * NT1])
        gkb = sb("gkb", [128, 1024]); gglab = sb("gglab", [128, 1024])
        sy.dma("s", flg[:], flagt, w=["flg"])
        sy.dma("s", gkb[:], gk_b, w=["gkb"])
        sy.dma("s", gglab[:], ggla_b, w=["gglab"])
        w1b = sb("w1b", [128, 16, W1C], BF16)
        for k in range(16):
            sy.dma("p", w1b[:, k, :], w1[:, k, :], w=["w1b"])
        wg2f = sb("wg2f", [17, 512]); wg2b = sb("wg2b", [17, 512], BF16)
        sy.dma("s", wg2f[:], wg2, w=["wg2f"])
        V(lambda e: e.tensor_copy(out=wg2b[:], in_=wg2f[:]), r=["wg2f"], w=["wg2b"])
        rKs = [sb(f"rK{i}", [128, 32]) for i in range(2)]; rIs = [sb(f"rI{i}", [128, 16]) for i in range(2)]
        xt = [sb(f"xt{i}", [128, D]) for i in range(1)]
        hT = sb("hT", [128, 16, 128], BF16)
        kf = sb("kf", [128, 1024]); sq = sb("sq", [128, 1024]); vf = sq
        kb = sb("kb", [128, 1024], BF16)
        ktb = sb("ktb", [128, 8, 128], BF16)
        vb = sb("vb", [128, 8, 130], BF16)
        ikf = sb("ikf", [128, 64]); ikb = sb("ikb", [128, 64], BF16); iktb = sb("iktb", [64, 128], BF16)
        gkfs = [sb(f"gkf{i}", [128, 512]) for i in range(2)]; gqfs = [sb(f"gqf{i}", [128, 512]) for i in range(2)]
        gvbs = [sb(f"gvb{i}", [128, 1024], BF16) for i in range(2)]
        glras = [sb(f"glra{i}", [17, 128], BF16) for i in range(2)]
        lg = sb("lg", [128, 512]); ee = sb("ee", [128, 512])
        kpb = sb("kpb", [128, 512], BF16)
        qtl = sb("qtl", [128, 512], BF16); ktl = sb("ktl", [128, 512], BF16)
        qkT = sb("qkT", [128, 8, 128], BF16)
        ATb = sb("ATb", [128, 4, 128], BF16)
        S = sb("S", [128, 1024]); Sb = kb
        dec = sb("dec", [128, 4])
        ogf = sb("ogf", [128, 1024])
        st8 = sb("st8", [128, 16]); rt8 = sb("rt8", [128, 16])
        rtmp = sb("rtmp", [128, 4, 8, 16])
        V(lambda e: e.memset(vb[:], 1.0), w=["vb"])
        for i_ in range(2):
            V(lambda e, i_=i_: e.memset(glras[i_][:], 1.0), w=[("glra", i_)])
        V(lambda e: e.memset(S[:], 0.0), w=["S"])

        def ln_transpose(xtile, xkey, which, hdst, hkey, gsc, gsh, gkeys):
            V(lambda e: e.scalar_tensor_tensor(out=sq[:].bitcast(BF16), in0=xtile, scalar=1.0, in1=xtile, op0=ALU.mult,
                                               op1=ALU.mult, accum_out=st8[:, 0:1]), r=[xkey], w=["sq", "st8"])
            rsqrt_small(rt8[:, 0:1], st8[:, 0:1], 1, 1.0 / D, "st8", "rt8")
            A(lambda e: e.activation(out=xtile, in_=xtile, func=AF.Copy, scale=rt8[:, 0:1]),
              r=[xkey, "rt8"], w=[xkey])
            for half in range(4):
                ps, pk = pp.get()
                for kk in range(4):
                    k = half * 4 + kk
                    T(lambda e, k=k, kk=kk: e.transpose(ps[:, kk * 128:(kk + 1) * 128], xtile[:, k * 128:(k + 1) * 128], IDF),
                      r=[xkey, "cm"], w=[pk])
                for kk in range(4):
                    k = half * 4 + kk
                    if isinstance(which, int):
                        A(lambda e, k=k, kk=kk: e.activation(out=hdst[:, k, :], in_=ps[:, kk * 128:(kk + 1) * 128],
                                                             func=AF.Identity, scale=gsc[:, which, k:k + 1],
                                                             bias=gsh[:, which, k:k + 1]),
                          r=[pk] + gkeys, w=[hkey])
                    else:
                        for (c0, c1, wh) in ((0, 64, 1), (64, 128, 0)):
                            A(lambda e, k=k, kk=kk, c0=c0, c1=c1, wh=wh: e.activation(
                                out=hdst[:, k, c0:c1], in_=ps[:, kk * 128 + c0:kk * 128 + c1], func=AF.Identity,
                                scale=gsc[:, wh, k:k + 1], bias=gsh[:, wh, k:k + 1]), r=[pk] + gkeys, w=[hkey])

        def proj(hsrc, hkey, wtile, wkey, c0, ncols, evac):
            ps, pk = pp.get()
            for k in range(16):
                T(lambda e, k=k: e.matmul(ps[:, 0:ncols], lhsT=hsrc[:, k, :], rhs=wtile[:, k, c0:c0 + ncols],
                                          start=(k == 0), stop=(k == 15)), r=[hkey, wkey], w=[pk])
            evac(ps, pk)

        def headnorm(src, skey, nh, hd, gb, gbkey, dst_keys):
            V(lambda e: e.tensor_tensor(out=sq[:, 0:nh * hd], in0=src, in1=src, op=ALU.mult), r=[skey], w=["sq"])
            V(lambda e: e.tensor_reduce(out=st8[:, 0:nh], in_=sq[:, 0:nh * hd].rearrange("p (h d) -> p h d", h=nh),
                                        axis=AX.X, op=ALU.add), r=["sq"], w=["st8"])
            rsqrt_small(rt8[:, 0:nh], st8[:, 0:nh], nh, 1.0 / hd, "st8", "rt8")
            V(lambda e: e.tensor_tensor(out=src.rearrange("p (h d) -> p h d", h=nh),
                                        in0=src.rearrange("p (h d) -> p h d", h=nh),
                                        in1=rt8[:, 0:nh].rearrange("p (h o) -> p h o", o=1).to_broadcast([128, nh, hd]),
                                        op=ALU.mult), r=[skey, "rt8"], w=[skey])
            V(lambda e: e.tensor_tensor(out=src, in0=src, in1=gb, op=ALU.mult), r=[skey, gbkey], w=[skey])

        def rope(src, skey, nh, hd, half, tab, tkey):
            v3 = src.rearrange("p (h d) -> p h d", h=nh)
            x1 = v3[:, :, 0:half]; x2 = v3[:, :, half:2 * half]
            cb = tab[:, 0:half].rearrange("p (o d) -> p o d", o=1).to_broadcast([128, nh, half])
            sn = tab[:, half:2 * half].rearrange("p (o d) -> p o d", o=1).to_broadcast([128, nh, half])
            t = [rtmp[:, i, 0:nh, 0:half] for i in range(4)]
            V(lambda e: e.tensor_tensor(out=t[0], in0=x1, in1=cb, op=ALU.mult), r=[skey, tkey], w=["rtmp"])
            V(lambda e: e.tensor_tensor(out=t[1], in0=x2, in1=sn, op=ALU.mult), r=[skey, tkey], w=["rtmp"])
            V(lambda e: e.tensor_tensor(out=t[2], in0=x2, in1=cb, op=ALU.mult), r=[skey, tkey], w=["rtmp"])
            V(lambda e: e.tensor_tensor(out=t[3], in0=x1, in1=sn, op=ALU.mult), r=[skey, tkey], w=["rtmp"])
            V(lambda e: e.tensor_tensor(out=x1, in0=t[0], in1=t[1], op=ALU.subtract), r=["rtmp"], w=[skey])
            V(lambda e: e.tensor_tensor(out=x2, in0=t[2], in1=t[3], op=ALU.add), r=["rtmp"], w=[skey])

        def k_store(kt):
            A(lambda e: e.copy(out=kb[:], in_=kf[:]), r=["kf"], w=["kb"])
            ps, pk = pp.get()
            pv = ps[:].bitcast(BF16) if hasattr(ps[:], "bitcast") else None
            for h in range(8):
                T(lambda e, h=h: e.transpose(pv[:, h * 128:(h + 1) * 128], kb[:, h * 128:(h + 1) * 128], cmb[:]),
                  r=["kb", "cmb"], w=[pk])
            A(lambda e: e.copy(out=ktb[:].rearrange("p h t -> p (h t)"), in_=pv[:, 0:1024]), r=[pk], w=["ktb"])
            sy.dma("s", KTs[kt], ktb[:], r=["ktb"], w=[("KT", kt)])

        def ik_store(kt):
            P(lambda e: e.tensor_copy(out=ikb[:], in_=ikf[:]), r=["ikf"], w=["ikb"])
            ps, pk = pp.get()
            pv = ps[:].bitcast(BF16)
            T(lambda e: e.transpose(pv[0:64, 0:128], ikb[:], cmb[:]), r=["ikb", "cmb"], w=[pk])
            A(lambda e: e.copy(out=iktb[:], in_=pv[0:64, 0:128]), r=[pk], w=["iktb"])
            sy.dma("s", IKTs[:, kt * 128:(kt + 1) * 128], iktb[:], r=["iktb"], w=[("IKT", kt)])

        def v_store(kt):
            sy.dma("s", Vs[kt * 128:(kt + 1) * 128, :], vb[:].rearrange("p h d -> p (h d)"), r=["vb"], w=[("V", kt)])

        import os
        P1STOP = int(os.environ.get("P1STOP", "99"))

        def load_x(xsrc, nrows, tcol, slot):
            sy.dma("p", rKs[slot][:], ropeK[:, tcol, :], w=[("rK", slot)])
            sy.dma("p", rIs[slot][:], ropeI[:, tcol, :], w=[("rI", slot)])
            if nrows < 128:
                V(lambda e: e.memset(xt[0][:], 0.0), w=[("xt", 0)])
            sy.dma("p", xt[0][0:nrows, :], xsrc, w=[("xt", 0)])

        def kv_tile(xsrc, nrows, which, tcol, kt, own_j, og_dst, outs, preloaded=False, prefetch=None):
            slot = kv_tile.n % 2
            kv_tile.n += 1
            xtile = xt[0]; xkey = ("xt", 0)
            rKt = rKs[slot]; rIt = rIs[slot]
            gkf = gkfs[slot]; gqf = gqfs[slot]; gvb = gvbs[slot]; glra = glras[slot]
            gkk = ("gkf", slot); gqk = ("gqf", slot); gvk = ("gvb", slot); glk = ("glra", slot)
            if not preloaded:
                load_x(xsrc, nrows, tcol, slot)
            ln_transpose(xtile[:], xkey, which, hT, "hT", g1, shm, ["gm"])
            if prefetch is not None:
                xsrc_n, tcol_n = prefetch
                load_x(xsrc_n, 128, tcol_n, 1 - slot)
            if P1STOP <= 2:
                return
            f01 = flg[:, tcol:tcol + 1]; fn16 = flg[:, NT1 + tcol:NT1 + tcol + 1]
            for b in range(2):
                proj(hT, "hT", w1b, "w1b", C_AK + b * 512, 512,
                     lambda ps, pk, b=b: V(lambda e: e.tensor_copy(out=kf[:, b * 512:(b + 1) * 512], in_=ps[:, :]),
                                           r=[pk], w=["kf"]))
            yield
            for b in range(2):
                def ev(ps, pk, b=b):
                    if outs is not None:
                        A(lambda e: e.copy(out=vf[:, b * 512:(b + 1) * 512], in_=ps[:, :]), r=[pk], w=["sq"])
                        P(lambda e: e.tensor_copy(out=vb[:, b * 4:(b + 1) * 4, 0:128],
                                                  in_=vf[:, b * 512:(b + 1) * 512].rearrange("p (h d) -> p h d", h=4)),
                          r=["sq"], w=["vb"])
                    else:
                        A(lambda e: e.copy(out=vb[:, b * 4:(b + 1) * 4, 0:128], in_=ps[:, :].rearrange("p (h d) -> p h d", h=4)),
                          r=[pk], w=["vb"])
                proj(hT, "hT", w1b, "w1b", C_AV + b * 512, 512, ev)
            if outs is not None:
                sy.dma("s", outs["v"], vf[0:nrows, :], r=["sq"], w=["o_v"], is_output=True)
            yield
            proj(hT, "hT", w1b, "w1b", C_IK, 64,
                 lambda ps, pk: V(lambda e: e.tensor_copy(out=ikf[:], in_=ps[:, 0:64]), r=[pk], w=["ikf"]))
            proj(hT, "hT", w1b, "w1b", C_GK, 512,
                 lambda ps, pk: V(lambda e: e.tensor_copy(out=gkf[:], in_=ps[:, :]), r=[pk], w=[gkk]))
            yield
            for b in range(2):
                proj(hT, "hT", w1b, "w1b", C_GV + b * 512, 512,
                     lambda ps, pk, b=b: A(lambda e: e.activation(out=gvb[:, b * 512:(b + 1) * 512], in_=ps[:, :],
                                                                  func=AF.Copy, scale=f01), r=[pk, "flg"], w=[gvk]))
            yield
            ps, pk = pp.get()
            for k in range(16):
                T(lambda e, k=k: e.matmul(ps[0:16, 0:128], lhsT=w1b[:, k, C_GLR:C_GLR + 16], rhs=hT[:, k, :],
                                          start=(k == 0), stop=(k == 15)), r=["hT", "w1b"], w=[pk])
            V(lambda e: e.tensor_copy(out=glra[0:16, :], in_=ps[0:16, 0:128]), r=[pk], w=[glk])
            if og_dst is not None:
                proj(hT, "hT", w1b, "w1b", C_GQ, 512,
                     lambda ps, pk: V(lambda e: e.tensor_copy(out=gqf[:], in_=ps[:, :]), r=[pk], w=[gqk]))
            yield
            headnorm(kf[:], "kf", 8, 128, gkb[:], "gkb", None)
            rope(kf[:], "kf", 8, 128, 16, rKt[:], ("rK", slot))
            if outs is not None:
                sy.dma("s", outs["k"], kf[0:nrows, :], r=["kf"], w=["o_k"], is_output=True)
            yield
            k_store(kt)
            v_store(kt)
            yield
            rope(ikf[:], "ikf", 1, 64, 8, rIt[:], ("rI", slot))
            if outs is not None:
                sy.dma("s", outs["ik"], ikf[0:nrows, :], r=["ikf"], w=["o_ik"], is_output=True)
            ik_store(kt)
            yield "S2"
            ps, pk = pp.get()
            T(lambda e: e.matmul(ps[:, :], lhsT=glra[:], rhs=wg2b[:], start=True, stop=True), r=[glk, "wg2b"], w=[pk])
            A(lambda e: e.activation(out=ee[:], in_=ps[:, :], func=AF.Exp, scale=-1.0), r=[pk], w=["ee"])
            A(lambda e: e.activation(out=ee[:], in_=ee[:], func=AF.Ln, bias=1.0), r=["ee"], w=["ee"])
            V(lambda e: e.tensor_scalar(out=lg[:], in0=ee[:], scalar1=fn16, scalar2=None, op0=ALU.mult),
              r=["ee", "flg"], w=["lg"])
            if P1STOP <= 7:
                return
            yield
            need_out = og_dst is not None
            if need_out:
                ps, pk = pp.get()
                T(lambda e: e.matmul(ps[:, :], lhsT=cm[:, 1, :], rhs=lg[:], start=True, stop=True), r=["cm", "lg"], w=[pk])
                A(lambda e: e.activation(out=ee[:], in_=ps[:, :], func=AF.Exp), r=[pk], w=["ee"])
                V(lambda e: e.scalar_tensor_tensor(out=qtl[:], in0=gqf[:], scalar=128.0 ** -0.5, in1=ee[:], op0=ALU.mult,
                                                   op1=ALU.mult), r=[gqk, "ee"], w=["qtl"])
                A(lambda e: e.activation(out=ee[:], in_=ps[:, :], func=AF.Exp, scale=-1.0), r=[pk, "qtl"], w=["ee"])
                V(lambda e: e.tensor_tensor(out=ktl[:], in0=gkf[:], in1=ee[:], op=ALU.mult), r=[gkk, "ee"], w=["ktl"])
                yield
                ps, pk = pp.get()
                pv = ps[:].bitcast(BF16)
                for h in range(4):
                    T(lambda e, h=h: e.transpose(pv[:, h * 128:(h + 1) * 128], qtl[:, h * 128:(h + 1) * 128], cmb[:]),
                      r=["qtl", "cmb"], w=[pk])
                    T(lambda e, h=h: e.transpose(pv[:, (4 + h) * 128:(5 + h) * 128], ktl[:, h * 128:(h + 1) * 128], cmb[:]),
                      r=["ktl", "cmb"], w=[pk])
                A(lambda e: e.copy(out=qkT[:].rearrange("p h t -> p (h t)"), in_=pv[:, 0:1024]), r=[pk], w=["qkT"])
                ps, pk = pp.get()
                for h in range(4):
                    T(lambda e, h=h: e.matmul(ps[:, h * 128:(h + 1) * 128], lhsT=qkT[:, 4 + h, :], rhs=qkT[:, h, :],
                                              start=True, stop=True), r=["qkT"], w=[pk])
                V(lambda e: e.tensor_tensor(out=ATb[:], in0=ps[:, :].rearrange("p (h t) -> p h t", h=4),
                                            in1=cm[:, 1, :].rearrange("p (o t) -> p o t", o=1).to_broadcast([128, 4, 128]),
                                            op=ALU.mult), r=[pk, "cm"], w=["ATb"])
                yield
                P(lambda e: e.tensor_copy(out=Sb[:], in_=S[:]), r=["S"], w=["kb"])
                pso = [pp.get(), pp.get()]
                for h in range(4):
                    po, pok = pso[h // 2]
                    oc = (h % 2) * 256
                    T(lambda e, h=h, po=po, oc=oc: e.matmul(po[:, oc:oc + 256], lhsT=qkT[:, h, :], rhs=Sb[:, h * 256:(h + 1) * 256],
                                                            start=True, stop=False), r=["qkT", "kb"], w=[pok])
                    T(lambda e, h=h, po=po, oc=oc: e.matmul(po[:, oc:oc + 256], lhsT=ATb[:, h, :], rhs=gvb[:, h * 256:(h + 1) * 256],
                                                            start=False, stop=True), r=["ATb", gvk], w=[pok])
                for i2 in range(2):
                    po, pok = pso[i2]
                    V(lambda e, po=po, i2=i2: e.tensor_copy(out=ogf[:, i2 * 512:(i2 + 1) * 512], in_=po[:, :]), r=[pok], w=["ogf"])
                headnorm(ogf[:], "ogf", 4, 256, gglab[:], "gglab", None)
                lo, hi, drow = og_dst
                sy.dma("s", OGs[drow:drow + (hi - lo), :], ogf[lo:hi, :], r=["ogf"], w=[("OG", drow)])
            if P1STOP <= 8:
                return
            yield
            ps, pk = pp.get()
            T(lambda e: e.matmul(ps[:, :], lhsT=cm[:, 2, :], rhs=lg[:], start=True, stop=True), r=["cm", "lg"], w=[pk])
            A(lambda e: e.activation(out=ee[:], in_=ps[:, :], func=AF.Exp), r=[pk, "ktl"], w=["ee"])
            V(lambda e: e.tensor_tensor(out=kpb[:], in0=gkf[:], in1=ee[:], op=ALU.mult), r=[gkk, "ee"], w=["kpb"])
            ps, pk = pp.get()
            for h in range(4):
                T(lambda e, h=h: e.matmul(ps[:, h:h + 1], lhsT=lg[:, h * 128:(h + 1) * 128], rhs=cm[:, 3, 0:1],
                                          start=True, stop=True), r=["lg", "cm"], w=[pk])
            A(lambda e: e.activation(out=dec[:], in_=ps[:, 0:4], func=AF.Exp), r=[pk], w=["dec"])
            yield
            psu = [pp.get(), pp.get()]
            for h in range(4):
                pu, puk = psu[h // 2]
                oc = (h % 2) * 256
                T(lambda e, h=h, pu=pu, oc=oc: e.matmul(pu[:, oc:oc + 256], lhsT=kpb[:, h * 128:(h + 1) * 128],
                                                        rhs=gvb[:, h * 256:(h + 1) * 256], start=True, stop=True),
                  r=["kpb", gvk], w=[puk])
            for h in range(4):
                pu, puk = psu[h // 2]
                oc = (h % 2) * 256
                V(lambda e, h=h, pu=pu, oc=oc: e.scalar_tensor_tensor(
                    out=S[:, h * 256:(h + 1) * 256], in0=S[:, h * 256:(h + 1) * 256], scalar=dec[:, h:h + 1],
                    in1=pu[:, oc:oc + 256], op0=ALU.mult, op1=ALU.add), r=["S", "dec", puk], w=["S"])
        kv_tile.n = 0

        ntiles = build_program.ntiles_p1 if hasattr(build_program, "ntiles_p1") else NREL
        conv_jobs = [(w2b[2 * i:2 * i + 2], w2_d[2 * i:2 * i + 2]) for i in range(6)] + [(wiwb, wiw_d)]
        conv_jobs += [(woutb[2 * i:2 * i + 2], wout_d[2 * i:2 * i + 2]) for i in range(4)]
        conv_jobs += [(wupb[2 * i:2 * i + 2], wup_d[2 * i:2 * i + 2]) for i in range(22)]
        conv_jobs += [(wdnb[nb, 11 * i:11 * i + 11], wdn_d[nb, 11 * i:11 * i + 11]) for nb in range(4) for i in range(4)]
        if "pq" not in phases:
            conv_jobs = []
        prev_g = None
        for r in range(ntiles):
            if conv_jobs:
                dst_, src_ = conv_jobs.pop(0)
                sy.dma("p", dst_, src_, w=["wconv"])
            own_j = r // 8 if r % 8 == 7 else None
            halo_j = r // 8 if r % 8 == 6 else None
            outs = None
            og = None
            if own_j is not None:
                outs = {"k": o_k[own_j * 128:(own_j + 1) * 128, :], "v": o_v[own_j * 128:(own_j + 1) * 128, :],
                        "ik": o_ik[own_j * 128:(own_j + 1) * 128, :]}
                og = (0, 128, (1 + own_j) * 128)
            if halo_j is not None:
                og = (126, 128, 64 + 2 * halo_j)
            if os.environ.get("P1OUTS", "1") == "0":
                outs = None
            if os.environ.get("P1OG", "1") == "0" and own_j is not None:
                og = None
            pf = (xp[(r + 1) * 128:(r + 2) * 128, :], r + 1) if r + 1 < ntiles else None
            g_ = kv_tile(xp[r * 128:(r + 1) * 128, :], 128, 0, r, r if r < 128 else NKT - 1, own_j, og, outs,
                         preloaded=(r > 0), prefetch=pf)
            dn = False; do = prev_g is None
            while not (dn and do):
                if not dn:
                    dn = next(g_, "END") in ("S2", "END")
                if not do:
                    do = next(prev_g, "END") == "END"
            prev_g = g_
        if prev_g is not None:
            for _ in prev_g:
                pass
        while conv_jobs:
            dst_, src_ = conv_jobs.pop(0)
            sy.dma("p", dst_, src_, w=["wconv"])
        sy.dma("s", o_glap, S[:], r=["S"], w=["o_glap"], is_output=True)
        P1POST = int(os.environ.get("P1POST", "1"))
        for i in range(8 if P1POST else 0):
            ktf = xt[0]; xkey = ("xt", 0)
            sy.dma("s", ktf[:, 0:1024].rearrange("p (h t) -> p h t", h=8), ckT[i], w=[xkey])
            P(lambda e, ktf=ktf: e.tensor_copy(out=ktb[:].rearrange("p h t -> p (h t)"), in_=ktf[:, 0:1024]), r=[xkey], w=["ktb"])
            sy.dma("s", KTs[KT_S0 + i], ktb[:], r=["ktb"], w=[("KT", KT_S0 + i)])
            sy.dma("s", ktf[:, 1024:2048], cv[i * 128:(i + 1) * 128, :], w=[xkey])
            A(lambda e, ktf=ktf: e.copy(out=vb[:, :, 0:128], in_=ktf[:, 1024:2048].rearrange("p (h d) -> p h d", h=8)),
              r=[xkey], w=["vb"])
            v_store(KT_S0 + i)
        ikc = xt[0][0:64, 0:1024]; ikcb = kb[0:64, :]
        if P1POST:
            sy.dma("s", ikc, cikT, w=[("xt", 0)])
            V(lambda e: e.tensor_copy(out=ikcb, in_=ikc), r=[("xt", 0)], w=["kb"])
            sy.dma("s", IKTs[:, KT_S0 * 128:(KT_S0 + 8) * 128], ikcb, r=["kb"], w=[("IKT", "c")])
            sy.dma("s", S[:], sgla, r=["o_glap"], w=["S"])
            outs = {"k": o_ks, "v": o_vs, "ik": o_iks}
            for _ in kv_tile(xs, 64, 1, NT1 - 1, KT_S0 + 8, None, (0, 64, 0), outs):
                pass
        sy.dma("s", o_glas, S[:], r=["S"], w=["o_glas"], is_output=True)
        sy.barrier()
        stack_holder[0].close()
        stack_holder[0] = None

    if "pq" in phases:
        import os
        stack_holder[0] = contextlib.ExitStack()
        pp.n = 5
        pp.i = 0
        OB = [(pp.banks[5 + i], ("ps", 5 + i)) for i in range(3)]
        NITER = int(os.environ.get("NITER", "26"))
        gqb = sb("gqb", [128, 1024])
        sy.dma("s", gqb[:], gq_b, w=["gqb"])
        gtc = [sb(f"gtc{i}", [128, 512]) for i in range(2)]
        uhalo = sb("uhalo", [128, 88, 32])
        convw = sb("convw", [128, 88, 4]); sconv = sb("sconv", [128, 88, 2]); hfl = sb("hfl", [128, 32])
        oconv = sb("oconv", [128, 88, 2])
        sy.dma("s", convw[:], convw_d, w=["convw"])
        sy.dma("s", sconv[:], sconv_d, w=["sconv"])
        sy.dma("s", hfl[:], hflag_d, w=["hfl"])
        xq = sb("xq", [128, D])
        hT = sb("hT", [128, 16, 128], BF16)
        wblk = [sb(f"wblk{i}", [128, 16, 256], BF16) for i in range(2)]
        wiw = sb("wiw", [128, 16, 16], BF16)
        qf = sb("qf", [128, 1024]); sq = sb("sq", [128, 1024]); ogt = sq
        QT = sb("QT", [128, 8, 128], BF16); iqT = sb("iqT", [128, 8, 128], BF16)
        iwf = sb("iwf", [128, 16])
        mixb = sb("mixb", [128, D], BF16); mixT = hT; qb = mixb[:, 0:1024]
        score = sb("score", [128, 16384]); xsq = score[:, 0:2048]
        hidT = score[:, 2048:4864].bitcast(BF16).rearrange("p (i t) -> p i t", i=44)
        IKc = [sb(f"IKc{i}", [128, 512], BF16) for i in range(2)]
        rl = [sb(f"rl{i}", [128, 512]) for i in range(2)]
        bch = [sb(f"bch{i}", [128, 512]) for i in range(2)]
        junk = qf[:].bitcast(BF16)
        bs = sb("bs", [128, 48]); cnts = sb("cnts", [128, 16]); hmx = sb("hmx", [128, 40]); lmx = sb("lmx", [128, 40])
        m01 = [sb(f"m01{i}", [128, 512], BF16) for i in range(2)]
        mT = [sb(f"mT{i}", [128, 512], BF16) for i in range(2)]
        KTc = [sb(f"KTc{i}", [128, 4, 8, 128], BF16) for i in range(2)]
        Vc = [sb(f"Vc{i}", [128, 4, 8 * 130], BF16) for i in range(2)]
        eT = [sb(f"eT{i}", [128, 512], BF16) for i in range(2)]
        PT = [sb(f"PT{i}", [128, 512], BF16) for i in range(2)]
        oa = sb("oa", [128, 8, 129]); rden = sb("rden", [128, 8])
        uexts = [sb(f"uext{i}", [128, 2, 130]) for i in range(2)]; tas = [sb(f"ta{i}", [128, 2, 128]) for i in range(2)]
        sas = [sb(f"sa{i}", [128, 128]) for i in range(2)]
        wdp = [sb(f"wdp{i}", [128, 2, 512], BF16) for i in range(3)]
        tmpy = rl[1]
        rKq = sb("rKq", [128, 32]); rIq = sb("rIq", [128, 16])
        st8 = sb("st8q", [128, 16]); rt8 = sb("rt8q", [128, 16])
        rtmp = sb("rtmpq", [128, 4, 16, 16])

        cntq = {"w": 0, "d": 0, "g": 0}

        def ln_transpose_q(which, gsc, gsh):
            V(lambda e: e.scalar_tensor_tensor(out=xsq, in0=xq[:], scalar=1.0, in1=xq[:], op0=ALU.mult,
                                               op1=ALU.mult, accum_out=st8[:, 0:1]), r=["xq"], w=["score", "st8"])
            rsqrt_small(rt8[:, 0:1], st8[:, 0:1], 1, 1.0 / D, "st8", "rt8")
            A(lambda e: e.activation(out=xsq, in_=xq[:], func=AF.Copy, scale=rt8[:, 0:1]), r=["xq", "rt8"], w=["score"])
            for half in range(4):
                ps, pk = pp.get()
                for kk in range(4):
                    k = half * 4 + kk
                    T(lambda e, k=k, kk=kk: e.transpose(ps[:, kk * 128:(kk + 1) * 128], xsq[:, k * 128:(k + 1) * 128], IDF),
                      r=["score"], w=[pk])
                for kk in range(4):
                    k = half * 4 + kk
                    A(lambda e, k=k, kk=kk: e.activation(out=hT[:, k, :], in_=ps[:, kk * 128:(kk + 1) * 128], func=AF.Identity,
                                                         scale=gsc[:, which, k:k + 1], bias=gsh[:, which, k:k + 1]),
                      r=[pk], w=["hT"])

        def wload(src):
            i = cntq["w"] % 2
            cntq["w"] += 1
            sy.dma("s", wblk[i][:], src, w=[("wblk", i)])
            return wblk[i], ("wblk", i)

        def projq(lhs, lkey, wt, wkey, c0, ncols, evac):
            ps, pk = pp.get()
            for k in range(16):
                T(lambda e, k=k: e.matmul(ps[:, 0:ncols], lhsT=lhs[:, k, :], rhs=wt[:, k, c0:c0 + ncols],
                                          start=(k == 0), stop=(k == 15)), r=[lkey, wkey], w=[pk])
            evac(ps, pk)

        def headnorm_q(src, skey, nh, hd, gb, gbkey):
            V(lambda e: e.tensor_tensor(out=sq[:, 0:nh * hd], in0=src, in1=src, op=ALU.mult), r=[skey], w=["sq"])
            V(lambda e: e.tensor_reduce(out=st8[:, 0:nh], in_=sq[:, 0:nh * hd].rearrange("p (h d) -> p h d", h=nh),
                                        axis=AX.X, op=ALU.add), r=["sq"], w=["st8"])
            rsqrt_small(rt8[:, 0:nh], st8[:, 0:nh], nh, 1.0 / hd, "st8", "rt8")
            V(lambda e: e.tensor_tensor(out=src.rearrange("p (h d) -> p h d", h=nh),
                                        in0=src.rearrange("p (h d) -> p h d", h=nh),
                                        in1=rt8[:, 0:nh].rearrange("p (h o) -> p h o", o=1).to_broadcast([128, nh, hd]),
                                        op=ALU.mult), r=[skey, "rt8"], w=[skey])
            V(lambda e: e.tensor_tensor(out=src, in0=src, in1=gb, op=ALU.mult), r=[skey, gbkey], w=[skey])

        def rope_q(src, skey, nh, hd, half, tab, tkey):
            v3 = src.rearrange("p (h d) -> p h d", h=nh)
            x1 = v3[:, :, 0:half]; x2 = v3[:, :, half:2 * half]
            cb = tab[:, 0:half].rearrange("p (o d) -> p o d", o=1).to_broadcast([128, nh, half])
            sn = tab[:, half:2 * half].rearrange("p (o d) -> p o d", o=1).to_broadcast([128, nh, half])
            t = [rtmp[:, i, 0:nh, 0:half] for i in range(4)]
            V(lambda e: e.tensor_tensor(out=t[0], in0=x1, in1=cb, op=ALU.mult), r=[skey, tkey], w=["rtmp"])
            V(lambda e: e.tensor_tensor(out=t[1], in0=x2, in1=sn, op=ALU.mult), r=[skey, tkey], w=["rtmp"])
            V(lambda e: e.tensor_tensor(out=t[2], in0=x2, in1=cb, op=ALU.mult), r=[skey, tkey], w=["rtmp"])
            V(lambda e: e.tensor_tensor(out=t[3], in0=x1, in1=sn, op=ALU.mult), r=[skey, tkey], w=["rtmp"])
            V(lambda e: e.tensor_tensor(out=x1, in0=t[0], in1=t[1], op=ALU.subtract), r=["rtmp"], w=[skey])
            V(lambda e: e.tensor_tensor(out=x2, in0=t[2], in1=t[3], op=ALU.add), r=["rtmp"], w=[skey])

        def q_tile(kind, j):
            which = 1 if kind == "sample" else 0
            if kind == "own":
                r0 = (8 * j + 7) * 128
                sy.dma("s", xq[:], xp[r0:r0 + 128, :], w=["xq"])
                sy.dma("s", rKq[:], ropeK[:, 8 * j + 7, :], w=["rKq"])
                sy.dma("s", rIq[:], ropeI[:, 8 * j + 7, :], w=["rIq"])
                nrows = 128
            elif kind == "sample":
                V(lambda e: e.memset(xq[:], 0.0), w=["xq"])
                sy.dma("s", xq[0:64, :], xs, w=["xq"])
                sy.dma("s", rKq[:], ropeK[:, NT1 - 1, :], w=["rKq"])
                sy.dma("s", rIq[:], ropeI[:, NT1 - 1, :], w=["rIq"])
                nrows = 64
            else:
                V(lambda e: e.memset(xq[:], 0.0), w=["xq"])
                for jj in range(16):
                    r0 = (8 * jj + 6) * 128 + 126
                    sy.dma("s", xq[2 * jj:2 * jj + 2, :], xp[r0:r0 + 2, :], w=["xq"])
                sy.dma("s", rKq[:], ropeKh_d, w=["rKq"])
                sy.dma("s", rIq[:], ropeIh_d, w=["rIq"])
                nrows = 32
            ln_transpose_q(which, g1, shm)
            for b in range(4):
                wt, wk = wload(w2b[b])
                projq(hT, "hT", wt, wk, 0, 256,
                      lambda ps, pk, b=b: A(lambda e: e.copy(out=qf[:, b * 256:(b + 1) * 256], in_=ps[:, 0:256]), r=[pk], w=["qf"]))
            headnorm_q(qf[:], "qf", 8, 128, gqb[:], "gqb")
            rope_q(qf[:], "qf", 8, 128, 16, rKq[:], "rKq")
            V(lambda e: e.tensor_copy(out=qb, in_=qf[:]), r=["qf"], w=["mixb"])
            ps, pk = pp.get()
            pv = ps[:].bitcast(BF16)
            for h in range(8):
                T(lambda e, h=h: e.transpose(pv[:, h * 128:(h + 1) * 128], qb[:, h * 128:(h + 1) * 128], cmb[:]), r=["mixb"], w=[pk])
            A(lambda e: e.copy(out=QT[:].rearrange("p h t -> p (h t)"), in_=pv[:, 0:1024]), r=[pk], w=["QT"])
            for b in range(4):
                wt, wk = wload(w2b[4 + b])
                projq(hT, "hT", wt, wk, 0, 256,
                      lambda ps, pk, b=b: A(lambda e: e.copy(out=qf[:, b * 256:(b + 1) * 256], in_=ps[:, 0:256]), r=[pk], w=["qf"]))
            rope_q(qf[:], "qf", 16, 64, 8, rIq[:], "rIq")
            V(lambda e: e.tensor_copy(out=qb, in_=qf[:]), r=["qf"], w=["mixb"])
            ps, pk = pp.get()
            pv = ps[:].bitcast(BF16)
            for h in range(8):
                T(lambda e, h=h: e.transpose(pv[:, h * 128:(h + 1) * 128], qb[:, h * 128:(h + 1) * 128], cmb[:]), r=["mixb"], w=[pk])
            A(lambda e: e.copy(out=iqT[:].rearrange("p h t -> p (h t)"), in_=pv[:, 0:1024]), r=[pk], w=["iqT"])
            sy.dma("s", wiw[:], wiwb, w=["wiw"])
            ps, pk = pp.get()
            for k in range(16):
                T(lambda e, k=k: e.matmul(ps[:, 0:16], lhsT=hT[:, k, :], rhs=wiw[:, k, :], start=(k == 0), stop=(k == 15)),
                  r=["hT", "wiw"], w=[pk])
            V(lambda e: e.tensor_copy(out=iwf[:], in_=ps[:, 0:16]), r=[pk], w=["iwf"])
            if nrows < 128:
                V(lambda e: e.memset(ogt[:], 0.0), w=["sq"])
            if kind == "own":
                sy.dma("s", ogt[:], OGs[(1 + j) * 128:(2 + j) * 128, :], w=["sq"])
            elif kind == "sample":
                sy.dma("s", ogt[0:64, :], OGs[0:64, :], w=["sq"])
            else:
                sy.dma("s", ogt[0:32, :], OGs[64:96, :], w=["sq"])
            for b in range(4):
                wt, wk = wload(w2b[8 + b])
                def evg(ps, pk, b=b):
                    A(lambda e: e.activation(out=qf[:, b * 256:(b + 1) * 256], in_=ps[:, 0:256], func=AF.Silu), r=[pk], w=["qf"])
                    V(lambda e: e.tensor_tensor(out=mixb[:, 1024 + b * 256:1024 + (b + 1) * 256], in0=qf[:, b * 256:(b + 1) * 256],
                                                in1=ogt[:, b * 256:(b + 1) * 256], op=ALU.mult), r=["qf", "sq"], w=["mixb"])
                projq(hT, "hT", wt, wk, 0, 256, evg)

            if kind == "own":
                nkt = 8 * j + 8
                chunks = [(4 * ci, 4) for ci in range(nkt // 4)]
            elif kind == "sample":
                chunks = [(KT_S0, 4), (KT_S0 + 4, 4), (KT_S0 + 8, 1)]
            else:
                chunks = [(4 * ci, 4) for ci in range(32)]
            nch = len(chunks)
            L = sum(n for _, n in chunks) * 128
            coff = [sum(n for _, n in chunks[:i]) * 128 for i in range(nch)]

            def bias_src(ci):
                if kind == "own":
                    last = (ci == nch - 1)
                    if ci == 0:
                        return biasO_d[0]
                    if ci == 1:
                        return biasO_d[2] if last else biasO_d[1]
                    return biasO_d[3] if last else None
                if kind == "sample":
                    return biasS_d if ci == nch - 1 else None
                return biasH_d[:, ci * 512:(ci + 1) * 512]

            for ci, (kt0, nk) in enumerate(chunks):
                nc_ = nk * 128
                ik = IKc[ci % 2]; ikk = ("IKc", ci % 2)
                sy.dma("s", ik[0:64, 0:nc_], IKTs[:, kt0 * 128:kt0 * 128 + nc_], w=[ikk])
                sy.dma("s", ik[64:128, 0:nc_], IKTs[:, kt0 * 128:kt0 * 128 + nc_], w=[ikk])
                bsrc = bias_src(ci)
                bt = bch[ci % 2]; bk = ("bch", ci % 2)
                if bsrc is not None:
                    sy.dma("s", bt[:, 0:nc_], bsrc if kind != "sample" else bsrc, w=[bk])
                sc = score[:, coff[ci]:coff[ci] + nc_]
                for h in range(16):
                    hp, lo = h // 2, (h % 2) * 64
                    ps, pk = pp.get()
                    T(lambda e, hp=hp, lo=lo, ps=ps: e.matmul(ps[:, 0:nc_], lhsT=iqT[lo:lo + 64, hp, :], rhs=ik[lo:lo + 64, 0:nc_],
                                                              start=True, stop=True), r=["iqT", ikk], w=[pk])
                    rt = rl[h % 2]; rk = ("rl", h % 2)
                    A(lambda e, ps=ps, rt=rt: e.activation(out=rt[:, 0:nc_], in_=ps[:, 0:nc_], func=AF.Relu), r=[pk], w=[rk])
                    if h == 0:
                        if bsrc is not None:
                            V(lambda e, rt=rt: e.scalar_tensor_tensor(out=sc, in0=rt[:, 0:nc_], scalar=iwf[:, 0:1], in1=bt[:, 0:nc_],
                                                                      op0=ALU.mult, op1=ALU.add), r=[rk, "iwf", bk], w=["score"])
                        else:
                            V(lambda e, rt=rt: e.tensor_scalar(out=sc, in0=rt[:, 0:nc_], scalar1=iwf[:, 0:1], scalar2=None, op0=ALU.mult),
                              r=[rk, "iwf"], w=["score"])
                    else:
                        V(lambda e, rt=rt, h=h: e.scalar_tensor_tensor(out=sc, in0=rt[:, 0:nc_], scalar=iwf[:, h:h + 1], in1=sc,
                                                                       op0=ALU.mult, op1=ALU.add), r=[rk, "iwf", "score"], w=["score"])
                V(lambda e, ci=ci: e.tensor_reduce(out=hmx[:, ci:ci + 1], in_=sc, axis=AX.X, op=ALU.max), r=["score"], w=["hmx"])
                rt = rl[0]; rk = ("rl", 0)
                V(lambda e, rt=rt: e.tensor_scalar(out=rt[:, 0:nc_], in0=sc, scalar1=-1.0e29, scalar2=-3.0e30, op0=ALU.is_lt, op1=ALU.mult),
                  r=["score"], w=[rk])
                V(lambda e, rt=rt: e.scalar_tensor_tensor(out=rt[:, 0:nc_], in0=sc, scalar=-1.0, in1=rt[:, 0:nc_], op0=ALU.mult, op1=ALU.add),
                  r=["score", rk], w=[rk])
                V(lambda e, rt=rt, ci=ci: e.tensor_reduce(out=lmx[:, ci:ci + 1], in_=rt[:, 0:nc_], axis=AX.X, op=ALU.max), r=[rk], w=["lmx"])
            V(lambda e: e.tensor_reduce(out=bs[:, 0:1], in_=hmx[:, 0:nch], axis=AX.X, op=ALU.max), r=["hmx"], w=["bs"])
            V(lambda e: e.tensor_reduce(out=bs[:, 1:2], in_=lmx[:, 0:nch], axis=AX.X, op=ALU.max), r=["lmx"], w=["bs"])
            V(lambda e: e.tensor_scalar(out=bs[:, 2:3], in0=bs[:, 1:2], scalar1=-1.0, scalar2=-1.0, op0=ALU.mult, op1=ALU.add),
              r=["bs"], w=["bs"])
            V(lambda e: e.tensor_tensor(out=bs[:, 3:4], in0=bs[:, 0:1], in1=bs[:, 2:3], op=ALU.subtract), r=["bs"], w=["bs"])
            nseg = (L + 2047) // 2048
            for it in range(NITER):
                fac = 0.5 ** (it + 1)
                V(lambda e, fac=fac: e.scalar_tensor_tensor(out=bs[:, 4:5], in0=bs[:, 3:4], scalar=fac, in1=bs[:, 2:3],
                                                            op0=ALU.mult, op1=ALU.add), r=["bs"], w=["bs"])
                for sg in range(nseg):
                    c0 = sg * 2048; c1 = min(L, c0 + 2048)
                    V(lambda e, c0=c0, c1=c1, sg=sg: e.tensor_scalar(out=junk[:, 0:c1 - c0], in0=score[:, c0:c1], scalar1=bs[:, 4:5],
                                                                     scalar2=0.0, op0=ALU.is_gt, op1=ALU.add,
                                                                     accum_out=cnts[:, sg:sg + 1]), r=["score", "bs"], w=["qf", "cnts"])
                V(lambda e: e.tensor_reduce(out=bs[:, 5:6], in_=cnts[:, 0:nseg], axis=AX.X, op=ALU.add), r=["cnts"], w=["bs"])
                V(lambda e, fac=fac: e.tensor_scalar(out=bs[:, 6:7], in0=bs[:, 5:6], scalar1=255.5, scalar2=fac, op0=ALU.is_gt, op1=ALU.mult),
                  r=["bs"], w=["bs"])
                V(lambda e: e.scalar_tensor_tensor(out=bs[:, 2:3], in0=bs[:, 3:4], scalar=bs[:, 6:7], in1=bs[:, 2:3],
                                                   op0=ALU.mult, op1=ALU.add), r=["bs"], w=["bs"])
            nob = 0
            for ci, (kt0, nk) in enumerate(chunks):
                nc_ = nk * 128
                kc = KTc[ci % 2]; kck = ("KTc", ci % 2)
                vc = Vc[ci % 2]; vck = ("Vc", ci % 2)
                sy.dma("s", kc[:, 0:nk], KTs[kt0:kt0 + nk].rearrange("t p h s -> p t h s"), w=[kck])
                sy.dma("s", vc[:, 0:nk, :], Vs[kt0 * 128:(kt0 + nk) * 128, :].rearrange("(t p) c -> p t c", p=128), w=[vck])
                mm = m01[ci % 2]; mmk = ("m01", ci % 2)
                V(lambda e, mm=mm: e.tensor_scalar(out=mm[:, 0:nc_], in0=score[:, coff[ci]:coff[ci] + nc_], scalar1=bs[:, 2:3],
                                                   scalar2=None, op0=ALU.is_gt), r=["score", "bs"], w=[mmk])
                ps, pk = pp.get()
                pv = ps[:].bitcast(BF16)
                for t in range(nk):
                    T(lambda e, t=t, mm=mm: e.transpose(pv[:, t * 128:(t + 1) * 128], mm[:, t * 128:(t + 1) * 128], cmb[:]), r=[mmk], w=[pk])
                mt = mT[ci % 2]; mtk = ("mT", ci % 2)
                A(lambda e, mt=mt: e.copy(out=mt[:, 0:nc_], in_=pv[:, 0:nc_]), r=[pk], w=[mtk])
                for h in range(8):
                    ps, pk = pp.get()
                    for t in range(nk):
                        T(lambda e, t=t, h=h, ps=ps: e.matmul(ps[:, t * 128:(t + 1) * 128], lhsT=kc[:, t, h, :], rhs=QT[:, h, :],
                                                              start=True, stop=True), r=[kck, "QT"], w=[pk])
                    et = eT[h % 2]; ek = ("eT", h % 2)
                    A(lambda e, et=et, ps=ps: e.activation(out=et[:, 0:nc_], in_=ps[:, 0:nc_], func=AF.Exp, scale=128.0 ** -0.5),
                      r=[pk], w=[ek])
                    pt = PT[h % 2]; ptk = ("PT", h % 2)
                    V(lambda e, et=et, pt=pt, mt=mt: e.tensor_tensor(out=pt[:, 0:nc_], in0=et[:, 0:nc_], in1=mt[:, 0:nc_], op=ALU.mult),
                      r=[ek, mtk], w=[ptk])
                    ob, obk = OB[h // 3]
                    oc = (h % 3) * 129
                    for t in range(nk):
                        T(lambda e, t=t, h=h, pt=pt, ob=ob, oc=oc: e.matmul(
                            ob[:, oc:oc + 129], lhsT=pt[:, t * 128:(t + 1) * 128], rhs=vc[:, t, h * 130:h * 130 + 129],
                            start=(t == 0), stop=(t == nk - 1)), r=[ptk, vck], w=[obk])
                    if h in (2, 5, 7):
                        bi = h // 3
                        nh_ = 3 if bi < 2 else 2
                        oav = oa[:, bi * 3:bi * 3 + nh_, :].rearrange("p h d -> p (h d)")
                        if ci == 0:
                            V(lambda e, ob=ob, oav=oav, nh_=nh_: e.tensor_copy(out=oav, in_=ob[:, 0:nh_ * 129]), r=[obk], w=["oa"])
                        else:
                            V(lambda e, ob=ob, oav=oav, nh_=nh_: e.tensor_tensor(out=oav, in0=oav, in1=ob[:, 0:nh_ * 129], op=ALU.add),
                              r=[obk, "oa"], w=["oa"])
            V(lambda e: e.tensor_scalar(out=rden[:], in0=oa[:, :, 128:129].rearrange("p h o -> p (h o)"), scalar1=1.0e-30, scalar2=None, op0=ALU.max), r=["oa"], w=["rden"])
            V(lambda e: e.reciprocal(out=rden[:], in_=rden[:]), r=["rden"], w=["rden"])
            V(lambda e: e.tensor_tensor(out=mixb[:, 0:1024].rearrange("p (h d) -> p h d", h=8), in0=oa[:, :, 0:128],
                                        in1=rden[:].rearrange("p (h o) -> p h o", o=1).to_broadcast([128, 8, 128]), op=ALU.mult),
              r=["oa", "rden"], w=["mixb"])
            if DBG:
                V(lambda e: e.tensor_copy(out=xsq, in_=mixb[:]), r=["mixb"], w=["score"])
                sy.dma("s", dbg_mix, xsq, r=["score"], w=["dbg_mix"], is_output=True)
            for half in range(2):
                ps, pk = pp.get()
                pv = ps[:].bitcast(BF16)
                for kk in range(8):
                    k = half * 8 + kk
                    T(lambda e, k=k, kk=kk: e.transpose(pv[:, kk * 128:(kk + 1) * 128], mixb[:, k * 128:(k + 1) * 128], cmb[:]), r=["mixb"], w=[pk])
                A(lambda e, half=half: e.copy(out=mixT[:, half * 8:(half + 1) * 8, :].rearrange("p h t -> p (h t)"), in_=pv[:, 0:1024]),
                  r=[pk], w=["hT"])
            for b in range(8):
                wt, wk = wload(woutb[b])
                gi = cntq["g"] % 2
                cntq["g"] += 1
                sy.dma("s", gtc[gi][:, 0:256], GTd[which:which + 1, b * 256:(b + 1) * 256].to_broadcast([128, 256]), w=[("gtc", gi)])
                def evo(ps, pk, b=b, gi=gi):
                    V(lambda e: e.tensor_tensor(out=tmpy[:, 0:256], in0=ps[:, 0:256], in1=gtc[gi][:, 0:256], op=ALU.mult),
                      r=[pk, ("gtc", gi)], w=[("rl", 1)])
                    V(lambda e: e.tensor_tensor(out=xq[:, b * 256:(b + 1) * 256], in0=xq[:, b * 256:(b + 1) * 256], in1=tmpy[:, 0:256], op=ALU.add),
                      r=[("rl", 1), "xq"], w=["xq"])
                projq(mixT, "hT", wt, wk, 0, 256, evo)
            if DBG:
                sy.dma("s", dbg_xm, xq[:], r=["xq"], w=["dbg_xm"], is_output=True)
            ln_transpose_q(which, g2, shf)
            for i in range(44):
                wt, wk = wload(wupb[i])
                uext = uexts[i % 2]; ta = tas[i % 2]; sa = sas[i % 2]
                uk = ("uext", i % 2); tk = ("ta", i % 2); sk = ("sa", i % 2)
                ps, pk = pp.get()
                for ab in range(2):
                    for k in range(16):
                        T(lambda e, k=k, ab=ab, ps=ps: e.matmul(ps[:, ab * 128:(ab + 1) * 128], lhsT=wt[:, k, ab * 128:(ab + 1) * 128],
                                                                rhs=hT[:, k, :], start=(k == 0), stop=(k == 15)), r=["hT", wk], w=[pk])
                A(lambda e, ps=ps: e.copy(out=uext[:, :, 2:130], in_=ps[:, 0:256].rearrange("p (a t) -> p a t", a=2)), r=[pk], w=[uk])
                for ab in range(2):
                    ch = ab * 44 + i
                    if kind == "own":
                        V(lambda e, ab=ab, ch=ch: e.tensor_copy(out=uext[:, ab, 0:2], in_=uhalo[:, ch, 2 * j:2 * j + 2]), r=["uhalo"], w=[uk])
                    elif kind == "sample":
                        V(lambda e, ab=ab, ch=ch: e.tensor_copy(out=uext[:, ab, 0:2], in_=sconv[:, ch, :]), r=["sconv"], w=[uk])
                    else:
                        V(lambda e, ab=ab: e.memset(uext[:, ab, 0:2], 0.0), w=[uk])
                    if kind == "halo":
                        V(lambda e, ab=ab, ch=ch: e.tensor_tensor(out=uhalo[:, ch, :], in0=uext[:, ab, 2:34], in1=hfl[:], op=ALU.mult),
                          r=[uk, "hfl"], w=["uhalo"])
                    if kind == "sample":
                        V(lambda e, ab=ab, ch=ch: e.tensor_copy(out=oconv[:, ch, :], in_=uext[:, ab, 64:66]), r=[uk], w=["oconv"])
                    if kind == "own" and j == 15:
                        V(lambda e, ab=ab, ch=ch: e.tensor_copy(out=oconv[:, ch, :], in_=uext[:, ab, 128:130]), r=[uk], w=["oconv"])
                    A(lambda e, ab=ab, ch=ch: e.activation(out=ta[:, ab, :], in_=uext[:, ab, 2:130], func=AF.Identity,
                                                           scale=convw[:, ch, 2:3], bias=convw[:, ch, 3:4]), r=[uk, "convw"], w=[tk])
                    V(lambda e, ab=ab, ch=ch: e.scalar_tensor_tensor(out=ta[:, ab, :], in0=uext[:, ab, 1:129], scalar=convw[:, ch, 1:2],
                                                                     in1=ta[:, ab, :], op0=ALU.mult, op1=ALU.add), r=[uk, tk, "convw"], w=[tk])
                    V(lambda e, ab=ab, ch=ch: e.scalar_tensor_tensor(out=ta[:, ab, :], in0=uext[:, ab, 0:128], scalar=convw[:, ch, 0:1],
                                                                     in1=ta[:, ab, :], op0=ALU.mult, op1=ALU.add), r=[uk, tk, "convw"], w=[tk])
                A(lambda e: e.activation(out=sa[:], in_=ta[:, 0, :], func=AF.Silu), r=[tk], w=[sk])
                V(lambda e, i=i: e.tensor_tensor(out=hidT[:, i, :], in0=sa[:], in1=ta[:, 1, :], op=ALU.mult), r=[sk, tk], w=["score"])
            if kind == "sample":
                sy.dma("s", o_convs, oconv[:], r=["oconv"], w=["o_convs"], is_output=True)
            if kind == "own" and j == 15:
                sy.dma("s", o_convp, oconv[:], r=["oconv"], w=["o_convp"], is_output=True)
            if kind != "halo":
                for nb in range(4):
                    ps, pk = pp.get()
                    for i2 in range(22):
                        di = cntq["d"] % 3
                        cntq["d"] += 1
                        sy.dma("s", wdp[di][:], wdnb[nb, 2 * i2:2 * i2 + 2].rearrange("i p c -> p i c"), w=[("wdp", di)])
                        for ii in range(2):
                            i = 2 * i2 + ii
                            T(lambda e, i=i, ii=ii, di=di, ps=ps: e.matmul(ps[:, :], lhsT=hidT[:, i, :], rhs=wdp[di][:, ii, :], start=(i == 0), stop=(i == 43)),
                              r=["score", ("wdp", di)], w=[pk])
                    gi = cntq["g"] % 2
                    cntq["g"] += 1
                    sy.dma("s", gtc[gi][:], GTd[which:which + 1, 2048 + nb * 512:2048 + (nb + 1) * 512].to_broadcast([128, 512]), w=[("gtc", gi)])
                    V(lambda e, ps=ps, nb=nb, gi=gi: e.tensor_tensor(out=tmpy[:, 0:512], in0=ps[:, :], in1=gtc[gi][:], op=ALU.mult),
                      r=[pk, ("gtc", gi)], w=[("rl", 1)])
                    V(lambda e, nb=nb: e.tensor_tensor(out=xq[:, nb * 512:(nb + 1) * 512], in0=xq[:, nb * 512:(nb + 1) * 512], in1=tmpy[:, 0:512], op=ALU.add),
                      r=[("rl", 1), "xq"], w=["xq"])
                if kind == "own":
                    sy.dma("s", o_y[j * 128:(j + 1) * 128, :], xq[:], r=["xq"], w=["o_y"], is_output=True)
                else:
                    sy.dma("s", o_ys, xq[0:64, :], r=["xq"], w=["o_ys"], is_output=True)

        QT_LIST = os.environ.get("QTILES", "all")
        tl = [("sample", 0), ("halo", 0)] + [("own", j) for j in range(16)]
        if QT_LIST != "all":
            tl = [tl[int(x)] for x in QT_LIST.split(",")]
        for kind, j in tl:
            q_tile(kind, j)
        sy.barrier()
        stack_holder[0].close()
        stack_holder[0] = None

    sy.finish()
    return nc


def _rope_tab(pos, half, theta=500000.0):
    inv = (np.float32(theta) ** (-(np.arange(half, dtype=np.float32) / np.float32(half)))).astype(np.float32)
    ang = (pos.astype(np.float32)[:, None] * inv[None, :]).astype(np.float32)
    return np.concatenate([np.cos(ang.astype(np.float64)), np.sin(ang.astype(np.float64))], axis=1).astype(np.float32)


def _host_inputs(inp):
    f = np.float32
    x_prompt = np.asarray(inp["x_prompt"], f)[0]
    xpad = np.zeros(((128 + 14) * 128, D), f)
    xpad[7 * 128:(7 + 128) * 128] = x_prompt
    w_in = np.asarray(inp["w_in"], f)[0]
    offs = np.cumsum([0, 1024, 1024, 1024, 1024, 64, 16, 512, 512, 1024, 1024, 16])
    aq, ak, av, iq, ik, iw, gq, gk, gv, gr, glr = [w_in[:, offs[i]:offs[i + 1]] for i in range(11)]

    def kmaj(w):
        return np.ascontiguousarray(w.reshape(16, 128, -1).transpose(1, 0, 2))
    w1 = kmaj(np.concatenate([ak, av, gk, gv, gq, ik, glr], axis=1))
    w_ada = np.asarray(inp["w_ada"], f)[0]
    wada = np.ascontiguousarray(w_ada.reshape(16, 128, 24, 512).transpose(2, 1, 0, 3))
    b_ada = np.asarray(inp["b_ada"], f)[0]
    badaT = np.ascontiguousarray(b_ada.reshape(96, 128).T)
    brow = np.concatenate([b_ada[4096:6144], b_ada[10240:12288]])
    badaR = np.ascontiguousarray(np.stack([brow, brow]))
    cmat = np.zeros((128, 6, 128), f)
    ii = np.arange(128)
    cmat[:, 0, :] = np.eye(128)
    cmat[:, 1, :] = (ii[:, None] <= ii[None, :])
    cmat[:, 2, :] = (ii[:, None] > ii[None, :])
    cmat[:, 3, :] = 1.0
    cmat[0, 4, :] = 1.0
    cmat[0, 5, 64:] = 1.0
    cmat[1, 5, :64] = 1.0
    common = {
        "badaT": badaT, "badaR": badaR,
        "gmixT": np.ascontiguousarray(np.asarray(inp["g_mix"], f)[0].reshape(16, 128).T),
        "gffnT": np.ascontiguousarray(np.asarray(inp["g_ffn"], f)[0].reshape(16, 128).T),
        "wada": wada, "w1": w1,
        "gk_b": np.ascontiguousarray(np.broadcast_to(np.tile(np.asarray(inp["g_k"], f)[0], 8)[None, :], (128, 1024))),
        "gq_b": np.ascontiguousarray(np.broadcast_to(np.tile(np.asarray(inp["g_q"], f)[0], 8)[None, :], (128, 1024))),
        "ggla_b": np.ascontiguousarray(np.broadcast_to(np.tile(np.asarray(inp["g_gla"], f)[0], 4)[None, :], (128, 1024))),
        "wg2": np.ascontiguousarray(np.concatenate([np.asarray(inp["w_gate2"], f)[0], np.asarray(inp["b_gate2"], f)[0][None, :]], 0)),
        "cmat": cmat,
    }
    w_out = np.asarray(inp["w_out"], f)[0]
    w_up = np.asarray(inp["w_up"], f)[0]
    w_down = np.asarray(inp["w_down"], f)[0]
    k2 = kmaj(np.concatenate([aq, iq, gr], axis=1))
    common["w2"] = np.ascontiguousarray(k2.reshape(128, 16, 12, 256).transpose(2, 0, 1, 3))
    common["wiw"] = kmaj(iw)
    common["wout"] = np.ascontiguousarray(kmaj(w_out).reshape(128, 16, 8, 256).transpose(2, 0, 1, 3))
    ku = kmaj(w_up)
    common["wup"] = np.ascontiguousarray(np.concatenate(
        [ku[:, :, :DFF].reshape(128, 16, 44, 128), ku[:, :, DFF:].reshape(128, 16, 44, 128)], axis=3).transpose(2, 0, 1, 3))
    common["wdn"] = np.ascontiguousarray(w_down.reshape(44, 128, 4, 512).transpose(2, 0, 1, 3))
    cw = np.concatenate([np.asarray(inp["w_conv"], f)[0], np.asarray(inp["b_conv"], f)[0][None, :]], axis=0)
    common["convw"] = np.ascontiguousarray(cw.reshape(4, 88, 128).transpose(2, 1, 0))
    bS = np.zeros((128, 128), f); bS[:, 64:] = -BIG
    common["biasS"] = bS
    maps = []
    for c in range(NCORE):
        m = dict(common)
        sc_ = np.asarray(inp["state_ffn_conv"], f)[0, c]
        m["sconv"] = np.ascontiguousarray(sc_.reshape(2, 88, 128).transpose(2, 1, 0))
        npad = (7 - c) * 128
        bO = np.zeros((4, 128, 512), f)
        padrow = np.zeros(1024, f); padrow[:npad] = -BIG
        bO[0] = padrow[None, 0:512]; bO[1] = padrow[None, 512:1024]
        diag = np.zeros((128, 512), f); diag[0:64, 448:512] = -BIG
        bO[2] = bO[1] + diag; bO[3] = diag
        m["biasO"] = bO
        bH = np.full((128, 16384), -BIG, f)
        hfl = np.zeros((128, 32), f)
        posh = np.zeros(128, np.int64)
        for jj in range(16):
            a = 8 * jj + c - 1
            if a >= 0:
                for e_ in range(2):
                    bH[2 * jj + e_, npad:(8 * jj + 7) * 128] = 0.0
                    hfl[:, 2 * jj + e_] = 1.0
                    posh[2 * jj + e_] = a * 128 + 126 + e_
        m["biasH"] = bH
        m["hflag"] = hfl
        m["ropeKh"] = _rope_tab(posh, 16)
        m["ropeIh"] = _rope_tab(posh, 8)
        m["xp"] = xpad[c * 128:(c + NREL) * 128]
        m["xs"] = np.ascontiguousarray(np.asarray(inp["x_sample"], f)[c])
        cT = np.stack([np.asarray(inp["c_prompt"], f)[0], np.asarray(inp["c_sample"], f)[c]], axis=1)
        m["cT"] = np.ascontiguousarray(cT.reshape(16, 128, 2).transpose(1, 0, 2))
        flag = np.zeros((128, 2 * NT1), f)
        posK = np.zeros((NT1, 128), np.int64)
        for r in range(NREL):
            a = r - (7 - c)
            if 0 <= a < 128:
                flag[:, r] = 1.0
                posK[r] = a * 128 + np.arange(128)
        flag[:64, NT1 - 1] = 1.0
        posK[NT1 - 1, :64] = 1024 + np.arange(64)
        flag[:, NT1:] = -flag[:, :NT1] / 16.0
        m["flagt"] = flag
        m["ropeK"] = np.ascontiguousarray(_rope_tab(posK.reshape(-1), 16).reshape(NT1, 128, 32).transpose(1, 0, 2))
        m["ropeI"] = np.ascontiguousarray(_rope_tab(posK.reshape(-1), 8).reshape(NT1, 128, 16).transpose(1, 0, 2))
        ck = np.asarray(inp["cache_k"], f)[0, c]
        m["ckT"] = np.ascontiguousarray(ck.reshape(8, 128, 8, 128).transpose(0, 3, 2, 1))
        m["cv"] = np.ascontiguousarray(np.asarray(inp["cache_v"], f)[0, c].reshape(1024, 1024))
        m["cikT"] = np.ascontiguousarray(np.asarray(inp["cache_idx_k"], f)[0, c].T)
        m["sgla"] = np.ascontiguousarray(np.asarray(inp["state_gla"], f)[0, c].transpose(1, 0, 2).reshape(128, 1024))
        maps.append(m)
    return maps


def kernel(**inp):
    maps = _host_inputs(inp)
    nc = build_program()
    res = run_bass_kernel_spmd(nc, maps, core_ids=list(range(NCORE)))
    R = res.results
    f = np.float32
    kp = np.zeros((128, 128, 1024), f); vp = np.zeros((128, 128, 1024), f); ikp = np.zeros((128, 128, 64), f)
    for c in range(NCORE):
        for j in range(16):
            kp[8 * j + c] = R[c]["o_k"][j * 128:(j + 1) * 128]
            vp[8 * j + c] = R[c]["o_v"][j * 128:(j + 1) * 128]
            ikp[8 * j + c] = R[c]["o_ik"][j * 128:(j + 1) * 128]
    k_prompt = kp.reshape(1, 1, 16384, 8, 128)
    v_prompt = vp.reshape(1, 1, 16384, 8, 128)
    idx_k_prompt = ikp.reshape(1, 1, 16384, 64)
    gla_p = np.ascontiguousarray(R[0]["o_glap"].reshape(128, 4, 256).transpose(1, 0, 2)).reshape(1, 1, 4, 128, 256)
    k_sample = np.stack([R[c]["o_ks"] for c in range(NCORE)]).reshape(1, 8, 64, 8, 128)
    v_sample = np.stack([R[c]["o_vs"] for c in range(NCORE)]).reshape(1, 8, 64, 8, 128)
    idx_k_sample = np.stack([R[c]["o_iks"] for c in range(NCORE)]).reshape(1, 8, 64, 64)
    gla_s = np.stack([R[c]["o_glas"].reshape(128, 4, 256).transpose(1, 0, 2) for c in range(NCORE)]).reshape(1, 8, 4, 128, 256)
    yp = np.zeros((128, 128, D), f)
    for c in range(NCORE):
        for j in range(16):
            yp[8 * j + c] = R[c]["o_y"][j * 128:(j + 1) * 128]
    y_prompt = yp.reshape(1, 16384, D)
    y_sample = np.stack([R[c]["o_ys"] for c in range(NCORE)])
    ffn_conv_prompt = np.ascontiguousarray(R[7]["o_convp"].transpose(2, 1, 0)).reshape(1, 1, 2, 2 * DFF)
    ffn_conv_sample = np.stack([R[c]["o_convs"].transpose(2, 1, 0).reshape(2, 2 * DFF) for c in range(NCORE)]).reshape(1, 8, 2, 2 * DFF)
    return (y_prompt, y_sample, k_prompt, v_prompt, idx_k_prompt, np.ascontiguousarray(gla_p), ffn_conv_prompt,
            k_sample, v_sample, idx_k_sample, np.ascontiguousarray(gla_s), ffn_conv_sample)
```

```python
import numpy as np
import concourse.bass as bass
import concourse.mybir as mybir
from concourse.bass_utils import run_bass_kernel_spmd

F32 = mybir.dt.float32
BF16 = mybir.dt.bfloat16
AF = mybir.ActivationFunctionType
ALU = mybir.AluOpType
AX = mybir.AxisListType

D = 2048
NCORE = 8
NREL = 135
NT1 = NREL + 1
NKT = 140
KT_S0 = 128
EPS = 1e-6
DFF = 5632
W1C = 4176
C_AK, C_AV, C_GK, C_GV, C_GQ, C_IK, C_GLR = 0, 1024, 2048, 2560, 3584, 4096, 4160
BIG = 1.0e30


class Sync:
    def __init__(self, nc, ndma=(16, 4, 10)):
        self.nc = nc
        self.eng = {"v": nc.vector, "a": nc.scalar, "p": nc.gpsimd, "t": nc.tensor, "s": nc.sync}
        self.csem = {k: nc.alloc_semaphore("c_" + k) for k in ("v", "a", "p", "t")}
        self.ccount = {k: 0 for k in self.csem}
        self.dpool = {}
        for q, n in zip(("s", "a", "p"), ndma):
            self.dpool[q] = [[nc.alloc_semaphore(f"d_{q}{i}"), 0] for i in range(n)]
        self.dnext = {q: 0 for q in self.dpool}
        self.waited = {}
        self.lastw = {}
        self.readers = {}
        self.out_tokens = []
        self.nwait = 0

    def _wait(self, e, tok):
        sem, val, src = tok[:3]
        key = (e, id(sem))
        if self.waited.get(key, 0) >= val:
            return
        self.eng[e].wait_ge(sem, val)
        self.nwait += 1
        self.waited[key] = val

    def _deps(self, e, reads, writes):
        for k in reads:
            w = self.lastw.get(k)
            if w is not None:
                self._wait(e, w)
        for k in writes:
            for r in self.readers.get(k, ()):
                if r[2] != e or r[3]:
                    self._wait(e, r[:3])
            w = self.lastw.get(k)
            if w is not None and (w[2] != e or w[3]):
                self._wait(e, w[:3])

    def _commit(self, tok, reads, writes):
        for k in reads:
            self.readers.setdefault(k, []).append(tok)
        for k in writes:
            self.lastw[k] = tok
            self.readers[k] = []

    def op(self, e, fn, r=(), w=()):
        self._deps(e, r, w)
        inst = fn(self.eng[e])
        self.ccount[e] += 1
        inst.then_inc(self.csem[e], 1)
        tok = (self.csem[e], self.ccount[e], e, False)
        self._commit(tok, r, w)
        return tok

    def dma(self, q, out, in_, r=(), w=(), is_output=False, **kw):
        self._deps(q, r, w)
        pool = self.dpool[q]
        i = self.dnext[q]
        self.dnext[q] = (i + 1) % len(pool)
        sem, cnt = pool[i]
        if cnt:
            self._wait(q, (sem, 16 * cnt, q))
        inst = self.eng[q].dma_start(out=out, in_=in_, **kw)
        inst.then_inc(sem, 16)
        pool[i][1] = cnt + 1
        tok = (sem, 16 * (cnt + 1), q, True)
        self._commit(tok, r, w)
        if is_output:
            self.out_tokens.append(tok)
        return tok

    def barrier(self):
        toks = [(self.csem[k], self.ccount[k], k) for k in self.csem if self.ccount[k]]
        for q, pool in self.dpool.items():
            for sem, cnt in pool:
                if cnt:
                    toks.append((sem, 16 * cnt, q))
        for e in ("v", "a", "p", "t", "s"):
            for t in toks:
                if t[2] != e or t[0] not in self.csem.values():
                    self._wait(e, t)
        self.lastw.clear()
        self.readers.clear()

    def finish(self):
        for t in self.out_tokens:
            self._wait("s", t[:3])
        for q, pool in self.dpool.items():
            for sem, cnt in pool:
                if cnt:
                    self._wait("s", (sem, 16 * cnt, q))


class PsumPool:
    def __init__(self, nc, n=8):
        self.banks = [nc.alloc_psum_tensor(f"psb{i}", [128, 512], F32) for i in range(n)]
        self.i = 0
        self.n = n

    def get(self):
        b = self.banks[self.i]
        k = ("ps", self.i)
        self.i = (self.i + 1) % self.n
        return b, k


def build_program(phases=("p0", "p1", "pq")):
    nc = bass.Bass("TRN2", target_bir_lowering=False)
    dt = nc.dram_tensor

    def din(name, shape, dtype=F32):
        return dt(name, list(shape), dtype, kind="ExternalInput").ap()

    def dout(name, shape, dtype=F32):
        return dt(name, list(shape), dtype, kind="ExternalOutput").ap()

    xp = din("xp", [NREL * 128, D])
    xs = din("xs", [64, D])
    cT = din("cT", [128, 16, 2])
    badaT = din("badaT", [128, 96])
    badaR = din("badaR", [2, 4096])
    gmixT = din("gmixT", [128, 16])
    gffnT = din("gffnT", [128, 16])
    wada = din("wada", [24, 128, 16, 512])
    w1 = din("w1", [128, 16, W1C])
    gk_b = din("gk_b", [128, 1024])
    gq_b = din("gq_b", [128, 1024])
    ggla_b = din("ggla_b", [128, 1024])
    wg2 = din("wg2", [17, 512])
    flagt = din("flagt", [128, 2 * NT1])
    ropeK = din("ropeK", [128, NT1, 32])
    ropeI = din("ropeI", [128, NT1, 16])
    cmat = din("cmat", [128, 6, 128])
    ckT = din("ckT", [8, 128, 8, 128])
    cv = din("cv", [1024, 1024])
    cikT = din("cikT", [64, 1024])
    sgla = din("sgla", [128, 1024])

    o_k = dout("o_k", [16 * 128, 1024])
    o_v = dout("o_v", [16 * 128, 1024])
    o_ik = dout("o_ik", [16 * 128, 64])
    o_ks = dout("o_ks", [64, 1024])
    o_vs = dout("o_vs", [64, 1024])
    o_iks = dout("o_iks", [64, 64])
    o_glap = dout("o_glap", [128, 1024])
    o_glas = dout("o_glas", [128, 1024])
    o_y = dout("o_y", [16 * 128, D])
    import os as _os
    DBG = _os.environ.get("DBG", "0") == "1"
    if DBG:
        dbg_mix = dout("dbg_mix", [128, D]); dbg_xm = dout("dbg_xm", [128, D]); dbg_q = dout("dbg_q", [128, 1024])
    o_ys = dout("o_ys", [64, D])
    o_convp = dout("o_convp", [128, 88, 2])
    o_convs = dout("o_convs", [128, 88, 2])
    w2_d = din("w2", [12, 128, 16, 256])
    wiw_d = din("wiw", [128, 16, 16])
    wout_d = din("wout", [8, 128, 16, 256])
    wup_d = din("wup", [44, 128, 16, 256])
    wdn_d = din("wdn", [4, 44, 128, 512])
    convw_d = din("convw", [128, 88, 4])
    sconv_d = din("sconv", [128, 88, 2])
    hflag_d = din("hflag", [128, 32])
    ropeKh_d = din("ropeKh", [128, 32])
    ropeIh_d = din("ropeIh", [128, 16])
    biasO_d = din("biasO", [4, 128, 512])
    biasS_d = din("biasS", [128, 128])
    biasH_d = din("biasH", [128, 16384])

    KTs = dt("KTs", [NKT, 128, 8, 128], BF16).ap()
    Vs = dt("Vs", [NKT * 128, 8 * 130], BF16).ap()
    IKTs = dt("IKTs", [64, NKT * 128], BF16).ap()
    OGs = dt("OGs", [17 * 128, 1024], F32).ap()
    GTd = dt("GTd", [2, 4096], F32).ap()
    w2b = dt("w2b", [12, 128, 16, 256], BF16).ap()
    wiwb = dt("wiwb", [128, 16, 16], BF16).ap()
    woutb = dt("woutb", [8, 128, 16, 256], BF16).ap()
    wupb = dt("wupb", [44, 128, 16, 256], BF16).ap()
    wdnb = dt("wdnb", [4, 44, 128, 512], BF16).ap()

    sy = Sync(nc)
    pp = PsumPool(nc)
    V, A, P, T = (lambda fn, r=(), w=(): sy.op("v", fn, r, w)), (lambda fn, r=(), w=(): sy.op("a", fn, r, w)), \
        (lambda fn, r=(), w=(): sy.op("p", fn, r, w)), (lambda fn, r=(), w=(): sy.op("t", fn, r, w))

    import contextlib
    stack_holder = [None]

    def sb(name, shape, dtype=F32):
        sb.n = getattr(sb, "n", 0) + 1
        name = f"sb{sb.n}_{name}"
        if stack_holder[0] is None:
            return nc.alloc_sbuf_tensor(name, list(shape), dtype)
        return stack_holder[0].enter_context(nc.sbuf_tensor(name, list(shape), dtype))

    cm = sb("cm", [128, 6, 128])
    cmb = sb("cmb", [128, 128], BF16)
    modT = sb("modT", [128, 96, 2])
    g1 = sb("g1", [128, 2, 16]); shm = sb("shm", [128, 2, 16])
    g2 = sb("g2", [128, 2, 16]); shf = sb("shf", [128, 2, 16])
    tmpc = sb("tmpc", [128, 16, 2])
    negh = sb("negh", [128, 1])

    sy.dma("s", cm[:], cmat, w=["cm"])
    V(lambda e: e.tensor_copy(out=cmb[:], in_=cm[:, 0, :]), r=["cm"], w=["cmb"])
    V(lambda e: e.memset(negh[:], -0.5), w=["negh"])
    IDF = cm[:, 0, :]

    def rsqrt_small(dst, src, n, mul, key_src, key_dst):
        V(lambda e: e.tensor_scalar(out=dst, in0=src, scalar1=mul, scalar2=EPS, op0=ALU.mult, op1=ALU.add),
          r=[key_src], w=[key_dst])
        P(lambda e: e.tensor_tensor(out=dst, in0=dst, in1=negh[:, 0:1].to_broadcast([128, n]), op=ALU.pow),
          r=[key_dst, "negh"], w=[key_dst])

    if "p0" in phases:
        stack_holder[0] = contextlib.ExitStack()
        gtrow = sb("gtrow", [2, 4096])
        cTt = sb("cTt", [128, 16, 2]); scb = sb("scb", [128, 16, 2], BF16)
        bT = sb("bT", [128, 96]); gmT = sb("gmT", [128, 16]); gfT = sb("gfT", [128, 16])
        brow = sb("brow", [2, 4096])
        wab = [sb(f"wab{i}", [128, 16, 512], BF16) for i in range(2)]
        sy.dma("s", cTt[:], cT, w=["cTt"])
        sy.dma("s", bT[:], badaT, w=["bT"])
        sy.dma("s", gmT[:], gmixT, w=["gmT"])
        sy.dma("s", gfT[:], gffnT, w=["gfT"])
        sy.dma("s", brow[:], badaR, w=["brow"])
        A(lambda e: e.activation(out=scb[:], in_=cTt[:], func=AF.Silu), r=["cTt"], w=["scb"])
        import os
        P0STOP = int(os.environ.get("P0STOP", "99"))
        for blk in range(24 if P0STOP > 2 else (1 if P0STOP > 0 else 0)):
            wb = wab[blk % 2]; wk = ("wab", blk % 2)
            sy.dma("p", wb[:], wada[blk], w=[wk])
            ps, pk = pp.get()
            for cc in range(4):
                for k in range(16):
                    T(lambda e, cc=cc, k=k: e.matmul(ps[:, cc * 2:cc * 2 + 2], lhsT=wb[:, k, cc * 128:(cc + 1) * 128],
                                                     rhs=scb[:, k, :], start=(k == 0), stop=(k == 15)),
                      r=[wk, "scb"], w=[pk])
            if P0STOP == 1:
                break
            V(lambda e, blk=blk: e.tensor_tensor(
                out=modT[:, blk * 4:(blk + 1) * 4, :],
                in0=ps[:, 0:8].rearrange("p (c t) -> p c t", t=2),
                in1=bT[:, blk * 4:(blk + 1) * 4].rearrange("p (c o) -> p c o", o=1).to_broadcast([128, 4, 2]),
                op=ALU.add), r=[pk, "bT"], w=["modT"])
            if blk in (8, 9, 10, 11, 20, 21, 22, 23):
                ro = (blk - 8) * 512 if blk < 12 else 2048 + (blk - 20) * 512
                ps2, pk2 = pp.get()
                for k in range(16):
                    T(lambda e, k=k: e.matmul(ps2[0:2, :], lhsT=scb[:, k, :], rhs=wb[:, k, :],
                                              start=(k == 0), stop=(k == 15)), r=[wk, "scb"], w=[pk2])
                V(lambda e, ro=ro: e.tensor_tensor(out=gtrow[:, ro:ro + 512], in0=ps2[0:2, :], in1=brow[:, ro:ro + 512],
                                                   op=ALU.add), r=[pk2, "brow"], w=["gtrow"])
        def modview(j):
            return modT[:, j * 16:(j + 1) * 16, :].rearrange("p k t -> p t k")
        for (gd, gsrc, jsc, sd, jsh, nm) in ((g1, gmT, 1, shm, 0, "m"), (g2, gfT, 4, shf, 3, "f")):
            for t in range(2):
                V(lambda e, gd=gd, jsc=jsc, t=t: e.tensor_scalar(out=gd[:, t, :], in0=modT[:, jsc * 16:(jsc + 1) * 16, t],
                                                                 scalar1=1.0, scalar2=None, op0=ALU.add),
                  r=["modT"], w=["g" + nm])
                V(lambda e, gd=gd, gsrc=gsrc, t=t: e.tensor_tensor(out=gd[:, t, :], in0=gd[:, t, :], in1=gsrc[:], op=ALU.mult),
                  r=["g" + nm, "gmT", "gfT"], w=["g" + nm])
                V(lambda e, sd=sd, jsh=jsh, t=t: e.tensor_copy(out=sd[:, t, :], in_=modT[:, jsh * 16:(jsh + 1) * 16, t]),
                  r=["modT"], w=["sh" + nm])
        sy.dma("s", GTd, gtrow[:], r=["gtrow"], w=["GTd"])
        sy.barrier()
        stack_holder[0].close()
        stack_holder[0] = None

    if "p1" in phases:
        stack_holder[0] = contextlib.ExitStack()
        flg = sb("flg", [128, 2 * NT1])
        gkb = sb("gkb", [128, 1024]); gglab = sb("gglab", [128, 1024])
        sy.dma("s", flg[:], flagt, w=["flg"])
        sy.dma("s", gkb[:], gk_b, w=["gkb"])
        sy.dma("s", gglab[:], ggla_b, w=["gglab"])
        w1b = sb("w1b", [128, 16, W1C], BF16)
        for k in range(16):
            sy.dma("p", w1b[:, k, :], w1[:, k, :], w=["w1b"])
        wg2f = sb("wg2f", [17, 512]); wg2b = sb("wg2b", [17, 512], BF16)
        sy.dma("s", wg2f[:], wg2, w=["wg2f"])
        V(lambda e: e.tensor_copy(out=wg2b[:], in_=wg2f[:]), r=["wg2f"], w=["wg2b"])
        rKs = [sb(f"rK{i}", [128, 32]) for i in range(2)]; rIs = [sb(f"rI{i}", [128, 16]) for i in range(2)]
        xt = [sb(f"xt{i}", [128, D]) for i in range(1)]
        hT = sb("hT", [128, 16, 128], BF16)
        kf = sb("kf", [128, 1024]); sq = sb("sq", [128, 1024]); vf = sq
        kb = sb("kb", [128, 1024], BF16)
        ktb = sb("ktb", [128, 8, 128], BF16)
        vb = sb("vb", [128, 8, 130], BF16)
        ikf = sb("ikf", [128, 64]); ikb = sb("ikb", [128, 64], BF16); iktb = sb("iktb", [64, 128], BF16)
        gkfs = [sb(f"gkf{i}", [128, 512]) for i in range(2)]; gqfs = [sb(f"gqf{i}", [128, 512]) for i in range(2)]
        gvbs = [sb(f"gvb{i}", [128, 1024], BF16) for i in range(2)]
        glras = [sb(f"glra{i}", [17, 128], BF16) for i in range(2)]
        lg = sb("lg", [128, 512]); ee = sb("ee", [128, 512])
        kpb = sb("kpb", [128, 512], BF16)
        qtl = sb("qtl", [128, 512], BF16); ktl = sb("ktl", [128, 512], BF16)
        qkT = sb("qkT", [128, 8, 128], BF16)
        ATb = sb("ATb", [128, 4, 128], BF16)
        S = sb("S", [128, 1024]); Sb = kb
        dec = sb("dec", [128, 4])
        ogf = sb("ogf", [128, 1024])
        st8 = sb("st8", [128, 16]); rt8 = sb("rt8", [128, 16])
        rtmp = sb("rtmp", [128, 4, 8, 16])
        V(lambda e: e.memset(vb[:], 1.0), w=["vb"])
        for i_ in range(2):
            V(lambda e, i_=i_: e.memset(glras[i_][:], 1.0), w=[("glra", i_)])
        V(lambda e: e.memset(S[:], 0.0), w=["S"])

        def ln_transpose(xtile, xkey, which, hdst, hkey, gsc, gsh, gkeys):
            V(lambda e: e.scalar_tensor_tensor(out=sq[:].bitcast(BF16), in0=xtile, scalar=1.0, in1=xtile, op0=ALU.mult,
                                               op1=ALU.mult, accum_out=st8[:, 0:1]), r=[xkey], w=["sq", "st8"])
            rsqrt_small(rt8[:, 0:1], st8[:, 0:1], 1, 1.0 / D, "st8", "rt8")
            A(lambda e: e.activation(out=xtile, in_=xtile, func=AF.Copy, scale=rt8[:, 0:1]),
              r=[xkey, "rt8"], w=[xkey])
            for half in range(4):
                ps, pk = pp.get()
                for kk in range(4):
                    k = half * 4 + kk
                    T(lambda e, k=k, kk=kk: e.transpose(ps[:, kk * 128:(kk + 1) * 128], xtile[:, k * 128:(k + 1) * 128], IDF),
                      r=[xkey, "cm"], w=[pk])
                for kk in range(4):
                    k = half * 4 + kk
                    if isinstance(which, int):
                        A(lambda e, k=k, kk=kk: e.activation(out=hdst[:, k, :], in_=ps[:, kk * 128:(kk + 1) * 128],
                                                             func=AF.Identity, scale=gsc[:, which, k:k + 1],
                                                             bias=gsh[:, which, k:k + 1]),
                          r=[pk] + gkeys, w=[hkey])
                    else:
                        for (c0, c1, wh) in ((0, 64, 1), (64, 128, 0)):
                            A(lambda e, k=k, kk=kk, c0=c0, c1=c1, wh=wh: e.activation(
                                out=hdst[:, k, c0:c1], in_=ps[:, kk * 128 + c0:kk * 128 + c1], func=AF.Identity,
                                scale=gsc[:, wh, k:k + 1], bias=gsh[:, wh, k:k + 1]), r=[pk] + gkeys, w=[hkey])

        def proj(hsrc, hkey, wtile, wkey, c0, ncols, evac):
            ps, pk = pp.get()
            for k in range(16):
                T(lambda e, k=k: e.matmul(ps[:, 0:ncols], lhsT=hsrc[:, k, :], rhs=wtile[:, k, c0:c0 + ncols],
                                          start=(k == 0), stop=(k == 15)), r=[hkey, wkey], w=[pk])
            evac(ps, pk)

        def headnorm(src, skey, nh, hd, gb, gbkey, dst_keys):
            V(lambda e: e.tensor_tensor(out=sq[:, 0:nh * hd], in0=src, in1=src, op=ALU.mult), r=[skey], w=["sq"])
            V(lambda e: e.tensor_reduce(out=st8[:, 0:nh], in_=sq[:, 0:nh * hd].rearrange("p (h d) -> p h d", h=nh),
                                        axis=AX.X, op=ALU.add), r=["sq"], w=["st8"])
            rsqrt_small(rt8[:, 0:nh], st8[:, 0:nh], nh, 1.0 / hd, "st8", "rt8")
            V(lambda e: e.tensor_tensor(out=src.rearrange("p (h d) -> p h d", h=nh),
                                        in0=src.rearrange("p (h d) -> p h d", h=nh),
                                        in1=rt8[:, 0:nh].rearrange("p (h o) -> p h o", o=1).to_broadcast([128, nh, hd]),
                                        op=ALU.mult), r=[skey, "rt8"], w=[skey])
            V(lambda e: e.tensor_tensor(out=src, in0=src, in1=gb, op=ALU.mult), r=[skey, gbkey], w=[skey])

        def rope(src, skey, nh, hd, half, tab, tkey):
            v3 = src.rearrange("p (h d) -> p h d", h=nh)
            x1 = v3[:, :, 0:half]; x2 = v3[:, :, half:2 * half]
            cb = tab[:, 0:half].rearrange("p (o d) -> p o d", o=1).to_broadcast([128, nh, half])
            sn = tab[:, half:2 * half].rearrange("p (o d) -> p o d", o=1).to_broadcast([128, nh, half])
            t = [rtmp[:, i, 0:nh, 0:half] for i in range(4)]
            V(lambda e: e.tensor_tensor(out=t[0], in0=x1, in1=cb, op=ALU.mult), r=[skey, tkey], w=["rtmp"])
            V(lambda e: e.tensor_tensor(out=t[1], in0=x2, in1=sn, op=ALU.mult), r=[skey, tkey], w=["rtmp"])
            V(lambda e: e.tensor_tensor(out=t[2], in0=x2, in1=cb, op=ALU.mult), r=[skey, tkey], w=["rtmp"])
            V(lambda e: e.tensor_tensor(out=t[3], in0=x1, in1=sn, op=ALU.mult), r=[skey, tkey], w=["rtmp"])
            V(lambda e: e.tensor_tensor(out=x1, in0=t[0], in1=t[1], op=ALU.subtract), r=["rtmp"], w=[skey])
            V(lambda e: e.tensor_tensor(out=x2, in0=t[2], in1=t[3], op=ALU.add), r=["rtmp"], w=[skey])

        def k_store(kt):
            A(lambda e: e.copy(out=kb[:], in_=kf[:]), r=["kf"], w=["kb"])
            ps, pk = pp.get()
            pv = ps[:].bitcast(BF16) if hasattr(ps[:], "bitcast") else None
            for h in range(8):
                T(lambda e, h=h: e.transpose(pv[:, h * 128:(h + 1) * 128], kb[:, h * 128:(h + 1) * 128], cmb[:]),
                  r=["kb", "cmb"], w=[pk])
            A(lambda e: e.copy(out=ktb[:].rearrange("p h t -> p (h t)"), in_=pv[:, 0:1024]), r=[pk], w=["ktb"])
            sy.dma("s", KTs[kt], ktb[:], r=["ktb"], w=[("KT", kt)])

        def ik_store(kt):
            P(lambda e: e.tensor_copy(out=ikb[:], in_=ikf[:]), r=["ikf"], w=["ikb"])
            ps, pk = pp.get()
            pv = ps[:].bitcast(BF16)
            T(lambda e: e.transpose(pv[0:64, 0:128], ikb[:], cmb[:]), r=["ikb", "cmb"], w=[pk])
            A(lambda e: e.copy(out=iktb[:], in_=pv[0:64, 0:128]), r=[pk], w=["iktb"])
            sy.dma("s", IKTs[:, kt * 128:(kt + 1) * 128], iktb[:], r=["iktb"], w=[("IKT", kt)])

        def v_store(kt):
            sy.dma("s", Vs[kt * 128:(kt + 1) * 128, :], vb[:].rearrange("p h d -> p (h d)"), r=["vb"], w=[("V", kt)])

        import os
        P1STOP = int(os.environ.get("P1STOP", "99"))

        def load_x(xsrc, nrows, tcol, slot):
            sy.dma("p", rKs[slot][:], ropeK[:, tcol, :], w=[("rK", slot)])
            sy.dma("p", rIs[slot][:], ropeI[:, tcol, :], w=[("rI", slot)])
            if nrows < 128:
                V(lambda e: e.memset(xt[0][:], 0.0), w=[("xt", 0)])
            sy.dma("p", xt[0][0:nrows, :], xsrc, w=[("xt", 0)])

        def kv_tile(xsrc, nrows, which, tcol, kt, own_j, og_dst, outs, preloaded=False, prefetch=None):
            slot = kv_tile.n % 2
            kv_tile.n += 1
            xtile = xt[0]; xkey = ("xt", 0)
            rKt = rKs[slot]; rIt = rIs[slot]
            gkf = gkfs[slot]; gqf = gqfs[slot]; gvb = gvbs[slot]; glra = glras[slot]
            gkk = ("gkf", slot); gqk = ("gqf", slot); gvk = ("gvb", slot); glk = ("glra", slot)
            if not preloaded:
                load_x(xsrc, nrows, tcol, slot)
            ln_transpose(xtile[:], xkey, which, hT, "hT", g1, shm, ["gm"])
            if prefetch is not None:
                xsrc_n, tcol_n = prefetch
                load_x(xsrc_n, 128, tcol_n, 1 - slot)
            if P1STOP <= 2:
                return
            f01 = flg[:, tcol:tcol + 1]; fn16 = flg[:, NT1 + tcol:NT1 + tcol + 1]
            for b in range(2):
                proj(hT, "hT", w1b, "w1b", C_AK + b * 512, 512,
                     lambda ps, pk, b=b: V(lambda e: e.tensor_copy(out=kf[:, b * 512:(b + 1) * 512], in_=ps[:, :]),
                                           r=[pk], w=["kf"]))
            yield
            for b in range(2):
                def ev(ps, pk, b=b):
                    if outs is not None:
                        A(lambda e: e.copy(out=vf[:, b * 512:(b + 1) * 512], in_=ps[:, :]), r=[pk], w=["sq"])
                        P(lambda e: e.tensor_copy(out=vb[:, b * 4:(b + 1) * 4, 0:128],
                                                  in_=vf[:, b * 512:(b + 1) * 512].rearrange("p (h d) -> p h d", h=4)),
                          r=["sq"], w=["vb"])
                    else:
                        A(lambda e: e.copy(out=vb[:, b * 4:(b + 1) * 4, 0:128], in_=ps[:, :].rearrange("p (h d) -> p h d", h=4)),
                          r=[pk], w=["vb"])
                proj(hT, "hT", w1b, "w1b", C_AV + b * 512, 512, ev)
            if outs is not None:
                sy.dma("s", outs["v"], vf[0:nrows, :], r=["sq"], w=["o_v"], is_output=True)
            yield
            proj(hT, "hT", w1b, "w1b", C_IK, 64,
                 lambda ps, pk: V(lambda e: e.tensor_copy(out=ikf[:], in_=ps[:, 0:64]), r=[pk], w=["ikf"]))
            proj(hT, "hT", w1b, "w1b", C_GK, 512,
                 lambda ps, pk: V(lambda e: e.tensor_copy(out=gkf[:], in_=ps[:, :]), r=[pk], w=[gkk]))
            yield
            for b in range(2):
                proj(hT, "hT", w1b, "w1b", C_GV + b * 512, 512,
                     lambda ps, pk, b=b: A(lambda e: e.activation(out=gvb[:, b * 512:(b + 1) * 512], in_=ps[:, :],
                                                                  func=AF.Copy, scale=f01), r=[pk, "flg"], w=[gvk]))
            yield
            ps, pk = pp.get()
            for k in range(16):
                T(lambda e, k=k: e.matmul(ps[0:16, 0:128], lhsT=w1b[:, k, C_GLR:C_GLR + 16], rhs=hT[:, k, :],
                                          start=(k == 0), stop=(k == 15)), r=["hT", "w1b"], w=[pk])
            V(lambda e: e.tensor_copy(out=glra[0:16, :], in_=ps[0:16, 0:128]), r=[pk], w=[glk])
            if og_dst is not None:
                proj(hT, "hT", w1b, "w1b", C_GQ, 512,
                     lambda ps, pk: V(lambda e: e.tensor_copy(out=gqf[:], in_=ps[:, :]), r=[pk], w=[gqk]))
            yield
            headnorm(kf[:], "kf", 8, 128, gkb[:], "gkb", None)
            rope(kf[:], "kf", 8, 128, 16, rKt[:], ("rK", slot))
            if outs is not None:
                sy.dma("s", outs["k"], kf[0:nrows, :], r=["kf"], w=["o_k"], is_output=True)
            yield
            k_store(kt)
            v_store(kt)
            yield
            rope(ikf[:], "ikf", 1, 64, 8, rIt[:], ("rI", slot))
            if outs is not None:
                sy.dma("s", outs["ik"], ikf[0:nrows, :], r=["ikf"], w=["o_ik"], is_output=True)
            ik_store(kt)
            yield "S2"
            ps, pk = pp.get()
            T(lambda e: e.matmul(ps[:, :], lhsT=glra[:], rhs=wg2b[:], start=True, stop=True), r=[glk, "wg2b"], w=[pk])
            A(lambda e: e.activation(out=ee[:], in_=ps[:, :], func=AF.Exp, scale=-1.0), r=[pk], w=["ee"])
            A(lambda e: e.activation(out=ee[:], in_=ee[:], func=AF.Ln, bias=1.0), r=["ee"], w=["ee"])
            V(lambda e: e.tensor_scalar(out=lg[:], in0=ee[:], scalar1=fn16, scalar2=None, op0=ALU.mult),
              r=["ee", "flg"], w=["lg"])
            if P1STOP <= 7:
                return
            yield
            need_out = og_dst is not None
            if need_out:
                ps, pk = pp.get()
                T(lambda e: e.matmul(ps[:, :], lhsT=cm[:, 1, :], rhs=lg[:], start=True, stop=True), r=["cm", "lg"], w=[pk])
                A(lambda e: e.activation(out=ee[:], in_=ps[:, :], func=AF.Exp), r=[pk], w=["ee"])
                V(lambda e: e.scalar_tensor_tensor(out=qtl[:], in0=gqf[:], scalar=128.0 ** -0.5, in1=ee[:], op0=ALU.mult,
                                                   op1=ALU.mult), r=[gqk, "ee"], w=["qtl"])
                A(lambda e: e.activation(out=ee[:], in_=ps[:, :], func=AF.Exp, scale=-1.0), r=[pk, "qtl"], w=["ee"])
                V(lambda e: e.tensor_tensor(out=ktl[:], in0=gkf[:], in1=ee[:], op=ALU.mult), r=[gkk, "ee"], w=["ktl"])
                yield
                ps, pk = pp.get()
                pv = ps[:].bitcast(BF16)
                for h in range(4):
                    T(lambda e, h=h: e.transpose(pv[:, h * 128:(h + 1) * 128], qtl[:, h * 128:(h + 1) * 128], cmb[:]),
                      r=["qtl", "cmb"], w=[pk])
                    T(lambda e, h=h: e.transpose(pv[:, (4 + h) * 128:(5 + h) * 128], ktl[:, h * 128:(h + 1) * 128], cmb[:]),
                      r=["ktl", "cmb"], w=[pk])
                A(lambda e: e.copy(out=qkT[:].rearrange("p h t -> p (h t)"), in_=pv[:, 0:1024]), r=[pk], w=["qkT"])
                ps, pk = pp.get()
                for h in range(4):
                    T(lambda e, h=h: e.matmul(ps[:, h * 128:(h + 1) * 128], lhsT=qkT[:, 4 + h, :], rhs=qkT[:, h, :],
                                              start=True, stop=True), r=["qkT"], w=[pk])
                V(lambda e: e.tensor_tensor(out=ATb[:], in0=ps[:, :].rearrange("p (h t) -> p h t", h=4),
                                            in1=cm[:, 1, :].rearrange("p (o t) -> p o t", o=1).to_broadcast([128, 4, 128]),
                                            op=ALU.mult), r=[pk, "cm"], w=["ATb"])
                yield
                P(lambda e: e.tensor_copy(out=Sb[:], in_=S[:]), r=["S"], w=["kb"])
                pso = [pp.get(), pp.get()]
                for h in range(4):
                    po, pok = pso[h // 2]
                    oc = (h % 2) * 256
                    T(lambda e, h=h, po=po, oc=oc: e.matmul(po[:, oc:oc + 256], lhsT=qkT[:, h, :], rhs=Sb[:, h * 256:(h + 1) * 256],
                                                            start=True, stop=False), r=["qkT", "kb"], w=[pok])
                    T(lambda e, h=h, po=po, oc=oc: e.matmul(po[:, oc:oc + 256], lhsT=ATb[:, h, :], rhs=gvb[:, h * 256:(h + 1) * 256],
                                                            start=False, stop=True), r=["ATb", gvk], w=[pok])
                for i2 in range(2):
                    po, pok = pso[i2]
                    V(lambda e, po=po, i2=i2: e.tensor_copy(out=ogf[:, i2 * 512:(i2 + 1) * 512], in_=po[:, :]), r=[pok], w=["ogf"])
                headnorm(ogf[:], "ogf", 4, 256, gglab[:], "gglab", None)
                lo, hi, drow = og_dst
                sy.dma("s", OGs[drow:drow + (hi - lo), :], ogf[lo:hi, :], r=["ogf"], w=[("OG", drow)])
            if P1STOP <= 8:
                return
            yield
            ps, pk = pp.get()
            T(lambda e: e.matmul(ps[:, :], lhsT=cm[:, 2, :], rhs=lg[:], start=True, stop=True), r=["cm", "lg"], w=[pk])
            A(lambda e: e.activation(out=ee[:], in_=ps[:, :], func=AF.Exp), r=[pk, "ktl"], w=["ee"])
            V(lambda e: e.tensor_tensor(out=kpb[:], in0=gkf[:], in1=ee[:], op=ALU.mult), r=[gkk, "ee"], w=["kpb"])
            ps, pk = pp.get()
            for h in range(4):
                T(lambda e, h=h: e.matmul(ps[:, h:h + 1], lhsT=lg[:, h * 128:(h + 1) * 128], rhs=cm[:, 3, 0:1],
                                          start=True, stop=True), r=["lg", "cm"], w=[pk])
            A(lambda e: e.activation(out=dec[:], in_=ps[:, 0:4], func=AF.Exp), r=[pk], w=["dec"])
            yield
            psu = [pp.get(), pp.get()]
            for h in range(4):
                pu, puk = psu[h // 2]
                oc = (h % 2) * 256
                T(lambda e, h=h, pu=pu, oc=oc: e.matmul(pu[:, oc:oc + 256], lhsT=kpb[:, h * 128:(h + 1) * 128],
                                                        rhs=gvb[:, h * 256:(h + 1) * 256], start=True, stop=True),
                  r=["kpb", gvk], w=[puk])
            for h in range(4):
                pu, puk = psu[h // 2]
                oc = (h % 2) * 256
                V(lambda e, h=h, pu=pu, oc=oc: e.scalar_tensor_tensor(
                    out=S[:, h * 256:(h + 1) * 256], in0=S[:, h * 256:(h + 1) * 256], scalar=dec[:, h:h + 1],
                    in1=pu[:, oc:oc + 256], op0=ALU.mult, op1=ALU.add), r=["S", "dec", puk], w=["S"])
        kv_tile.n = 0

        ntiles = build_program.ntiles_p1 if hasattr(build_program, "ntiles_p1") else NREL
        conv_jobs = [(w2b[2 * i:2 * i + 2], w2_d[2 * i:2 * i + 2]) for i in range(6)] + [(wiwb, wiw_d)]
        conv_jobs += [(woutb[2 * i:2 * i + 2], wout_d[2 * i:2 * i + 2]) for i in range(4)]
        conv_jobs += [(wupb[2 * i:2 * i + 2], wup_d[2 * i:2 * i + 2]) for i in range(22)]
        conv_jobs += [(wdnb[nb, 11 * i:11 * i + 11], wdn_d[nb, 11 * i:11 * i + 11]) for nb in range(4) for i in range(4)]
        if "pq" not in phases:
            conv_jobs = []
        prev_g = None
        for r in range(ntiles):
            if conv_jobs:
                dst_, src_ = conv_jobs.pop(0)
                sy.dma("p", dst_, src_, w=["wconv"])
            own_j = r // 8 if r % 8 == 7 else None
            halo_j = r // 8 if r % 8 == 6 else None
            outs = None
            og = None
            if own_j is not None:
                outs = {"k": o_k[own_j * 128:(own_j + 1) * 128, :], "v": o_v[own_j * 128:(own_j + 1) * 128, :],
                        "ik": o_ik[own_j * 128:(own_j + 1) * 128, :]}
                og = (0, 128, (1 + own_j) * 128)
            if halo_j is not None:
                og = (126, 128, 64 + 2 * halo_j)
            if os.environ.get("P1OUTS", "1") == "0":
                outs = None
            if os.environ.get("P1OG", "1") == "0" and own_j is not None:
                og = None
            pf = (xp[(r + 1) * 128:(r + 2) * 128, :], r + 1) if r + 1 < ntiles else None
            g_ = kv_tile(xp[r * 128:(r + 1) * 128, :], 128, 0, r, r if r < 128 else NKT - 1, own_j, og, outs,
                         preloaded=(r > 0), prefetch=pf)
            dn = False; do = prev_g is None
            while not (dn and do):
                if not dn:
                    dn = next(g_, "END") in ("S2", "END")
                if not do:
                    do = next(prev_g, "END") == "END"
            prev_g = g_
        if prev_g is not None:
            for _ in prev_g:
                pass
        while conv_jobs:
            dst_, src_ = conv_jobs.pop(0)
            sy.dma("p", dst_, src_, w=["wconv"])
        sy.dma("s", o_glap, S[:], r=["S"], w=["o_glap"], is_output=True)
        P1POST = int(os.environ.get("P1POST", "1"))
        for i in range(8 if P1POST else 0):
            ktf = xt[0]; xkey = ("xt", 0)
            sy.dma("s", ktf[:, 0:1024].rearrange("p (h t) -> p h t", h=8), ckT[i], w=[xkey])
            P(lambda e, ktf=ktf: e.tensor_copy(out=ktb[:].rearrange("p h t -> p (h t)"), in_=ktf[:, 0:1024]), r=[xkey], w=["ktb"])
            sy.dma("s", KTs[KT_S0 + i], ktb[:], r=["ktb"], w=[("KT", KT_S0 + i)])
            sy.dma("s", ktf[:, 1024:2048], cv[i * 128:(i + 1) * 128, :], w=[xkey])
            A(lambda e, ktf=ktf: e.copy(out=vb[:, :, 0:128], in_=ktf[:, 1024:2048].rearrange("p (h d) -> p h d", h=8)),
              r=[xkey], w=["vb"])
            v_store(KT_S0 + i)
        ikc = xt[0][0:64, 0:1024]; ikcb = kb[0:64, :]
        if P1POST:
            sy.dma("s", ikc, cikT, w=[("xt", 0)])
            V(lambda e: e.tensor_copy(out=ikcb, in_=ikc), r=[("xt", 0)], w=["kb"])
            sy.dma("s", IKTs[:, KT_S0 * 128:(KT_S0 + 8) * 128], ikcb, r=["kb"], w=[("IKT", "c")])
            sy.dma("s", S[:], sgla, r=["o_glap"], w=["S"])
            outs = {"k": o_ks, "v": o_vs, "ik": o_iks}
            for _ in kv_tile(xs, 64, 1, NT1 - 1, KT_S0 + 8, None, (0, 64, 0), outs):
                pass
        sy.dma("s", o_glas, S[:], r=["S"], w=["o_glas"], is_output=True)
        sy.barrier()
        stack_holder[0].close()
        stack_holder[0] = None

    if "pq" in phases:
        import os
        stack_holder[0] = contextlib.ExitStack()
        pp.n = 5
        pp.i = 0
        OB = [(pp.banks[5 + i], ("ps", 5 + i)) for i in range(3)]
        NITER = int(os.environ.get("NITER", "26"))
        gqb = sb("gqb", [128, 1024])
        sy.dma("s", gqb[:], gq_b, w=["gqb"])
        gtc = [sb(f"gtc{i}", [128, 512]) for i in range(2)]
        uhalo = sb("uhalo", [128, 88, 32])
        convw = sb("convw", [128, 88, 4]); sconv = sb("sconv", [128, 88, 2]); hfl = sb("hfl", [128, 32])
        oconv = sb("oconv", [128, 88, 2])
        sy.dma("s", convw[:], convw_d, w=["convw"])
        sy.dma("s", sconv[:], sconv_d, w=["sconv"])
        sy.dma("s", hfl[:], hflag_d, w=["hfl"])
        xq = sb("xq", [128, D])
        hT = sb("hT", [128, 16, 128], BF16)
        wblk = [sb(f"wblk{i}", [128, 16, 256], BF16) for i in range(2)]
        wiw = sb("wiw", [128, 16, 16], BF16)
        qf = sb("qf", [128, 1024]); sq = sb("sq", [128, 1024]); ogt = sq
        QT = sb("QT", [128, 8, 128], BF16); iqT = sb("iqT", [128, 8, 128], BF16)
        iwf = sb("iwf", [128, 16])
        mixb = sb("mixb", [128, D], BF16); mixT = hT; qb = mixb[:, 0:1024]
        score = sb("score", [128, 16384]); xsq = score[:, 0:2048]
        hidT = score[:, 2048:4864].bitcast(BF16).rearrange("p (i t) -> p i t", i=44)
        IKc = [sb(f"IKc{i}", [128, 512], BF16) for i in range(2)]
        rl = [sb(f"rl{i}", [128, 512]) for i in range(2)]
        bch = [sb(f"bch{i}", [128, 512]) for i in range(2)]
        junk = qf[:].bitcast(BF16)
        junkA = sq[:].bitcast(BF16)
        acnt = sb("acnt", [128, 8])
        BSPLIT = os.environ.get("BSPLIT", "1") == "1"
        bs = sb("bs", [128, 48]); cnts = sb("cnts", [128, 16]); hmx = sb("hmx", [128, 40]); lmx = sb("lmx", [128, 40])
        m01 = [sb(f"m01{i}", [128, 512], BF16) for i in range(2)]
        mT = [sb(f"mT{i}", [128, 512], BF16) for i in range(2)]
        KTc = [sb(f"KTc{i}", [128, 4, 8, 128], BF16) for i in range(2)]
        Vc = [sb(f"Vc{i}", [128, 4, 8 * 130], BF16) for i in range(2)]
        eT = [sb(f"eT{i}", [128, 512], BF16) for i in range(2)]
        PT = [sb(f"PT{i}", [128, 512], BF16) for i in range(2)]
        oa = sb("oa", [128, 8, 129]); rden = sb("rden", [128, 8])
        uexts = [sb(f"uext{i}", [128, 2, 130]) for i in range(2)]; tas = [sb(f"ta{i}", [128, 2, 128]) for i in range(2)]
        sas = [sb(f"sa{i}", [128, 128]) for i in range(2)]
        wdp = [sb(f"wdp{i}", [128, 2, 512], BF16) for i in range(3)]
        tmpy = rl[1]
        rKq = sb("rKq", [128, 32]); rIq = sb("rIq", [128, 16])
        st8 = sb("st8q", [128, 16]); rt8 = sb("rt8q", [128, 16])
        rtmp = sb("rtmpq", [128, 4, 16, 16])

        cntq = {"w": 0, "d": 0, "g": 0}

        def ln_transpose_q(which, gsc, gsh):
            V(lambda e: e.scalar_tensor_tensor(out=xsq, in0=xq[:], scalar=1.0, in1=xq[:], op0=ALU.mult,
                                               op1=ALU.mult, accum_out=st8[:, 0:1]), r=["xq"], w=["score", "st8"])
            rsqrt_small(rt8[:, 0:1], st8[:, 0:1], 1, 1.0 / D, "st8", "rt8")
            A(lambda e: e.activation(out=xsq, in_=xq[:], func=AF.Copy, scale=rt8[:, 0:1]), r=["xq", "rt8"], w=["score"])
            for half in range(4):
                ps, pk = pp.get()
                for kk in range(4):
                    k = half * 4 + kk
                    T(lambda e, k=k, kk=kk: e.transpose(ps[:, kk * 128:(kk + 1) * 128], xsq[:, k * 128:(k + 1) * 128], IDF),
                      r=["score"], w=[pk])
                for kk in range(4):
                    k = half * 4 + kk
                    A(lambda e, k=k, kk=kk: e.activation(out=hT[:, k, :], in_=ps[:, kk * 128:(kk + 1) * 128], func=AF.Identity,
                                                         scale=gsc[:, which, k:k + 1], bias=gsh[:, which, k:k + 1]),
                      r=[pk], w=["hT"])

        def wload(src):
            i = cntq["w"] % 2
            cntq["w"] += 1
            sy.dma("s", wblk[i][:], src, w=[("wblk", i)])
            return wblk[i], ("wblk", i)

        def projq(lhs, lkey, wt, wkey, c0, ncols, evac):
            ps, pk = pp.get()
            for k in range(16):
                T(lambda e, k=k: e.matmul(ps[:, 0:ncols], lhsT=lhs[:, k, :], rhs=wt[:, k, c0:c0 + ncols],
                                          start=(k == 0), stop=(k == 15)), r=[lkey, wkey], w=[pk])
            evac(ps, pk)

        def headnorm_q(src, skey, nh, hd, gb, gbkey):
            V(lambda e: e.tensor_tensor(out=sq[:, 0:nh * hd], in0=src, in1=src, op=ALU.mult), r=[skey], w=["sq"])
            V(lambda e: e.tensor_reduce(out=st8[:, 0:nh], in_=sq[:, 0:nh * hd].rearrange("p (h d) -> p h d", h=nh),
                                        axis=AX.X, op=ALU.add), r=["sq"], w=["st8"])
            rsqrt_small(rt8[:, 0:nh], st8[:, 0:nh], nh, 1.0 / hd, "st8", "rt8")
            V(lambda e: e.tensor_tensor(out=src.rearrange("p (h d) -> p h d", h=nh),
                                        in0=src.rearrange("p (h d) -> p h d", h=nh),
                                        in1=rt8[:, 0:nh].rearrange("p (h o) -> p h o", o=1).to_broadcast([128, nh, hd]),
                                        op=ALU.mult), r=[skey, "rt8"], w=[skey])
            V(lambda e: e.tensor_tensor(out=src, in0=src, in1=gb, op=ALU.mult), r=[skey, gbkey], w=[skey])

        def rope_q(src, skey, nh, hd, half, tab, tkey):
            v3 = src.rearrange("p (h d) -> p h d", h=nh)
            x1 = v3[:, :, 0:half]; x2 = v3[:, :, half:2 * half]
            cb = tab[:, 0:half].rearrange("p (o d) -> p o d", o=1).to_broadcast([128, nh, half])
            sn = tab[:, half:2 * half].rearrange("p (o d) -> p o d", o=1).to_broadcast([128, nh, half])
            t = [rtmp[:, i, 0:nh, 0:half] for i in range(4)]
            V(lambda e: e.tensor_tensor(out=t[0], in0=x1, in1=cb, op=ALU.mult), r=[skey, tkey], w=["rtmp"])
            V(lambda e: e.tensor_tensor(out=t[1], in0=x2, in1=sn, op=ALU.mult), r=[skey, tkey], w=["rtmp"])
            V(lambda e: e.tensor_tensor(out=t[2], in0=x2, in1=cb, op=ALU.mult), r=[skey, tkey], w=["rtmp"])
            V(lambda e: e.tensor_tensor(out=t[3], in0=x1, in1=sn, op=ALU.mult), r=[skey, tkey], w=["rtmp"])
            V(lambda e: e.tensor_tensor(out=x1, in0=t[0], in1=t[1], op=ALU.subtract), r=["rtmp"], w=[skey])
            V(lambda e: e.tensor_tensor(out=x2, in0=t[2], in1=t[3], op=ALU.add), r=["rtmp"], w=[skey])

        def q_tile(kind, j):
            which = 1 if kind == "sample" else 0
            if kind == "own":
                r0 = (8 * j + 7) * 128
                sy.dma("s", xq[:], xp[r0:r0 + 128, :], w=["xq"])
                sy.dma("s", rKq[:], ropeK[:, 8 * j + 7, :], w=["rKq"])
                sy.dma("s", rIq[:], ropeI[:, 8 * j + 7, :], w=["rIq"])
                nrows = 128
            elif kind == "sample":
                V(lambda e: e.memset(xq[:], 0.0), w=["xq"])
                sy.dma("s", xq[0:64, :], xs, w=["xq"])
                sy.dma("s", rKq[:], ropeK[:, NT1 - 1, :], w=["rKq"])
                sy.dma("s", rIq[:], ropeI[:, NT1 - 1, :], w=["rIq"])
                nrows = 64
            else:
                V(lambda e: e.memset(xq[:], 0.0), w=["xq"])
                for jj in range(16):
                    r0 = (8 * jj + 6) * 128 + 126
                    sy.dma("s", xq[2 * jj:2 * jj + 2, :], xp[r0:r0 + 2, :], w=["xq"])
                sy.dma("s", rKq[:], ropeKh_d, w=["rKq"])
                sy.dma("s", rIq[:], ropeIh_d, w=["rIq"])
                nrows = 32
            ln_transpose_q(which, g1, shm)
            for b in range(4):
                wt, wk = wload(w2b[b])
                projq(hT, "hT", wt, wk, 0, 256,
                      lambda ps, pk, b=b: A(lambda e: e.copy(out=qf[:, b * 256:(b + 1) * 256], in_=ps[:, 0:256]), r=[pk], w=["qf"]))
            headnorm_q(qf[:], "qf", 8, 128, gqb[:], "gqb")
            rope_q(qf[:], "qf", 8, 128, 16, rKq[:], "rKq")
            V(lambda e: e.tensor_copy(out=qb, in_=qf[:]), r=["qf"], w=["mixb"])
            ps, pk = pp.get()
            pv = ps[:].bitcast(BF16)
            for h in range(8):
                T(lambda e, h=h: e.transpose(pv[:, h * 128:(h + 1) * 128], qb[:, h * 128:(h + 1) * 128], cmb[:]), r=["mixb"], w=[pk])
            A(lambda e: e.copy(out=QT[:].rearrange("p h t -> p (h t)"), in_=pv[:, 0:1024]), r=[pk], w=["QT"])
            for b in range(4):
                wt, wk = wload(w2b[4 + b])
                projq(hT, "hT", wt, wk, 0, 256,
                      lambda ps, pk, b=b: A(lambda e: e.copy(out=qf[:, b * 256:(b + 1) * 256], in_=ps[:, 0:256]), r=[pk], w=["qf"]))
            rope_q(qf[:], "qf", 16, 64, 8, rIq[:], "rIq")
            V(lambda e: e.tensor_copy(out=qb, in_=qf[:]), r=["qf"], w=["mixb"])
            ps, pk = pp.get()
            pv = ps[:].bitcast(BF16)
            for h in range(8):
                T(lambda e, h=h: e.transpose(pv[:, h * 128:(h + 1) * 128], qb[:, h * 128:(h + 1) * 128], cmb[:]), r=["mixb"], w=[pk])
            A(lambda e: e.copy(out=iqT[:].rearrange("p h t -> p (h t)"), in_=pv[:, 0:1024]), r=[pk], w=["iqT"])
            sy.dma("s", wiw[:], wiwb, w=["wiw"])
            ps, pk = pp.get()
            for k in range(16):
                T(lambda e, k=k: e.matmul(ps[:, 0:16], lhsT=hT[:, k, :], rhs=wiw[:, k, :], start=(k == 0), stop=(k == 15)),
                  r=["hT", "wiw"], w=[pk])
            V(lambda e: e.tensor_copy(out=iwf[:], in_=ps[:, 0:16]), r=[pk], w=["iwf"])
            if nrows < 128:
                V(lambda e: e.memset(ogt[:], 0.0), w=["sq"])
            if kind == "own":
                sy.dma("s", ogt[:], OGs[(1 + j) * 128:(2 + j) * 128, :], w=["sq"])
            elif kind == "sample":
                sy.dma("s", ogt[0:64, :], OGs[0:64, :], w=["sq"])
            else:
                sy.dma("s", ogt[0:32, :], OGs[64:96, :], w=["sq"])
            for b in range(4):
                wt, wk = wload(w2b[8 + b])
                def evg(ps, pk, b=b):
                    A(lambda e: e.activation(out=qf[:, b * 256:(b + 1) * 256], in_=ps[:, 0:256], func=AF.Silu), r=[pk], w=["qf"])
                    V(lambda e: e.tensor_tensor(out=mixb[:, 1024 + b * 256:1024 + (b + 1) * 256], in0=qf[:, b * 256:(b + 1) * 256],
                                                in1=ogt[:, b * 256:(b + 1) * 256], op=ALU.mult), r=["qf", "sq"], w=["mixb"])
                projq(hT, "hT", wt, wk, 0, 256, evg)

            if kind == "own":
                nkt = 8 * j + 8
                chunks = [(4 * ci, 4) for ci in range(nkt // 4)]
            elif kind == "sample":
                chunks = [(KT_S0, 4), (KT_S0 + 4, 4), (KT_S0 + 8, 1)]
            else:
                chunks = [(4 * ci, 4) for ci in range(32)]
            nch = len(chunks)
            L = sum(n for _, n in chunks) * 128
            coff = [sum(n for _, n in chunks[:i]) * 128 for i in range(nch)]

            def bias_src(ci):
                if kind == "own":
                    last = (ci == nch - 1)
                    if ci == 0:
                        return biasO_d[0]
                    if ci == 1:
                        return biasO_d[2] if last else biasO_d[1]
                    return biasO_d[3] if last else None
                if kind == "sample":
                    return biasS_d if ci == nch - 1 else None
                return biasH_d[:, ci * 512:(ci + 1) * 512]

            for ci, (kt0, nk) in enumerate(chunks):
                nc_ = nk * 128
                ik = IKc[ci % 2]; ikk = ("IKc", ci % 2)
                sy.dma("s", ik[0:64, 0:nc_], IKTs[:, kt0 * 128:kt0 * 128 + nc_], w=[ikk])
                sy.dma("s", ik[64:128, 0:nc_], IKTs[:, kt0 * 128:kt0 * 128 + nc_], w=[ikk])
                bsrc = bias_src(ci)
                bt = bch[ci % 2]; bk = ("bch", ci % 2)
                if bsrc is not None:
                    sy.dma("s", bt[:, 0:nc_], bsrc if kind != "sample" else bsrc, w=[bk])
                sc = score[:, coff[ci]:coff[ci] + nc_]
                for h in range(16):
                    hp, lo = h // 2, (h % 2) * 64
                    ps, pk = pp.get()
                    T(lambda e, hp=hp, lo=lo, ps=ps: e.matmul(ps[:, 0:nc_], lhsT=iqT[lo:lo + 64, hp, :], rhs=ik[lo:lo + 64, 0:nc_],
                                                              start=True, stop=True), r=["iqT", ikk], w=[pk])
                    rt = rl[h % 2]; rk = ("rl", h % 2)
                    A(lambda e, ps=ps, rt=rt: e.activation(out=rt[:, 0:nc_], in_=ps[:, 0:nc_], func=AF.Relu), r=[pk], w=[rk])
                    if h == 0:
                        if bsrc is not None:
                            V(lambda e, rt=rt: e.scalar_tensor_tensor(out=sc, in0=rt[:, 0:nc_], scalar=iwf[:, 0:1], in1=bt[:, 0:nc_],
                                                                      op0=ALU.mult, op1=ALU.add), r=[rk, "iwf", bk], w=["score"])
                        else:
                            V(lambda e, rt=rt: e.tensor_scalar(out=sc, in0=rt[:, 0:nc_], scalar1=iwf[:, 0:1], scalar2=None, op0=ALU.mult),
                              r=[rk, "iwf"], w=["score"])
                    else:
                        V(lambda e, rt=rt, h=h: e.scalar_tensor_tensor(out=sc, in0=rt[:, 0:nc_], scalar=iwf[:, h:h + 1], in1=sc,
                                                                       op0=ALU.mult, op1=ALU.add), r=[rk, "iwf", "score"], w=["score"])
                V(lambda e, ci=ci: e.tensor_reduce(out=hmx[:, ci:ci + 1], in_=sc, axis=AX.X, op=ALU.max), r=["score"], w=["hmx"])
                if bsrc is None:
                    V(lambda e, ci=ci: e.tensor_reduce(out=lmx[:, ci:ci + 1], in_=sc, axis=AX.X, op=ALU.min, negate=True), r=["score"], w=["lmx"])
                    continue
                rt = rl[0]; rk = ("rl", 0)
                V(lambda e, rt=rt: e.tensor_scalar(out=rt[:, 0:nc_], in0=sc, scalar1=-1.0e29, scalar2=-3.0e30, op0=ALU.is_lt, op1=ALU.mult),
                  r=["score"], w=[rk])
                V(lambda e, rt=rt: e.scalar_tensor_tensor(out=rt[:, 0:nc_], in0=sc, scalar=-1.0, in1=rt[:, 0:nc_], op0=ALU.mult, op1=ALU.add),
                  r=["score", rk], w=[rk])
                V(lambda e, rt=rt, ci=ci: e.tensor_reduce(out=lmx[:, ci:ci + 1], in_=rt[:, 0:nc_], axis=AX.X, op=ALU.max), r=[rk], w=["lmx"])
            V(lambda e: e.tensor_reduce(out=bs[:, 0:1], in_=hmx[:, 0:nch], axis=AX.X, op=ALU.max), r=["hmx"], w=["bs"])
            V(lambda e: e.tensor_reduce(out=bs[:, 1:2], in_=lmx[:, 0:nch], axis=AX.X, op=ALU.max), r=["lmx"], w=["bs"])
            V(lambda e: e.tensor_scalar(out=bs[:, 2:3], in0=bs[:, 1:2], scalar1=-1.0, scalar2=-1.0, op0=ALU.mult, op1=ALU.add),
              r=["bs"], w=["bs"])
            V(lambda e: e.tensor_tensor(out=bs[:, 3:4], in0=bs[:, 0:1], in1=bs[:, 2:3], op=ALU.subtract), r=["bs"], w=["bs"])
            nseg = (L + 2047) // 2048
            for it in range(NITER):
                fac = 0.5 ** (it + 1)
                V(lambda e, fac=fac: e.scalar_tensor_tensor(out=bs[:, 4:5], in0=bs[:, 3:4], scalar=fac, in1=bs[:, 2:3],
                                                            op0=ALU.mult, op1=ALU.add), r=["bs"], w=["bs"])
                nact = (nseg * 5 + 4) // 8 if (nseg >= 4 and BSPLIT) else 0
                ndve = nseg - nact
                if nact:
                    V(lambda e: e.tensor_scalar(out=bs[:, 7:8], in0=bs[:, 4:5], scalar1=-1.0, scalar2=None, op0=ALU.mult), r=["bs"], w=["bs7"])
                    for sg in range(ndve, nseg):
                        c0 = sg * 2048; c1 = min(L, c0 + 2048)
                        A(lambda e, c0=c0, c1=c1, sg=sg: e.activation(out=junkA[:, 0:c1 - c0], in_=score[:, c0:c1], func=AF.Sign,
                                                                      bias=bs[:, 7:8], scale=1.0, accum_out=acnt[:, sg - ndve:sg - ndve + 1]),
                          r=["score", "bs7"], w=["sq", "acnt"])
                for sg in range(ndve):
                    c0 = sg * 2048; c1 = min(L, c0 + 2048)
                    V(lambda e, c0=c0, c1=c1, sg=sg: e.tensor_scalar(out=junk[:, 0:c1 - c0], in0=score[:, c0:c1], scalar1=bs[:, 4:5],
                                                                     scalar2=0.0, op0=ALU.is_gt, op1=ALU.add,
                                                                     accum_out=cnts[:, sg:sg + 1]), r=["score", "bs"], w=["qf", "cnts"])
                V(lambda e: e.tensor_reduce(out=bs[:, 5:6], in_=cnts[:, 0:ndve], axis=AX.X, op=ALU.add), r=["cnts"], w=["bs"])
                if nact:
                    nel = float(L - ndve * 2048)
                    V(lambda e: e.tensor_reduce(out=bs[:, 8:9], in_=acnt[:, 0:nact], axis=AX.X, op=ALU.add), r=["acnt"], w=["bs"])
                    V(lambda e, nel=nel: e.tensor_scalar(out=bs[:, 8:9], in0=bs[:, 8:9], scalar1=0.5, scalar2=0.5 * nel, op0=ALU.mult, op1=ALU.add),
                      r=["bs"], w=["bs"])
                    V(lambda e: e.tensor_tensor(out=bs[:, 5:6], in0=bs[:, 5:6], in1=bs[:, 8:9], op=ALU.add), r=["bs"], w=["bs"])
                V(lambda e, fac=fac: e.tensor_scalar(out=bs[:, 6:7], in0=bs[:, 5:6], scalar1=255.5, scalar2=fac, op0=ALU.is_gt, op1=ALU.mult),
                  r=["bs"], w=["bs"])
                V(lambda e: e.scalar_tensor_tensor(out=bs[:, 2:3], in0=bs[:, 3:4], scalar=bs[:, 6:7], in1=bs[:, 2:3],
                                                   op0=ALU.mult, op1=ALU.add), r=["bs"], w=["bs"])
            nob = 0
            for ci, (kt0, nk) in enumerate(chunks):
                nc_ = nk * 128
                kc = KTc[ci % 2]; kck = ("KTc", ci % 2)
                vc = Vc[ci % 2]; vck = ("Vc", ci % 2)
                sy.dma("s", kc[:, 0:nk], KTs[kt0:kt0 + nk].rearrange("t p h s -> p t h s"), w=[kck])
                sy.dma("s", vc[:, 0:nk, :], Vs[kt0 * 128:(kt0 + nk) * 128, :].rearrange("(t p) c -> p t c", p=128), w=[vck])
                mm = m01[ci % 2]; mmk = ("m01", ci % 2)
                V(lambda e, mm=mm: e.tensor_scalar(out=mm[:, 0:nc_], in0=score[:, coff[ci]:coff[ci] + nc_], scalar1=bs[:, 2:3],
                                                   scalar2=None, op0=ALU.is_gt), r=["score", "bs"], w=[mmk])
                ps, pk = pp.get()
                pv = ps[:].bitcast(BF16)
                for t in range(nk):
                    T(lambda e, t=t, mm=mm: e.transpose(pv[:, t * 128:(t + 1) * 128], mm[:, t * 128:(t + 1) * 128], cmb[:]), r=[mmk], w=[pk])
                mt = mT[ci % 2]; mtk = ("mT", ci % 2)
                A(lambda e, mt=mt: e.copy(out=mt[:, 0:nc_], in_=pv[:, 0:nc_]), r=[pk], w=[mtk])
                for h in range(8):
                    ps, pk = pp.get()
                    for t in range(nk):
                        T(lambda e, t=t, h=h, ps=ps: e.matmul(ps[:, t * 128:(t + 1) * 128], lhsT=kc[:, t, h, :], rhs=QT[:, h, :],
                                                              start=True, stop=True), r=[kck, "QT"], w=[pk])
                    et = eT[h % 2]; ek = ("eT", h % 2)
                    A(lambda e, et=et, ps=ps: e.activation(out=et[:, 0:nc_], in_=ps[:, 0:nc_], func=AF.Exp, scale=128.0 ** -0.5),
                      r=[pk], w=[ek])
                    pt = PT[h % 2]; ptk = ("PT", h % 2)
                    V(lambda e, et=et, pt=pt, mt=mt: e.tensor_tensor(out=pt[:, 0:nc_], in0=et[:, 0:nc_], in1=mt[:, 0:nc_], op=ALU.mult),
                      r=[ek, mtk], w=[ptk])
                    ob, obk = OB[h // 3]
                    oc = (h % 3) * 129
                    for t in range(nk):
                        T(lambda e, t=t, h=h, pt=pt, ob=ob, oc=oc: e.matmul(
                            ob[:, oc:oc + 129], lhsT=pt[:, t * 128:(t + 1) * 128], rhs=vc[:, t, h * 130:h * 130 + 129],
                            start=(t == 0), stop=(t == nk - 1)), r=[ptk, vck], w=[obk])
                    if h in (2, 5, 7):
                        bi = h // 3
                        nh_ = 3 if bi < 2 else 2
                        oav = oa[:, bi * 3:bi * 3 + nh_, :].rearrange("p h d -> p (h d)")
                        if ci == 0:
                            V(lambda e, ob=ob, oav=oav, nh_=nh_: e.tensor_copy(out=oav, in_=ob[:, 0:nh_ * 129]), r=[obk], w=["oa"])
                        else:
                            V(lambda e, ob=ob, oav=oav, nh_=nh_: e.tensor_tensor(out=oav, in0=oav, in1=ob[:, 0:nh_ * 129], op=ALU.add),
                              r=[obk, "oa"], w=["oa"])
            V(lambda e: e.tensor_scalar(out=rden[:], in0=oa[:, :, 128:129].rearrange("p h o -> p (h o)"), scalar1=1.0e-30, scalar2=None, op0=ALU.max), r=["oa"], w=["rden"])
            V(lambda e: e.reciprocal(out=rden[:], in_=rden[:]), r=["rden"], w=["rden"])
            V(lambda e: e.tensor_tensor(out=mixb[:, 0:1024].rearrange("p (h d) -> p h d", h=8), in0=oa[:, :, 0:128],
                                        in1=rden[:].rearrange("p (h o) -> p h o", o=1).to_broadcast([128, 8, 128]), op=ALU.mult),
              r=["oa", "rden"], w=["mixb"])
            if DBG:
                V(lambda e: e.tensor_copy(out=xsq, in_=mixb[:]), r=["mixb"], w=["score"])
                sy.dma("s", dbg_mix, xsq, r=["score"], w=["dbg_mix"], is_output=True)
            for half in range(2):
                ps, pk = pp.get()
                pv = ps[:].bitcast(BF16)
                for kk in range(8):
                    k = half * 8 + kk
                    T(lambda e, k=k, kk=kk: e.transpose(pv[:, kk * 128:(kk + 1) * 128], mixb[:, k * 128:(k + 1) * 128], cmb[:]), r=["mixb"], w=[pk])
                A(lambda e, half=half: e.copy(out=mixT[:, half * 8:(half + 1) * 8, :].rearrange("p h t -> p (h t)"), in_=pv[:, 0:1024]),
                  r=[pk], w=["hT"])
            for b in range(8):
                wt, wk = wload(woutb[b])
                gi = cntq["g"] % 2
                cntq["g"] += 1
                sy.dma("s", gtc[gi][:, 0:256], GTd[which:which + 1, b * 256:(b + 1) * 256].to_broadcast([128, 256]), w=[("gtc", gi)])
                def evo(ps, pk, b=b, gi=gi):
                    V(lambda e: e.tensor_tensor(out=tmpy[:, 0:256], in0=ps[:, 0:256], in1=gtc[gi][:, 0:256], op=ALU.mult),
                      r=[pk, ("gtc", gi)], w=[("rl", 1)])
                    V(lambda e: e.tensor_tensor(out=xq[:, b * 256:(b + 1) * 256], in0=xq[:, b * 256:(b + 1) * 256], in1=tmpy[:, 0:256], op=ALU.add),
                      r=[("rl", 1), "xq"], w=["xq"])
                projq(mixT, "hT", wt, wk, 0, 256, evo)
            if DBG:
                sy.dma("s", dbg_xm, xq[:], r=["xq"], w=["dbg_xm"], is_output=True)
            ln_transpose_q(which, g2, shf)
            for i in range(44):
                wt, wk = wload(wupb[i])
                uext = uexts[i % 2]; ta = tas[i % 2]; sa = sas[i % 2]
                uk = ("uext", i % 2); tk = ("ta", i % 2); sk = ("sa", i % 2)
                ps, pk = pp.get()
                for ab in range(2):
                    for k in range(16):
                        T(lambda e, k=k, ab=ab, ps=ps: e.matmul(ps[:, ab * 128:(ab + 1) * 128], lhsT=wt[:, k, ab * 128:(ab + 1) * 128],
                                                                rhs=hT[:, k, :], start=(k == 0), stop=(k == 15)), r=["hT", wk], w=[pk])
                A(lambda e, ps=ps: e.copy(out=uext[:, :, 2:130], in_=ps[:, 0:256].rearrange("p (a t) -> p a t", a=2)), r=[pk], w=[uk])
                for ab in range(2):
                    ch = ab * 44 + i
                    if kind == "own":
                        V(lambda e, ab=ab, ch=ch: e.tensor_copy(out=uext[:, ab, 0:2], in_=uhalo[:, ch, 2 * j:2 * j + 2]), r=["uhalo"], w=[uk])
                    elif kind == "sample":
                        V(lambda e, ab=ab, ch=ch: e.tensor_copy(out=uext[:, ab, 0:2], in_=sconv[:, ch, :]), r=["sconv"], w=[uk])
                    else:
                        V(lambda e, ab=ab: e.memset(uext[:, ab, 0:2], 0.0), w=[uk])
                    if kind == "halo":
                        V(lambda e, ab=ab, ch=ch: e.tensor_tensor(out=uhalo[:, ch, :], in0=uext[:, ab, 2:34], in1=hfl[:], op=ALU.mult),
                          r=[uk, "hfl"], w=["uhalo"])
                    if kind == "sample":
                        V(lambda e, ab=ab, ch=ch: e.tensor_copy(out=oconv[:, ch, :], in_=uext[:, ab, 64:66]), r=[uk], w=["oconv"])
                    if kind == "own" and j == 15:
                        V(lambda e, ab=ab, ch=ch: e.tensor_copy(out=oconv[:, ch, :], in_=uext[:, ab, 128:130]), r=[uk], w=["oconv"])
                    A(lambda e, ab=ab, ch=ch: e.activation(out=ta[:, ab, :], in_=uext[:, ab, 2:130], func=AF.Identity,
                                                           scale=convw[:, ch, 2:3], bias=convw[:, ch, 3:4]), r=[uk, "convw"], w=[tk])
                    V(lambda e, ab=ab, ch=ch: e.scalar_tensor_tensor(out=ta[:, ab, :], in0=uext[:, ab, 1:129], scalar=convw[:, ch, 1:2],
                                                                     in1=ta[:, ab, :], op0=ALU.mult, op1=ALU.add), r=[uk, tk, "convw"], w=[tk])
                    V(lambda e, ab=ab, ch=ch: e.scalar_tensor_tensor(out=ta[:, ab, :], in0=uext[:, ab, 0:128], scalar=convw[:, ch, 0:1],
                                                                     in1=ta[:, ab, :], op0=ALU.mult, op1=ALU.add), r=[uk, tk, "convw"], w=[tk])
                A(lambda e: e.activation(out=sa[:], in_=ta[:, 0, :], func=AF.Silu), r=[tk], w=[sk])
                V(lambda e, i=i: e.tensor_tensor(out=hidT[:, i, :], in0=sa[:], in1=ta[:, 1, :], op=ALU.mult), r=[sk, tk], w=["score"])
            if kind == "sample":
                sy.dma("s", o_convs, oconv[:], r=["oconv"], w=["o_convs"], is_output=True)
            if kind == "own" and j == 15:
                sy.dma("s", o_convp, oconv[:], r=["oconv"], w=["o_convp"], is_output=True)
            if kind != "halo":
                for nb in range(4):
                    ps, pk = pp.get()
                    for i2 in range(22):
                        di = cntq["d"] % 3
                        cntq["d"] += 1
                        sy.dma("s", wdp[di][:], wdnb[nb, 2 * i2:2 * i2 + 2].rearrange("i p c -> p i c"), w=[("wdp", di)])
                        for ii in range(2):
                            i = 2 * i2 + ii
                            T(lambda e, i=i, ii=ii, di=di, ps=ps: e.matmul(ps[:, :], lhsT=hidT[:, i, :], rhs=wdp[di][:, ii, :], start=(i == 0), stop=(i == 43)),
                              r=["score", ("wdp", di)], w=[pk])
                    gi = cntq["g"] % 2
                    cntq["g"] += 1
                    sy.dma("s", gtc[gi][:], GTd[which:which + 1, 2048 + nb * 512:2048 + (nb + 1) * 512].to_broadcast([128, 512]), w=[("gtc", gi)])
                    V(lambda e, ps=ps, nb=nb, gi=gi: e.tensor_tensor(out=tmpy[:, 0:512], in0=ps[:, :], in1=gtc[gi][:], op=ALU.mult),
                      r=[pk, ("gtc", gi)], w=[("rl", 1)])
                    V(lambda e, nb=nb: e.tensor_tensor(out=xq[:, nb * 512:(nb + 1) * 512], in0=xq[:, nb * 512:(nb + 1) * 512], in1=tmpy[:, 0:512], op=ALU.add),
                      r=[("rl", 1), "xq"], w=["xq"])
                if kind == "own":
                    sy.dma("s", o_y[j * 128:(j + 1) * 128, :], xq[:], r=["xq"], w=["o_y"], is_output=True)
                else:
                    sy.dma("s", o_ys, xq[0:64, :], r=["xq"], w=["o_ys"], is_output=True)

        QT_LIST = os.environ.get("QTILES", "all")
        tl = [("sample", 0), ("halo", 0)] + [("own", j) for j in range(16)]
        if QT_LIST != "all":
            tl = [tl[int(x)] for x in QT_LIST.split(",")]
        for kind, j in tl:
            q_tile(kind, j)
        sy.barrier()
        stack_holder[0].close()
        stack_holder[0] = None

    sy.finish()
    return nc


def _rope_tab(pos, half, theta=500000.0):
    inv = (np.float32(theta) ** (-(np.arange(half, dtype=np.float32) / np.float32(half)))).astype(np.float32)
    ang = (pos.astype(np.float32)[:, None] * inv[None, :]).astype(np.float32)
    return np.concatenate([np.cos(ang.astype(np.float64)), np.sin(ang.astype(np.float64))], axis=1).astype(np.float32)


def _host_inputs(inp):
    f = np.float32
    x_prompt = np.asarray(inp["x_prompt"], f)[0]
    xpad = np.zeros(((128 + 14) * 128, D), f)
    xpad[7 * 128:(7 + 128) * 128] = x_prompt
    w_in = np.asarray(inp["w_in"], f)[0]
    offs = np.cumsum([0, 1024, 1024, 1024, 1024, 64, 16, 512, 512, 1024, 1024, 16])
    aq, ak, av, iq, ik, iw, gq, gk, gv, gr, glr = [w_in[:, offs[i]:offs[i + 1]] for i in range(11)]

    def kmaj(w):
        return np.ascontiguousarray(w.reshape(16, 128, -1).transpose(1, 0, 2))
    w1 = kmaj(np.concatenate([ak, av, gk, gv, gq, ik, glr], axis=1))
    w_ada = np.asarray(inp["w_ada"], f)[0]
    wada = np.ascontiguousarray(w_ada.reshape(16, 128, 24, 512).transpose(2, 1, 0, 3))
    b_ada = np.asarray(inp["b_ada"], f)[0]
    badaT = np.ascontiguousarray(b_ada.reshape(96, 128).T)
    brow = np.concatenate([b_ada[4096:6144], b_ada[10240:12288]])
    badaR = np.ascontiguousarray(np.stack([brow, brow]))
    cmat = np.zeros((128, 6, 128), f)
    ii = np.arange(128)
    cmat[:, 0, :] = np.eye(128)
    cmat[:, 1, :] = (ii[:, None] <= ii[None, :])
    cmat[:, 2, :] = (ii[:, None] > ii[None, :])
    cmat[:, 3, :] = 1.0
    cmat[0, 4, :] = 1.0
    cmat[0, 5, 64:] = 1.0
    cmat[1, 5, :64] = 1.0
    common = {
        "badaT": badaT, "badaR": badaR,
        "gmixT": np.ascontiguousarray(np.asarray(inp["g_mix"], f)[0].reshape(16, 128).T),
        "gffnT": np.ascontiguousarray(np.asarray(inp["g_ffn"], f)[0].reshape(16, 128).T),
        "wada": wada, "w1": w1,
        "gk_b": np.ascontiguousarray(np.broadcast_to(np.tile(np.asarray(inp["g_k"], f)[0], 8)[None, :], (128, 1024))),
        "gq_b": np.ascontiguousarray(np.broadcast_to(np.tile(np.asarray(inp["g_q"], f)[0], 8)[None, :], (128, 1024))),
        "ggla_b": np.ascontiguousarray(np.broadcast_to(np.tile(np.asarray(inp["g_gla"], f)[0], 4)[None, :], (128, 1024))),
        "wg2": np.ascontiguousarray(np.concatenate([np.asarray(inp["w_gate2"], f)[0], np.asarray(inp["b_gate2"], f)[0][None, :]], 0)),
        "cmat": cmat,
    }
    w_out = np.asarray(inp["w_out"], f)[0]
    w_up = np.asarray(inp["w_up"], f)[0]
    w_down = np.asarray(inp["w_down"], f)[0]
    k2 = kmaj(np.concatenate([aq, iq, gr], axis=1))
    common["w2"] = np.ascontiguousarray(k2.reshape(128, 16, 12, 256).transpose(2, 0, 1, 3))
    common["wiw"] = kmaj(iw)
    common["wout"] = np.ascontiguousarray(kmaj(w_out).reshape(128, 16, 8, 256).transpose(2, 0, 1, 3))
    ku = kmaj(w_up)
    common["wup"] = np.ascontiguousarray(np.concatenate(
        [ku[:, :, :DFF].reshape(128, 16, 44, 128), ku[:, :, DFF:].reshape(128, 16, 44, 128)], axis=3).transpose(2, 0, 1, 3))
    common["wdn"] = np.ascontiguousarray(w_down.reshape(44, 128, 4, 512).transpose(2, 0, 1, 3))
    cw = np.concatenate([np.asarray(inp["w_conv"], f)[0], np.asarray(inp["b_conv"], f)[0][None, :]], axis=0)
    common["convw"] = np.ascontiguousarray(cw.reshape(4, 88, 128).transpose(2, 1, 0))
    bS = np.zeros((128, 128), f); bS[:, 64:] = -BIG
    common["biasS"] = bS
    maps = []
    for c in range(NCORE):
        m = dict(common)
        sc_ = np.asarray(inp["state_ffn_conv"], f)[0, c]
        m["sconv"] = np.ascontiguousarray(sc_.reshape(2, 88, 128).transpose(2, 1, 0))
        npad = (7 - c) * 128
        bO = np.zeros((4, 128, 512), f)
        padrow = np.zeros(1024, f); padrow[:npad] = -BIG
        bO[0] = padrow[None, 0:512]; bO[1] = padrow[None, 512:1024]
        diag = np.zeros((128, 512), f); diag[0:64, 448:512] = -BIG
        bO[2] = bO[1] + diag; bO[3] = diag
        m["biasO"] = bO
        bH = np.full((128, 16384), -BIG, f)
        hfl = np.zeros((128, 32), f)
        posh = np.zeros(128, np.int64)
        for jj in range(16):
            a = 8 * jj + c - 1
            if a >= 0:
                for e_ in range(2):
                    bH[2 * jj + e_, npad:(8 * jj + 7) * 128] = 0.0
                    hfl[:, 2 * jj + e_] = 1.0
                    posh[2 * jj + e_] = a * 128 + 126 + e_
        m["biasH"] = bH
        m["hflag"] = hfl
        m["ropeKh"] = _rope_tab(posh, 16)
        m["ropeIh"] = _rope_tab(posh, 8)
        m["xp"] = xpad[c * 128:(c + NREL) * 128]
        m["xs"] = np.ascontiguousarray(np.asarray(inp["x_sample"], f)[c])
        cT = np.stack([np.asarray(inp["c_prompt"], f)[0], np.asarray(inp["c_sample"], f)[c]], axis=1)
        m["cT"] = np.ascontiguousarray(cT.reshape(16, 128, 2).transpose(1, 0, 2))
        flag = np.zeros((128, 2 * NT1), f)
        posK = np.zeros((NT1, 128), np.int64)
        for r in range(NREL):
            a = r - (7 - c)
            if 0 <= a < 128:
                flag[:, r] = 1.0
                posK[r] = a * 128 + np.arange(128)
        flag[:64, NT1 - 1] = 1.0
        posK[NT1 - 1, :64] = 1024 + np.arange(64)
        flag[:, NT1:] = -flag[:, :NT1] / 16.0
        m["flagt"] = flag
        m["ropeK"] = np.ascontiguousarray(_rope_tab(posK.reshape(-1), 16).reshape(NT1, 128, 32).transpose(1, 0, 2))
        m["ropeI"] = np.ascontiguousarray(_rope_tab(posK.reshape(-1), 8).reshape(NT1, 128, 16).transpose(1, 0, 2))
        ck = np.asarray(inp["cache_k"], f)[0, c]
        m["ckT"] = np.ascontiguousarray(ck.reshape(8, 128, 8, 128).transpose(0, 3, 2, 1))
        m["cv"] = np.ascontiguousarray(np.asarray(inp["cache_v"], f)[0, c].reshape(1024, 1024))
        m["cikT"] = np.ascontiguousarray(np.asarray(inp["cache_idx_k"], f)[0, c].T)
        m["sgla"] = np.ascontiguousarray(np.asarray(inp["state_gla"], f)[0, c].transpose(1, 0, 2).reshape(128, 1024))
        maps.append(m)
    return maps


def kernel(**inp):
    maps = _host_inputs(inp)
    nc = build_program()
    res = run_bass_kernel_spmd(nc, maps, core_ids=list(range(NCORE)))
    R = res.results
    f = np.float32
    kp = np.zeros((128, 128, 1024), f); vp = np.zeros((128, 128, 1024), f); ikp = np.zeros((128, 128, 64), f)
    for c in range(NCORE):
        for j in range(16):
            kp[8 * j + c] = R[c]["o_k"][j * 128:(j + 1) * 128]
            vp[8 * j + c] = R[c]["o_v"][j * 128:(j + 1) * 128]
            ikp[8 * j + c] = R[c]["o_ik"][j * 128:(j + 1) * 128]
    k_prompt = kp.reshape(1, 1, 16384, 8, 128)
    v_prompt = vp.reshape(1, 1, 16384, 8, 128)
    idx_k_prompt = ikp.reshape(1, 1, 16384, 64)
    gla_p = np.ascontiguousarray(R[0]["o_glap"].reshape(128, 4, 256).transpose(1, 0, 2)).reshape(1, 1, 4, 128, 256)
    k_sample = np.stack([R[c]["o_ks"] for c in range(NCORE)]).reshape(1, 8, 64, 8, 128)
    v_sample = np.stack([R[c]["o_vs"] for c in range(NCORE)]).reshape(1, 8, 64, 8, 128)
    idx_k_sample = np.stack([R[c]["o_iks"] for c in range(NCORE)]).reshape(1, 8, 64, 64)
    gla_s = np.stack([R[c]["o_glas"].reshape(128, 4, 256).transpose(1, 0, 2) for c in range(NCORE)]).reshape(1, 8, 4, 128, 256)
    yp = np.zeros((128, 128, D), f)
    for c in range(NCORE):
        for j in range(16):
            yp[8 * j + c] = R[c]["o_y"][j * 128:(j + 1) * 128]
    y_prompt = yp.reshape(1, 16384, D)
    y_sample = np.stack([R[c]["o_ys"] for c in range(NCORE)])
    ffn_conv_prompt = np.ascontiguousarray(R[7]["o_convp"].transpose(2, 1, 0)).reshape(1, 1, 2, 2 * DFF)
    ffn_conv_sample = np.stack([R[c]["o_convs"].transpose(2, 1, 0).reshape(2, 2 * DFF) for c in range(NCORE)]).reshape(1, 8, 2, 2 * DFF)
    return (y_prompt, y_sample, k_prompt, v_prompt, idx_k_prompt, np.ascontiguousarray(gla_p), ffn_conv_prompt,
            k_sample, v_sample, idx_k_sample, np.ascontiguousarray(gla_s), ffn_conv_sample)
```
